# Optimizing a Trainium2 kernel written in Bass

```python
import math
import jax
import jax.numpy as jnp
from jax import lax

D_MODEL = 2048
BATCH = 2
SEQ = 4096
DEPTH = 4

HEAD_DIM = 128
GRID_W = 64
QBLK = 128
NORM_EPS = 1e-6
NEG_INF = -1e30

DIFF_HEADS = D_MODEL // 512
DIFF_VDIM = 2 * HEAD_DIM
GQA_Q_HEADS = D_MODEL // 256
GQA_KV_HEADS = GQA_Q_HEADS // 4
GQA_GROUP = GQA_Q_HEADS // GQA_KV_HEADS
ROPE_THETA = 10000.0
ROPE_AXIS_DIM = HEAD_DIM // 2
DIL_CONFIGS = ((128, 1), (512, 4), (2048, 16))
DIL_HEADS = D_MODEL // 256
N_DIL = len(DIL_CONFIGS)
REL_BUCKETS = 32
REL_MAX_DIST = 1024
REL_HEADS = DIFF_HEADS + N_DIL * DIL_HEADS
D_FF = 5632
CONV_WIDTH = 3

A_QK_W = DIFF_HEADS * 2 * HEAD_DIM
A_V_W = DIFF_HEADS * DIFF_VDIM
B_Q_W = GQA_Q_HEADS * HEAD_DIM
B_KV_W = GQA_KV_HEADS * HEAD_DIM
AB_IN_W = 2 * A_QK_W + A_V_W + B_Q_W + 2 * B_KV_W
AB_OUT_W = A_V_W + B_Q_W
C_IN_W = N_DIL * 3 * DIL_HEADS * HEAD_DIM
C_OUT_W = DIL_HEADS * HEAD_DIM

kernel_name = 'hybrid_diff_gqa_dilated_encoder'


def rmsnorm(x, g, eps=NORM_EPS):
    xf = x.astype(jnp.float32)
    y = xf * lax.rsqrt(jnp.mean(xf * xf, axis=-1, keepdims=True) + eps)
    return (y * g.astype(jnp.float32)).astype(x.dtype)


def rel_bucket(rel):
    nb = REL_BUCKETS // 2
    max_exact = nb // 2
    n = jnp.abs(rel)
    nf = jnp.maximum(n, 1).astype(jnp.float32)
    large = max_exact + (jnp.log(nf / max_exact) / math.log(REL_MAX_DIST / max_exact)
                         * (nb - max_exact)).astype(jnp.int32)
    large = jnp.minimum(large, nb - 1)
    return jnp.where(rel > 0, nb, 0) + jnp.where(n < max_exact, n, large)


def axial_rope(t, row, col):
    inv_freq = ROPE_THETA ** (-jnp.arange(ROPE_AXIS_DIM // 2, dtype=jnp.float32) * 2.0 / ROPE_AXIS_DIM)

    def rotate(th, pos):
        ang = pos[:, None] * inv_freq[None, :]
        cos = jnp.cos(ang)[None, :, None, :].astype(t.dtype)
        sin = jnp.sin(ang)[None, :, None, :].astype(t.dtype)
        x1, x2 = jnp.split(th, 2, axis=-1)
        return jnp.concatenate([x1 * cos - x2 * sin, x2 * cos + x1 * sin], axis=-1)

    return jnp.concatenate([rotate(t[..., :ROPE_AXIS_DIM], row),
                            rotate(t[..., ROPE_AXIS_DIM:], col)], axis=-1)


def mixer_ab(x, w_in, diff_lambda, diff_subln, qk_norm, w_out, rel_table, layer_idx):
    bsz, seq, _ = x.shape
    rows = seq // GRID_W
    row = jnp.repeat(jnp.arange(rows, dtype=jnp.float32), GRID_W)
    col = jnp.tile(jnp.arange(GRID_W, dtype=jnp.float32), rows)
    scale = HEAD_DIM ** -0.5

    proj = jnp.einsum('bsd,de->bse', x, w_in)
    o1 = A_QK_W
    o2 = o1 + A_QK_W
    o3 = o2 + A_V_W
    o4 = o3 + B_Q_W
    o5 = o4 + B_KV_W
    a_q, a_k, a_v, b_q, b_k, b_v = jnp.split(proj, (o1, o2, o3, o4, o5), axis=-1)

    a_q = a_q.reshape(bsz, seq, DIFF_HEADS, 2, HEAD_DIM).transpose(0, 2, 3, 1, 4)
    a_k = a_k.reshape(bsz, seq, DIFF_HEADS, 2, HEAD_DIM).transpose(0, 2, 3, 1, 4)
    a_v = a_v.reshape(bsz, seq, DIFF_HEADS, DIFF_VDIM).transpose(0, 2, 1, 3)
    lambda_init = 0.8 - 0.6 * math.exp(-0.3 * layer_idx)
    lp = diff_lambda.astype(jnp.float32)
    lam = jnp.exp(jnp.sum(lp[0] * lp[1])) - jnp.exp(jnp.sum(lp[2] * lp[3])) + lambda_init
    table_a = rel_table[:, :DIFF_HEADS]

    b_q = axial_rope(rmsnorm(b_q.reshape(bsz, seq, GQA_Q_HEADS, HEAD_DIM), qk_norm[0]), row, col)
    b_k = axial_rope(rmsnorm(b_k.reshape(bsz, seq, GQA_KV_HEADS, HEAD_DIM), qk_norm[1]), row, col)
    b_q = b_q.reshape(bsz, seq, GQA_KV_HEADS, GQA_GROUP, HEAD_DIM).transpose(0, 2, 3, 1, 4)
    b_k = b_k.transpose(0, 2, 1, 3)
    b_v = b_v.reshape(bsz, seq, GQA_KV_HEADS, HEAD_DIM).transpose(0, 2, 1, 3)

    kpos = jnp.arange(seq)

    def block(i0):
        qa = lax.dynamic_slice_in_dim(a_q, i0, QBLK, axis=3)
        qb = lax.dynamic_slice_in_dim(b_q, i0, QBLK, axis=3)
        rel = kpos[None, :] - (i0 + jnp.arange(QBLK))[:, None]
        bias = jnp.transpose(table_a[rel_bucket(rel)], (2, 0, 1))
        s_a = jnp.einsum('bhmqd,bhmkd->bhmqk', qa, a_k).astype(jnp.float32) * scale + bias[None, :, None]
        p_a = jax.nn.softmax(s_a, axis=-1)
        diff = p_a[:, :, 0] - lam * p_a[:, :, 1]
        o_a = jnp.einsum('bhqk,bhkd->bhqd', diff.astype(a_v.dtype), a_v)
        s_b = jnp.einsum('bgrqd,bgkd->bgrqk', qb, b_k).astype(jnp.float32) * scale
        p_b = jax.nn.softmax(s_b, axis=-1)
        o_b = jnp.einsum('bgrqk,bgkd->bgrqd', p_b.astype(b_v.dtype), b_v)
        return o_a, o_b

    starts = jnp.arange(seq // QBLK) * QBLK
    o_a, o_b = lax.map(block, starts)
    o_a = o_a.transpose(1, 0, 3, 2, 4).reshape(bsz, seq, DIFF_HEADS, DIFF_VDIM)
    o_a = (rmsnorm(o_a, diff_subln) * (1.0 - lambda_init)).reshape(bsz, seq, A_V_W)
    o_b = o_b.transpose(1, 0, 4, 2, 3, 5).reshape(bsz, seq, B_Q_W)
    o = jnp.concatenate([o_a.astype(x.dtype), o_b.astype(x.dtype)], axis=-1)
    return jnp.einsum('bse,ed->bsd', o, w_out)


def dilated_window_attention(q, k, v, dilation, radius, table):
    bsz, seq, nh, hd = q.shape
    sub = seq // dilation
    nblk = -(-sub // radius)
    padded = nblk * radius
    scale = HEAD_DIM ** -0.5

    def to_sub(t):
        t = t.reshape(bsz, sub, dilation, nh, hd).transpose(0, 2, 1, 3, 4)
        return jnp.pad(t, ((0, 0), (0, 0), (0, padded - sub), (0, 0), (0, 0)))

    def band(t):
        tp = jnp.pad(t, ((0, 0), (0, 0), (radius, radius), (0, 0), (0, 0)))
        return jnp.concatenate(
            [tp[:, :, o:o + padded].reshape(bsz, dilation, nblk, radius, nh, hd)
             for o in (0, radius, 2 * radius)], axis=3)

    qblk = to_sub(q).reshape(bsz, dilation, nblk, radius, nh, hd)
    kb = band(to_sub(k))
    vb = band(to_sub(v))

    qi = jnp.arange(radius)
    kj = jnp.arange(3 * radius)
    rel_sub = kj[None, :] - radius - qi[:, None]
    key_sub = jnp.arange(nblk)[:, None] * radius - radius + kj[None, :]
    mask = (jnp.abs(rel_sub) <= radius)[None] & ((key_sub >= 0) & (key_sub < sub))[:, None, :]
    bias = jnp.transpose(table[rel_bucket(rel_sub * dilation)], (2, 0, 1))

    s = jnp.einsum('brnqhd,brnkhd->brnhqk', qblk, kb).astype(jnp.float32) * scale + bias
    s = jnp.where(mask[:, None], s, NEG_INF)
    m = jnp.max(s, axis=-1, keepdims=True)
    e = jnp.exp(s - m)
    den = jnp.sum(e, axis=-1, keepdims=True)
    o = jnp.einsum('brnhqk,brnkhd->brnqhd', (e / den).astype(v.dtype), vb)
    lse = (m + jnp.log(den))[..., 0]

    o = o.reshape(bsz, dilation, padded, nh, hd)[:, :, :sub].transpose(0, 2, 1, 3, 4).reshape(bsz, seq, nh, hd)
    lse = lse.transpose(0, 1, 2, 4, 3).reshape(bsz, dilation, padded, nh)[:, :, :sub]
    lse = lse.transpose(0, 2, 1, 3).reshape(bsz, seq, nh)
    return o, lse


def mixer_c(x, w_in, w_out, rel_table):
    bsz, seq, _ = x.shape
    proj = jnp.einsum('bsd,de->bse', x, w_in).reshape(bsz, seq, N_DIL, 3, DIL_HEADS, HEAD_DIM)
    outs, lses = [], []
    for g, (window, dilation) in enumerate(DIL_CONFIGS):
        c0 = DIFF_HEADS + g * DIL_HEADS
        o, lse = dilated_window_attention(proj[:, :, g, 0], proj[:, :, g, 1], proj[:, :, g, 2],
                                          dilation, window // (2 * dilation),
                                          rel_table[:, c0:c0 + DIL_HEADS])
        outs.append(o)
        lses.append(lse)
    alpha = jax.nn.softmax(jnp.stack(lses), axis=0)
    o = jnp.sum(alpha[..., None] * jnp.stack(outs).astype(jnp.float32), axis=0)
    return jnp.einsum('bse,ed->bsd', o.reshape(bsz, seq, C_OUT_W).astype(x.dtype), w_out)


def depthwise_conv(u, w, b):
    seq = u.shape[1]
    pad = CONV_WIDTH // 2
    up = jnp.pad(u, ((0, 0), (pad, pad), (0, 0)))
    return b + sum(up[:, i:i + seq] * w[i] for i in range(CONV_WIDTH))


def conv_ffn(x, w_up, conv_w, conv_b, w_down):
    u = depthwise_conv(jnp.einsum('bsd,df->bsf', x, w_up), conv_w, conv_b)
    gate, val = jnp.split(u, 2, axis=-1)
    return jnp.einsum('bsf,fd->bsd', jax.nn.gelu(gate, approximate=True) * val, w_down)


def setup_inputs(seed: int = 0) -> dict:
    key = jax.random.key(seed)
    keys = iter(jax.random.split(key, 16 * DEPTH + 4))

    def normal(shape, scale):
        return scale * jax.random.normal(next(keys), shape, jnp.float32)

    def gain(shape):
        return 1.0 + normal(shape, 0.05)

    p = {'x': normal((BATCH, SEQ, D_MODEL), 1.0),
         'rel_bias_table': normal((REL_BUCKETS, REL_HEADS), 0.5)}
    for i in range(DEPTH):
        p[f'l{i}_mix_pre_norm'] = gain((D_MODEL,))
        if i % 2 == 0:
            p[f'l{i}_w_in'] = normal((D_MODEL, AB_IN_W), D_MODEL ** -0.5)
            p[f'l{i}_diff_lambda'] = normal((4, HEAD_DIM), 0.1)
            p[f'l{i}_diff_subln'] = gain((DIFF_VDIM,))
            p[f'l{i}_qk_norm'] = gain((2, HEAD_DIM))
            p[f'l{i}_w_out'] = normal((AB_OUT_W, D_MODEL), AB_OUT_W ** -0.5)
        else:
            p[f'l{i}_w_in'] = normal((D_MODEL, C_IN_W), D_MODEL ** -0.5)
            p[f'l{i}_w_out'] = normal((C_OUT_W, D_MODEL), C_OUT_W ** -0.5)
        p[f'l{i}_mix_post_norm'] = gain((D_MODEL,))
        p[f'l{i}_ffn_pre_norm'] = gain((D_MODEL,))
        p[f'l{i}_w_up'] = normal((D_MODEL, 2 * D_FF), D_MODEL ** -0.5)
        p[f'l{i}_conv_w'] = normal((CONV_WIDTH, 2 * D_FF), CONV_WIDTH ** -0.5)
        p[f'l{i}_conv_b'] = normal((2 * D_FF,), 0.01)
        p[f'l{i}_w_down'] = normal((D_FF, D_MODEL), D_FF ** -0.5)
        p[f'l{i}_ffn_post_norm'] = gain((D_MODEL,))
    return p


def reference(x, rel_bias_table,
              l0_mix_pre_norm, l0_w_in, l0_diff_lambda, l0_diff_subln, l0_qk_norm, l0_w_out, l0_mix_post_norm,
              l0_ffn_pre_norm, l0_w_up, l0_conv_w, l0_conv_b, l0_w_down, l0_ffn_post_norm,
              l1_mix_pre_norm, l1_w_in, l1_w_out, l1_mix_post_norm,
              l1_ffn_pre_norm, l1_w_up, l1_conv_w, l1_conv_b, l1_w_down, l1_ffn_post_norm,
              l2_mix_pre_norm, l2_w_in, l2_diff_lambda, l2_diff_subln, l2_qk_norm, l2_w_out, l2_mix_post_norm,
              l2_ffn_pre_norm, l2_w_up, l2_conv_w, l2_conv_b, l2_w_down, l2_ffn_post_norm,
              l3_mix_pre_norm, l3_w_in, l3_w_out, l3_mix_post_norm,
              l3_ffn_pre_norm, l3_w_up, l3_conv_w, l3_conv_b, l3_w_down, l3_ffn_post_norm):
    mix_norms = [(l0_mix_pre_norm, l0_mix_post_norm), (l1_mix_pre_norm, l1_mix_post_norm),
                 (l2_mix_pre_norm, l2_mix_post_norm), (l3_mix_pre_norm, l3_mix_post_norm)]
    mix_params = [(l0_w_in, l0_diff_lambda, l0_diff_subln, l0_qk_norm, l0_w_out),
                  (l1_w_in, l1_w_out),
                  (l2_w_in, l2_diff_lambda, l2_diff_subln, l2_qk_norm, l2_w_out),
                  (l3_w_in, l3_w_out)]
    ffn_params = [(l0_ffn_pre_norm, l0_w_up, l0_conv_w, l0_conv_b, l0_w_down, l0_ffn_post_norm),
                  (l1_ffn_pre_norm, l1_w_up, l1_conv_w, l1_conv_b, l1_w_down, l1_ffn_post_norm),
                  (l2_ffn_pre_norm, l2_w_up, l2_conv_w, l2_conv_b, l2_w_down, l2_ffn_post_norm),
                  (l3_ffn_pre_norm, l3_w_up, l3_conv_w, l3_conv_b, l3_w_down, l3_ffn_post_norm)]
    h = x
    for i in range(DEPTH):
        pre, post = mix_norms[i]
        u = rmsnorm(h, pre)
        if i % 2 == 0:
            y = mixer_ab(u, *mix_params[i], rel_bias_table, i)
        else:
            y = mixer_c(u, *mix_params[i], rel_bias_table)
        h = h + rmsnorm(y, post)
        f_pre, w_up, conv_w, conv_b, w_down, f_post = ffn_params[i]
        h = h + rmsnorm(conv_ffn(rmsnorm(h, f_pre), w_up, conv_w, conv_b, w_down), f_post)
    return h
```

```python
import contextlib
import numpy as np
import ml_dtypes
import concourse.bass as bass
import concourse.mybir as mybir
from concourse.bass_utils import run_bass_kernel_spmd

F32 = mybir.dt.float32
BF16 = mybir.dt.bfloat16
I32 = mybir.dt.int32
AF = mybir.ActivationFunctionType
ALU = mybir.AluOpType
BF = ml_dtypes.bfloat16

NCORES = 8
D = 2048
DC = 16
T = 1024
SEQ = 4096
DFF = 5632
FC = 44
EPS = 1e-6
NDMA_SEM = 12


class Op:
    __slots__ = ("eng", "fn", "deps", "needed", "count", "dma", "dsem", "dval", "prev_dma", "where")

    def __init__(self, eng, fn, dma):
        self.eng = eng
        self.fn = fn
        self.deps = []
        self.needed = False
        self.count = 0
        self.dma = dma
        self.dsem = None
        self.dval = 0
        self.prev_dma = None


class Sched:
    ENGS = ("pe", "act", "dve", "pool", "sp")

    def __init__(self):
        self.ops = []
        self.lastw = {}
        self.readers = {}
        self.dma_hist = {e: [] for e in self.ENGS}
        self.swq = []

    def add(self, eng, fn, reads=(), writes=(), dma=False, sig=False, ndesc=0):
        op = Op(eng, fn, dma)
        op.needed = sig
        import sys as _s
        f_ = _s._getframe(1)
        op.where = []
        while f_ is not None and len(op.where) < 5:
            op.where.append(f_.f_lineno)
            f_ = f_.f_back
        deps = {}
        for r in reads:
            w = self.lastw.get(r)
            if w is not None:
                deps[id(w)] = w
        for w_ in writes:
            lw = self.lastw.get(w_)
            if lw is not None:
                deps[id(lw)] = lw
            rd = self.readers.get(w_)
            if rd:
                for o in rd[0].values():
                    deps[id(o)] = o
                for o in rd[1]:
                    deps[id(o)] = o
        for r in reads:
            rd = self.readers.get(r)
            if rd is None:
                rd = self.readers[r] = ({}, [])
            if dma:
                rd[1].append(op)
            else:
                rd[0][eng] = op
        for w_ in writes:
            self.lastw[w_] = op
            self.readers[w_] = ({}, [])
        for d in deps.values():
            if d is op:
                continue
            if d.eng == "pe" and eng == "pe" and not d.dma and not dma:
                continue
            d.needed = True
            op.deps.append(d)
        if dma and ndesc:
            out = self.swq
            while out and sum(n for _, n in out) + ndesc > 700:
                o, _ = out.pop(0)
                op.deps.append(o)
            out.append((op, ndesc))
        if dma:
            hist = self.dma_hist[eng]
            j = len(hist)
            op.dsem = j % NDMA_SEM
            op.dval = 16 * (j // NDMA_SEM + 1)
            if j >= NDMA_SEM:
                op.prev_dma = hist[j - NDMA_SEM]
            hist.append(op)
        self.ops.append(op)
        return op

    def emit(self, nc):
        cnt = {e: 0 for e in self.ENGS}
        for op in self.ops:
            if op.dma:
                continue
            if op.needed:
                cnt[op.eng] += 1
                op.count = cnt[op.eng]
        per_eng = {e: [o for o in self.ops if o.eng == e] for e in self.ENGS}
        with contextlib.ExitStack() as st:
            esem = {e: st.enter_context(nc.semaphore("s_" + e)) for e in self.ENGS}
            dsem = {
                e: [st.enter_context(nc.semaphore(f"d_{e}_{i}")) for i in range(NDMA_SEM)]
                for e in self.ENGS
                if self.dma_hist[e]
            }
            block = st.enter_context(nc.Block())

            def run(engname, eng):
                waited = {}
                for op in per_eng[engname]:
                    ws = {}
                    deps = list(op.deps)
                    if op.prev_dma is not None:
                        deps.append(op.prev_dma)
                    for d in deps:
                        if d.dma:
                            key = ("d", d.eng, d.dsem)
                            val = d.dval
                        else:
                            key = ("e", d.eng)
                            val = d.count
                        if waited.get(key, 0) >= val:
                            continue
                        if ws.get(key, 0) < val:
                            ws[key] = val
                    for key, val in ws.items():
                        if key[0] == "d":
                            eng.wait_ge(dsem[key[1]][key[2]], val)
                        else:
                            eng.wait_ge(esem[key[1]], val)
                        waited[key] = val
                    try:
                        inst = op.fn(eng)
                    except Exception:
                        print('OP FAILED at lines', op.where, flush=True)
                        raise
                    if op.dma:
                        inst.then_inc(dsem[engname][op.dsem], 16)
                    elif op.needed:
                        inst.then_inc(esem[engname], 1)
                if engname == "sp":
                    for e in self.ENGS:
                        last = {}
                        for o in self.dma_hist[e]:
                            last[o.dsem] = o.dval
                        for s, v in last.items():
                            eng.wait_ge(dsem[e][s], v)
                        if cnt[e] > 0:
                            eng.wait_ge(esem[e], cnt[e])

            @block.tensor
            def _(e):
                run("pe", e)

            @block.scalar
            def _(e):
                run("act", e)

            @block.vector
            def _(e):
                run("dve", e)

            @block.gpsimd
            def _(e):
                run("pool", e)

            @block.sync
            def _(e):
                run("sp", e)


O_HT = 0
O_ONES = 65536
O_CF = O_ONES + 256
O_RSTD = 67584
O_R1 = 69632
O_R2 = 102464
O_R3 = 147520
O_WS = 180288
O_CV = 202816
O_END = 211008
WS_ELEMS = 11264

C_EPS, C_NLAM, C_VL, C_VR, C_M0, C_ML, C_MR, C_MRLO, C_GS0, C_GS1, C_GQ, C_GK = range(12)
C_GA = 16
C_GB = 32
C_CW = 48
C_CB = 48 + 264

LAMBDA_INIT = [0.8 - 0.6 * float(np.exp(-0.3 * i)) for i in range(4)]
SCALE = 128 ** -0.5
DILS = (1, 4, 16)


class Ctx:
    def __init__(self):
        self.nc = bass.Bass("TRN2", target_bir_lowering=False)
        self.S = Sched()
        self.base = (self.nc.sbuf_base + 63) // 64 * 64
        assert self.base + O_END <= self.nc.sbuf_top, (self.base, self.nc.sbuf_top)
        self.n = 0
        self.barrier_op = None

    def dram_in(self, name, shape, dt):
        return self.nc.dram_tensor(name, list(shape), dt, kind="ExternalInput").ap()

    def dram_out(self, name, shape, dt):
        return self.nc.dram_tensor(name, list(shape), dt, kind="ExternalOutput").ap()

    def dram(self, name, shape, dt):
        return self.nc.dram_tensor(name, list(shape), dt).ap()

    def at(self, off, shape, dt):
        self.n += 1
        return self.nc.alloc_sbuf_tensor_at(f"t{self.n}", list(shape), dt, offset=self.base + off)

    def add(self, eng, fn, reads=(), writes=(), dma=False, sig=False, ndesc=0):
        return self.S.add(eng, fn, list(reads) + ["ARENA"], writes, dma, sig, ndesc)

    def barrier(self):
        self.S.add("dve", lambda e: e.memset(self.dummy, 0.0), reads=["ARENA"], writes=["ARENA", "dummy"])

    def load(self, dst, src, key, eng="sp"):
        self.add(eng, lambda e: e.dma_start(out=dst, in_=src), writes=[key] if isinstance(key, str) else key, dma=True)

    def store(self, dst, src, rkeys, wkey=None, eng="sp"):
        self.add(eng, lambda e: e.dma_start(out=dst, in_=src), reads=[rkeys] if isinstance(rkeys, str) else rkeys,
                 writes=[wkey] if wkey else [], dma=True)

    def act(self, out, in_, func, reads, writes, **kw):
        self.add("act", lambda e: e.activation(out=out, in_=in_, func=func, **kw), reads, writes)

    def tt(self, out, in0, in1, op, reads, writes, eng="dve"):
        self.add(eng, lambda e: e.tensor_tensor(out=out, in0=in0, in1=in1, op=op), reads, writes)

    def stt(self, out, in0, scalar, in1, op0, op1, reads, writes):
        self.add("dve", lambda e: e.scalar_tensor_tensor(out=out, in0=in0, scalar=scalar, in1=in1, op0=op0, op1=op1),
                 reads, writes)

    def ts(self, out, in0, s1, op0, reads, writes, s2=None, op1=None, eng="dve"):
        if op1 is None:
            self.add(eng, lambda e: e.tensor_scalar(out=out, in0=in0, scalar1=s1, scalar2=None, op0=op0), reads, writes)
        else:
            self.add(eng, lambda e: e.tensor_scalar(out=out, in0=in0, scalar1=s1, scalar2=s2, op0=op0, op1=op1), reads,
                     writes)

    def cp(self, out, in_, reads, writes, eng="dve"):
        self.add(eng, lambda e: e.tensor_copy(out=out, in_=in_), reads, writes)

    def rc(self, out, in_, reads, writes):
        self.add("dve", lambda e: e.reciprocal(out=out, in_=in_), reads, writes)

    def mm(self, steps, reads, writes):
        steps = list(steps)

        def f(e):
            inst = None
            for (o, l, r, st, sp) in steps:
                inst = e.matmul(o, lhsT=l, rhs=r, start=st, stop=sp)
            return inst

        self.add("pe", f, reads, writes)

    def finish(self):
        self.S.emit(self.nc)
        return self.nc


_PV = {}


def getpv(e):
    k = id(e)
    if k not in _PV:
        pid = e.partition_id()
        b4 = (pid // 4) * 4
        _PV.clear()
        _PV[k] = dict(pid=pid, b4=b4)
    return _PV[k]


class WStream:
    GR = 512

    def __init__(self, K):
        self.K = K
        self.buf = K.at(O_WS, [128, WS_ELEMS], BF16)
        self.i = 0
        self.cur = None

    def fetch(self, w, kc, col0, ncols, row0=0):
        size = kc * ncols
        if self.cur != size:
            self.cur = size
            self.i = 0
        nslots = WS_ELEMS // size
        s = self.i % nslots
        self.i += 1
        off = s * size
        t = self.buf[:, off:off + size].rearrange("p (c e) -> p c e", c=kc)
        keys = [f"wsg{g}" for g in range(off // self.GR, (off + size + self.GR - 1) // self.GR)]
        src = w[row0:row0 + 128 * kc, col0:col0 + ncols].rearrange("(c p) e -> p c e", p=128)
        self.K.add("pool", lambda e: e.dma_start(out=t, in_=src), reads=[w.tensor.name], writes=keys, dma=True,
                   ndesc=8 * kc)
        return t, keys


def setup_common(K):
    nc = K.nc
    K._st = contextlib.ExitStack()
    K.bank = [K._st.enter_context(nc.psum_tensor(f"bank{i}", [128, 512], F32)) for i in range(8)]
    K.hT = K.at(O_HT, [128, DC, T], F32)
    K.ones = K.at(O_ONES, [128, 128], BF16)
    K.cf = K.at(O_CF, [128, 448], F32)
    K.rstd = K.at(O_RSTD, [128, 512], F32)
    K.dummy = K.cf[:, 440:441]
    K.add("dve", lambda e: e.memset(K.ones[:], 1.0), writes=["ones"])
    K.add("dve", lambda e: e.memset(K.cf[:, C_EPS:C_EPS + 1], EPS), writes=["cf_eps"])
    K.ws = WStream(K)
    K.hkeys = [f"h{c}" for c in range(DC)]


def rmsnorm_stats(K, src3, src_keys, TN, sq3, sq_keys, bank, inv_n):
    C = src3.shape[1]
    K.act(sq3, src3, AF.Square, src_keys, sq_keys)
    pb = K.bank[bank]
    K.mm([(pb[:, 0:TN], K.ones[:], sq3[:, c, :], c == 0, c == C - 1) for c in range(C)],
         list(sq_keys) + ["ones"], [f"bank{bank}"])
    rs = K.rstd[:, 0:TN]
    K.act(rs, pb[:, 0:TN], AF.Sqrt, [f"bank{bank}", "cf_eps"], ["rstd"], bias=K.cf[:, C_EPS:C_EPS + 1], scale=inv_n)
    K.rc(rs, rs, ["rstd"], ["rstd"])


def load_vec(K, vec_d, lo, n, col, key):
    K.load(K.cf[:, col:col + n], vec_d[:, lo:lo + n], key)


def ffn_phase(K, L, vec_d, wup, wdn, xedge_loc, xg_pad, pid):
    hT = K.hT
    cf = K.cf
    ws = K.ws
    B = K.bank
    K.barrier()
    load_vec(K, vec_d, 32, 16, C_GA, "gA")
    load_vec(K, vec_d, 48, 16, C_GB, "gB")
    load_vec(K, vec_d, 64, 352, C_CW, "cwb")
    xn = K.at(O_R1, [128, DC, 514], BF16)
    xmid = K.at(O_R1 + 16448, [128, DC, 2], BF16)
    xh = K.at(O_R1 + 16448 + 64, [128, 2, DC], BF16)
    xe = K.at(O_R1 + 16448 + 128, [128, 2, DC], BF16)
    act = K.at(O_R2, [128, FC, 512], BF16)
    ybuf = K.at(O_R3, [128, DC, 512], F32)
    cvb = [[K.at(O_CV + (p * 2 + q) * 2048, [128, 512], F32) for q in range(2)] for p in range(2)]
    sq = act[:, 0:DC, :]
    sqk = [f"act{j}" for j in range(DC)]
    rstd = K.rstd

    def gA(c):
        return cf[:, C_GA + c:C_GA + c + 1]

    def gB(c):
        return cf[:, C_GB + c:C_GB + c + 1]

    e4 = K.at(O_R3, [128, DC, 4], F32)
    for i, tok in enumerate((0, 511, 512, 1023)):
        K.cp(e4[:, :, i:i + 1], hT[:, :, tok:tok + 1], K.hkeys, ["e4"])
    rmsnorm_stats(K, e4[:, :, :], ["e4"], 4, act[:, 0:DC, 0:4], sqk, 0, 1.0 / D)
    for c in range(DC):
        K.stt(e4[:, c, :], e4[:, c, :], gA(c), rstd[:, 0:4], ALU.mult, ALU.mult, ["e4", "rstd", "gA"], ["e4"])
    K.cp(xmid[:, :, :], e4[:, :, 1:3], ["e4"], ["xmid"])
    K.cp(xe[:, 0, :], e4[:, :, 0], ["e4"], ["xe"])
    K.cp(xe[:, 1, :], e4[:, :, 3], ["e4"], ["xe"])
    K.store(xedge_loc.rearrange("(a p) c -> p a c", p=128), xe[:, :, :], ["xe"], "xedge")
    K.add("pool", lambda e: e.collective_compute("AllGather", ALU.bypass, replica_groups=[list(range(NCORES))],
                                                  ins=[xedge_loc], outs=[xg_pad[256:256 * 9, :]]),
          reads=["xedge"], writes=["xgath"], sig=True)
    xg3 = xg_pad.rearrange("(s a p) c -> s a p c", a=2, p=128)

    xgw = K.xg_win

    def cpwin(e):
        pid = getpv(e)['pid']
        return e.dma_start(out=xgw[:, :, :, :], in_=xg3[bass.ds(pid, 3), :, :, :])

    K.add("sp", cpwin, reads=["xgath"], writes=["xgw"], dma=True)

    def ld_l(e):
        return e.dma_start(out=xh[:, 0, :], in_=xgw[0, 1, :, :])

    def ld_r(e):
        return e.dma_start(out=xh[:, 1, :], in_=xgw[2, 0, :, :])

    K.add("sp", ld_l, reads=["xgw"], writes=["xh0"], dma=True)
    K.add("sp", ld_r, reads=["xgw"], writes=["xh1"], dma=True)
    xhf = K.at(O_R1 + 16448 + 256, [128, DC, 2], F32)
    K.ts(xhf[:, :, 0], xh[:, 0, :], cf[:, C_VL:C_VL + 1], ALU.mult, ["xh0", "pc"], ["xhf"])
    K.ts(xhf[:, :, 1], xh[:, 1, :], cf[:, C_VR:C_VR + 1], ALU.mult, ["xh1", "pc", "xhf"], ["xhf"])
    for hb in range(2):
        E0 = hb * 512
        rmsnorm_stats(K, hT[:, :, E0:E0 + 512], K.hkeys, 512, sq, sqk, 0, 1.0 / D)
        for c in range(DC):
            K.stt(xn[:, c, 1:513], hT[:, c, E0:E0 + 512], gA(c), rstd[:, :], ALU.mult, ALU.mult,
                  [f"h{c}", "rstd", "gA"], [f"xn{c}"])
        lsrc = xhf[:, :, 0:1] if hb == 0 else xmid[:, :, 0:1]
        rsrc = xmid[:, :, 1:2] if hb == 0 else xhf[:, :, 1:2]
        K.cp(xn[:, :, 0:1], lsrc, ["xhf", "xmid"], ["xnl"])
        K.cp(xn[:, :, 513:514], rsrc, ["xhf", "xmid"], ["xnr"])
        xn_keys = [f"xn{c}" for c in range(DC)] + ["xnl", "xnr"]
        pair = 0
        for j in range(FC):
            for part in range(2):
                f = part * FC + j
                wt, wk = ws.fetch(wup, DC, part * DFF + j * 128, 128)
                b0 = (pair % 3) * 2
                pair += 1
                pA, pB = B[b0], B[b0 + 1]
                K.mm([(pA[:, 0:258], wt[:, kc, :], xn[:, kc, 0:258], kc == 0, kc == DC - 1) for kc in range(DC)] +
                     [(pB[:, 0:258], wt[:, kc, :], xn[:, kc, 256:514], kc == 0, kc == DC - 1) for kc in range(DC)],
                     wk + xn_keys, [f"bank{b0}", f"bank{b0 + 1}"])
                cv = cvb[part][j % 2]
                ck = f"cv{part}_{j % 2}"
                for blk, pb in enumerate((pA, pB)):
                    o = cv[:, blk * 256:(blk + 1) * 256]
                    kk = ck + f"_{blk}"
                    bk = f"bank{b0 + blk}"
                    K.act(o, pb[:, 1:257], AF.Identity, [bk, "cwb"], [kk], bias=cf[:, C_CB + f:C_CB + f + 1],
                          scale=cf[:, C_CW + 3 * f + 1:C_CW + 3 * f + 2])
                    K.stt(o, pb[:, 0:256], cf[:, C_CW + 3 * f:C_CW + 3 * f + 1], o, ALU.mult, ALU.add, [bk, "cwb", kk], [kk])
                    K.stt(o, pb[:, 2:258], cf[:, C_CW + 3 * f + 2:C_CW + 3 * f + 3], o, ALU.mult, ALU.add,
                          [bk, "cwb", kk], [kk])
            cg = cvb[0][j % 2]
            cvv = cvb[1][j % 2]
            gk = [f"cv0_{j % 2}_0", f"cv0_{j % 2}_1"]
            vk = [f"cv1_{j % 2}_0", f"cv1_{j % 2}_1"]
            K.act(cg[:, :], cg[:, :], AF.Gelu_apprx_tanh, gk, gk)
            K.tt(act[:, j, :], cg[:, :], cvv[:, :], ALU.mult, gk + vk, [f"act{j}"], eng="pool")
        act_keys = [f"act{j}" for j in range(FC)]
        for dc in range(DC):
            wt, wk = ws.fetch(wdn, FC, dc * 128, 128)
            bi = 6 + dc % 2
            pb = B[bi]
            K.mm([(pb[:, :], wt[:, fc, :], act[:, fc, :], fc == 0, fc == FC - 1) for fc in range(FC)],
                 wk + act_keys, [f"bank{bi}"])
            K.act(ybuf[:, dc, :], pb[:, :], AF.Identity, [f"bank{bi}"], [f"y{dc}"])
        post_norm_residual(K, ybuf, E0, sq, sqk)


def post_norm_residual(K, ybuf, E0, sq, sqk):
    ykeys = [f"y{dc}" for dc in range(DC)]
    rmsnorm_stats(K, ybuf[:, :, :], ykeys, 512, sq, sqk, 0, 1.0 / D)
    for c in range(DC):
        K.tt(ybuf[:, c, :], ybuf[:, c, :], K.rstd[:, :], ALU.mult, [f"y{c}", "rstd"], [f"y{c}"], eng="pool")
        K.stt(K.hT[:, c, E0:E0 + 512], ybuf[:, c, :], K.cf[:, C_GB + c:C_GB + c + 1], K.hT[:, c, E0:E0 + 512],
              ALU.mult, ALU.add, [f"y{c}", "gB", f"h{c}"], [f"h{c}"])


def out_proj_phase(K, wout, ec_n, oT, okeys):
    K.barrier()
    ybuf = K.at(O_R3, [128, DC, 512], F32)
    sq = K.at(O_R2, [128, DC, 512], BF16)[:, :, :]
    sqk = [f"sq{j}" for j in range(DC)]
    for hb in range(2):
        E0 = hb * 512
        for dc in range(DC):
            wt, wk = K.ws.fetch(wout, ec_n, dc * 128, 128)
            bi = 6 + dc % 2
            pb = K.bank[bi]
            K.mm([(pb[:, :], wt[:, ec, :], oT[:, ec, E0:E0 + 512], ec == 0, ec == ec_n - 1) for ec in range(ec_n)],
                 wk + okeys, [f"bank{bi}"])
            K.act(ybuf[:, dc, :], pb[:, :], AF.Identity, [f"bank{bi}"], [f"y{dc}"])
        post_norm_residual(K, ybuf, E0, sq, sqk)


def pre_norm_full(K, xn):
    sq = K.at(O_R2, [128, DC, 512], BF16)[:, :, :]
    sqk = [f"sq{j}" for j in range(DC)]
    for hb in range(2):
        E0 = hb * 512
        rmsnorm_stats(K, K.hT[:, :, E0:E0 + 512], K.hkeys, 512, sq, sqk, 0, 1.0 / D)
        for c in range(DC):
            K.stt(xn[:, c, E0:E0 + 512], K.hT[:, c, E0:E0 + 512], K.cf[:, C_GA + c:C_GA + c + 1], K.rstd[:, :],
                  ALU.mult, ALU.mult, [f"h{c}", "rstd", "gA"], [f"xn{c}_{hb}"])
    return [f"xn{c}_{hb}" for c in range(DC) for hb in range(2)]


def bias_setup(K, table_d, ohA_d, ohC_d, TR2, TRC):
    K.barrier()
    tabx = K.at(O_R2, [33, 28], F32)
    ones33 = K.at(O_R2 + 128, [33, 128], F32)
    trf = [K.at(O_R2 + 1024 + i * 512, [33, 128], F32) for i in range(2)]
    trr = [K.at(O_R2 + 2048 + i * 512, [33, 128], F32) for i in range(2)]
    trh = [K.at(O_R2 + 3072 + i * 256, [33, 128], BF16) for i in range(2)]
    trl = [K.at(O_R2 + 3584 + i * 256, [33, 128], BF16) for i in range(2)]
    ohA = K.at(O_R1, [32, 5120], BF16)
    ohC = K.at(O_R1 + 10240, [33, 3 * 384], BF16)
    stg = [K.at(O_R3 + i * 2048, [128, 512], F32) for i in range(2)]
    K.add("dve", lambda e: e.memset(tabx[:, :], -30000.0), writes=["tabx"])
    K.load(tabx[0:32, :], table_d, "tabx")
    K.add("dve", lambda e: e.memset(ones33[:, :], 1.0), writes=["ones33"])
    K.load(ohA[:, :], ohA_d, "ohA")
    K.load(ohC[:, :], ohC_d, "ohC")
    n = 0
    for c in range(28):
        i2 = c % 2
        tk = f"tabrep{i2}"
        K.ts(trf[i2][:, :], ones33[:, :], tabx[:, c:c + 1], ALU.mult, ["ones33", "tabx"], [tk + "f"])
        K.cp(trh[i2][:, :], trf[i2][:, :], [tk + "f"], [tk + "h"])
        K.tt(trr[i2][:, :], trf[i2][:, :], trh[i2][:, :], ALU.subtract, [tk + "f", tk + "h"], [tk + "r"])
        K.cp(trl[i2][:, :], trr[i2][:, :], [tk + "r"], [tk + "l"])
        tks = [tk + "h", tk + "l"]
        if c < 4:
            for blk in range(10):
                bi = n % 2
                pb = K.bank[bi]
                rhs = ohA[:, blk * 512:(blk + 1) * 512]
                K.mm([(pb[:, :], trh[i2][0:32, :], rhs, True, False), (pb[:, :], trl[i2][0:32, :], rhs, False, True)],
                     tks + ["ohA"], [f"bank{bi}"])
                K.act(stg[bi][:, :], pb[:, :], AF.Identity, [f"bank{bi}"], [f"stg{bi}"])
                K.store(TR2[c * 128:(c + 1) * 128, blk * 512:(blk + 1) * 512], stg[bi][:, :], [f"stg{bi}"], f"TR2_{c}_{blk}")
                n += 1
        else:
            g = (c - 4) // 8
            bi = n % 2
            pb = K.bank[bi]
            rhs = ohC[:, g * 384:(g + 1) * 384]
            K.mm([(pb[:, 0:384], trh[i2][:, :], rhs, True, False), (pb[:, 0:384], trl[i2][:, :], rhs, False, True)],
                 tks + ["ohC"], [f"bank{bi}"])
            K.act(stg[bi][:, 0:384], pb[:, 0:384], AF.Identity, [f"bank{bi}"], [f"stg{bi}"])
            K.store(TRC[(c - 4) * 128:(c - 3) * 128, :], stg[bi][:, 0:384], [f"stg{bi}"], f"TRC_{c}")
            n += 1
    K.tr2_keys = [f"TR2_{c}_{blk}" for c in range(4) for blk in range(10)]
    K.trc_keys = [f"TRC_{c}" for c in range(4, 28)]


def ab_phase(K, L, vec_d, ab_d, lam_d, win, wout, xb_loc, xb_g, TR2, ropeC_d, ropeS_d, PT_d, pid):
    cf = K.cf
    hT = K.hT
    B = K.bank
    ws = K.ws
    K.barrier()
    load_vec(K, vec_d, 0, 16, C_GA, "gA")
    load_vec(K, vec_d, 16, 16, C_GB, "gB")
    K.load(cf[:, C_GS0:C_GS0 + 4], ab_d, "abv")
    lam = K.at(O_CV, [1, 512], F32)
    lw = K.at(O_CV + 2048, [1, 256], F32)
    ls = K.at(O_CV + 3072, [1, 8], F32)
    onef = K.at(O_CV + 3200, [1, 128], F32)
    K.load(lam[:, :], lam_d, "lam")
    K.add("dve", lambda e: e.memset(onef[:, :], 1.0), writes=["onef"])
    K.tt(lw[:, 0:128], lam[:, 0:128], lam[:, 128:256], ALU.mult, ["lam"], ["lw"])
    K.tt(lw[:, 128:256], lam[:, 256:384], lam[:, 384:512], ALU.mult, ["lam", "lw"], ["lw"])
    K.add("dve", lambda e: e.tensor_reduce(out=ls[:, 0:2], in_=lw[:, :].rearrange("p (a b) -> p a b", a=2),
                                           axis=mybir.AxisListType.X, op=ALU.add), reads=["lw"], writes=["ls"])
    K.act(ls[:, 2:4], ls[:, 0:2], AF.Exp, ["ls"], ["ls2"])
    K.tt(ls[:, 4:5], ls[:, 2:3], ls[:, 3:4], ALU.subtract, ["ls2"], ["ls3"])
    K.ts(ls[:, 5:6], ls[:, 4:5], -1.0, ALU.mult, ["ls3"], ["ls4"], s2=-LAMBDA_INIT[L], op1=ALU.add)
    K.mm([(B[0][:, 0:1], onef[0:1, :], ls[0:1, 5:6], True, True)], ["onef", "ls4"], ["bank0"])
    K.act(cf[:, C_NLAM:C_NLAM + 1], B[0][:, 0:1], AF.Identity, ["bank0"], ["nlam"])
    K.ts(cf[:, C_GS0:C_GS0 + 2], cf[:, C_GS0:C_GS0 + 2], 1.0 - LAMBDA_INIT[L], ALU.mult, ["abv"], ["abv"])
    xn = K.at(O_R1, [128, DC, T], BF16)
    xn_keys = pre_norm_full(K, xn)
    K.barrier()
    qA = K.at(O_R2, [128, 8, T], BF16)
    qB = K.at(O_R2 + 16384, [128, 8, T], BF16)
    ropeC = K.at(O_R2 + 32768, [128, T], F32)
    ropeS = K.at(O_R2 + 36864, [128, T], F32)
    PT = K.at(O_R2 + 40960, [128, 128], BF16)
    K.load(ropeC[:, :], ropeC_d, "ropeC")
    K.load(ropeS[:, :], ropeS_d, "ropeS")
    K.load(PT[:, :], PT_d, "PT")
    stgk = [K.at(O_R3 + i * 2048, [128, T], BF16) for i in range(2)]
    qraw = [K.at(O_R3 + 4096 + i * 2048, [128, 512], F32) for i in range(2)]
    qn = [K.at(O_R3 + 8192 + i * 2048, [128, 512], F32) for i in range(2)]
    sqb = [K.at(O_R3 + 12288 + i * 1024, [128, 512], BF16) for i in range(2)]
    t1 = [K.at(O_R3 + 14336 + i * 2048, [128, 512], F32) for i in range(2)]
    stgv = [K.at(O_R3 + 18432 + i * 512, [128, 256], BF16) for i in range(4)]
    qh = [K.at(O_R3 + 20480 + i * 1024, [128, 512], BF16) for i in range(2)]
    ql = [K.at(O_R3 + 22528 + i * 1024, [128, 512], BF16) for i in range(2)]
    qr = [K.at(O_R3 + 24576 + i * 2048, [128, 512], F32) for i in range(2)]
    xkeys = []
    nb = 0
    nk = 0
    for cb in list(range(0, 16)) + list(range(24, 34)):
        wt, wk = ws.fetch(win, DC, cb * 128, 128)
        for hb in range(2):
            E0 = hb * 512
            bi = nb % 2
            nb += 1
            pb = B[bi]
            bk = f"bank{bi}"
            K.mm([(pb[:, :], wt[:, kc, :], xn[:, kc, E0:E0 + 512], kc == 0, kc == DC - 1) for kc in range(DC)],
                 wk + xn_keys, [bk])
            if cb < 8:
                K.act(qA[:, cb, E0:E0 + 512], pb[:, :], AF.Identity, [bk], [f"qA{cb}_{hb}"])
            elif cb < 16:
                sk = stgk[(cb) % 2]
                K.act(sk[:, E0:E0 + 512], pb[:, :], AF.Identity, [bk], [f"stgk{cb % 2}_{hb}"])
            else:
                i2 = nk % 2
                nk += 1
                gcol = C_GQ if cb < 32 else C_GK
                K.act(qraw[i2][:, :], pb[:, :], AF.Identity, [bk], [f"qraw{i2}"])
                K.act(sqb[i2][:, :], qraw[i2][:, :], AF.Square, [f"qraw{i2}"], [f"sqb{i2}"])
                K.mm([(B[2][:, :], K.ones[:], sqb[i2][:, :], True, True)], [f"sqb{i2}", "ones"], ["bank2"])
                K.act(K.rstd[:, :], B[2][:, :], AF.Sqrt, ["bank2", "cf_eps"], ["rstd"], bias=cf[:, C_EPS:C_EPS + 1],
                      scale=1.0 / 128)
                K.rc(K.rstd[:, :], K.rstd[:, :], ["rstd"], ["rstd"])
                K.stt(qn[i2][:, :], qraw[i2][:, :], cf[:, gcol:gcol + 1], K.rstd[:, :], ALU.mult, ALU.mult,
                      [f"qraw{i2}", "rstd", "abv"], [f"qn{i2}"])
                K.cp(qh[i2][:, :], qn[i2][:, :], [f"qn{i2}"], [f"qh{i2}"])
                K.tt(qr[i2][:, :], qn[i2][:, :], qh[i2][:, :], ALU.subtract, [f"qn{i2}", f"qh{i2}"], [f"qr{i2}"])
                K.cp(ql[i2][:, :], qr[i2][:, :], [f"qr{i2}"], [f"ql{i2}"])
                K.mm([(B[3][:, :], PT[:, :], qh[i2][:, :], True, False), (B[3][:, :], PT[:, :], ql[i2][:, :], False, True)],
                     ["PT", f"qh{i2}", f"ql{i2}"], ["bank3"])
                K.tt(t1[i2][:, :], qn[i2][:, :], ropeC[:, E0:E0 + 512], ALU.mult, [f"qn{i2}", "ropeC"], [f"t1{i2}"])
                K.tt(qn[i2][:, :], B[3][:, :], ropeS[:, E0:E0 + 512], ALU.mult, ["bank3", "ropeS", f"qn{i2}"], [f"qn{i2}"])
                if cb < 32:
                    K.tt(qB[:, cb - 24, E0:E0 + 512], t1[i2][:, :], qn[i2][:, :], ALU.add, [f"t1{i2}", f"qn{i2}"],
                         [f"qB{cb - 24}_{hb}"])
                else:
                    sk = stgk[cb % 2]
                    K.tt(sk[:, E0:E0 + 512], t1[i2][:, :], qn[i2][:, :], ALU.add, [f"t1{i2}", f"qn{i2}"],
                         [f"stgk{cb % 2}_{hb}"])
        if 8 <= cb < 16 or cb >= 32:
            row0 = (cb - 8) * 128 if cb < 16 else 1024 + (cb - 32) * 128
            sk = stgk[cb % 2]
            K.store(xb_loc[row0:row0 + 128, :], sk[:, :], [f"stgk{cb % 2}_0", f"stgk{cb % 2}_1"], f"xb_k{cb}")
            xkeys.append(f"xb_k{cb}")
    nv = 0
    for cg in range(5):
        col0 = 2048 + cg * 256 if cg < 4 else 4352
        wt, wk = ws.fetch(win, DC, col0, 256)
        for tt in range(8):
            bi = 4 + nv % 2
            pb = B[bi]
            K.mm([(pb[:, 0:256], xn[:, kc, tt * 128:(tt + 1) * 128], wt[:, kc, :], kc == 0, kc == DC - 1)
                  for kc in range(DC)], wk + xn_keys, [f"bank{bi}"])
            sv = stgv[nv % 4]
            K.act(sv[:, :], pb[:, 0:256], AF.Identity, [f"bank{bi}"], [f"stgv{nv % 4}"])
            if cg < 4:
                dst = xb_loc[1280 + tt * 128:1280 + (tt + 1) * 128, cg * 256:(cg + 1) * 256]
            else:
                dst = xb_loc[2304:2560, :].rearrange("a (b f) -> (a b) f", f=256)[tt * 128:(tt + 1) * 128, :]
            K.store(dst, sv[:, :], [f"stgv{nv % 4}"], f"xb_v{cg}_{tt}")
            xkeys.append(f"xb_v{cg}_{tt}")
            nv += 1
    K.add("pool", lambda e: e.collective_compute("AllGather", ALU.bypass, replica_groups=[list(range(NCORES))],
                                                  ins=[xb_loc], outs=[xb_g]), reads=xkeys, writes=["xbg"], sig=True)
    K.barrier()
    oT = K.at(O_R1, [128, DC, T], BF16)
    kv0 = O_R2 + 32768
    Kt = K.at(kv0, [128, 2, SEQ], BF16)
    Vt = K.at(kv0 + 16384, [128, 32, 256], BF16)
    G = K.at(O_WS, [128, 5120], F32)
    eo = O_R2 + 65536
    Et = [K.at(O_CV + i * 1024, [128, 512], BF16) for i in range(4)]
    tmp = [K.at(O_CV + 4096 + i * 2048, [128, 512], F32) for i in range(2)]
    rr = [K.at(eo + i * 2048, [128, 512], F32) for i in range(2)]
    od = K.at(eo + 4096, [128, 2, 512], F32)
    tq = [K.at(eo + 8192 + i * 2048, [128, 512], F32) for i in range(2)]
    sqo = K.at(O_WS + 20480, [128, 2, 512], BF16)
    xg3d = xb_g.rearrange("(r n) f -> r n f", r=NCORES)
    xg3 = K.xb_win

    def cpwin(e):
        b4 = getpv(e)['b4']
        return e.dma_start(out=xg3[:, :, :], in_=xg3d[bass.ds(b4, 4), :, :])

    K.add("sp", cpwin, reads=["xbg"], writes=["xbw"], dma=True)
    okeys = []
    ne = 0
    for h in range(4):
        for m in range(2):
            def ldk(e, h=h, m=m):
                return e.dma_start(out=Kt[:, m, :].rearrange("p (r f) -> p r f", r=4),
                                   in_=xg3[:, (h * 2 + m) * 128:(h * 2 + m + 1) * 128, :].rearrange(
                                       "r p f -> p r f"))

            K.add("sp", ldk, reads=["xbw"], writes=["Kt"], dma=True)
        for r in range(4):
            def ldv(e, h=h, r=r):
                return e.dma_start(out=Vt[:, r * 8:(r + 1) * 8, :],
                                   in_=xg3[:, 1280:2304, h * 256:(h + 1) * 256][r].rearrange(
                                       "(t p) f -> p t f", p=128))

            K.add("sp", ldv, reads=["xbw"], writes=["Vt"], dma=True)
        gsrc = bass.AP(tensor=TR2.tensor, offset=h * 128 * 5120 + 127, ap=[[5119, 128], [1, 4993]])
        K.add("sp", lambda e, gsrc=gsrc: e.dma_start(out=G[:, 0:4993], in_=gsrc), reads=K.tr2_keys, writes=["G"], dma=True)
        for qb in range(2):
            Q0 = qb * 512
            for kt in range(32):
                m0 = 3969 - kt * 128 + qb * 512
                for m in range(2):
                    sb_ = B[m]
                    K.mm([(sb_[:, :], Kt[:, m, kt * 128:(kt + 1) * 128], qA[:, h * 2 + m, Q0:Q0 + 512], True, True)],
                         ["Kt", f"qA{h * 2 + m}_{qb}"], [f"bank{m}"])
                    tm = tmp[m]
                    K.stt(tm[:, :], sb_[:, :], SCALE, G[:, m0:m0 + 512], ALU.mult, ALU.add, [f"bank{m}", "G"], [f"tmp{m}"])
                    et = Et[ne % 4]
                    ek = f"Et{ne % 4}"
                    ne += 1
                    K.act(et[:, :], tm[:, :], AF.Exp, [f"tmp{m}"], [ek])
                    st, sp = kt == 0, kt == 31
                    K.mm([(B[2 + 3 * m][:, :], Vt[:, kt, 0:128], et[:, :], st, sp),
                          (B[3 + 3 * m][:, :], Vt[:, kt, 128:256], et[:, :], st, sp),
                          (B[4 + 3 * m][:, :], K.ones[:], et[:, :], st, sp)],
                         ["Vt", ek, "ones"], [f"bank{2 + 3 * m}", f"bank{3 + 3 * m}", f"bank{4 + 3 * m}"])
            for m in range(2):
                K.rc(rr[m][:, :], B[4 + 3 * m][:, :], [f"bank{4 + 3 * m}"], [f"rr{m}"])
            for dv in range(2):
                K.tt(tq[0][:, :], B[2 + dv][:, :], rr[0][:, :], ALU.mult, [f"bank{2 + dv}", "rr0"], ["tq0"])
                K.tt(tq[1][:, :], B[5 + dv][:, :], rr[1][:, :], ALU.mult, [f"bank{5 + dv}", "rr1"], ["tq1"])
                K.stt(od[:, dv, :], tq[1][:, :], cf[:, C_NLAM:C_NLAM + 1], tq[0][:, :], ALU.mult, ALU.add,
                      ["tq0", "tq1", "nlam"], [f"od{dv}"])
            rmsnorm_stats(K, od[:, :, :], ["od0", "od1"], 512, sqo[:, :, :], ["sqo"], 0, 1.0 / 256)
            for dv in range(2):
                K.stt(oT[:, h * 2 + dv, Q0:Q0 + 512], od[:, dv, :], cf[:, C_GS0 + dv:C_GS0 + dv + 1], K.rstd[:, :],
                      ALU.mult, ALU.mult, [f"od{dv}", "rstd", "abv"], [f"oT{h * 2 + dv}_{qb}"])
                okeys.append(f"oT{h * 2 + dv}_{qb}")
    for kv in range(2):
        def ldk(e, kv=kv):
            return e.dma_start(out=Kt[:, 0, :].rearrange("p (r f) -> p r f", r=4),
                               in_=xg3[:, 1024 + kv * 128:1024 + (kv + 1) * 128, :].rearrange(
                                   "r p f -> p r f"))

        K.add("sp", ldk, reads=["xbw"], writes=["Kt"], dma=True)

        for r in range(4):
            def ldv(e, kv=kv, r=r):
                src = xg3[:, 2304:2560, :][r].rearrange("a (b f) -> (a b) f", f=256)
                return e.dma_start(out=Vt[:, r * 8:(r + 1) * 8, 0:128],
                                   in_=src[:, kv * 128:(kv + 1) * 128].rearrange("(t p) f -> p t f", p=128))

            K.add("sp", ldv, reads=["xbw"], writes=["Vt"], dma=True)
        for g in range(4):
            hq = kv * 4 + g
            for qb in range(2):
                Q0 = qb * 512
                ob, db = (4, 5) if (g * 2 + qb) % 2 == 0 else (6, 7)
                for kt in range(32):
                    sbk = kt % 2
                    sb_ = B[sbk]
                    K.mm([(sb_[:, :], Kt[:, 0, kt * 128:(kt + 1) * 128], qB[:, hq, Q0:Q0 + 512], True, True)],
                         ["Kt", f"qB{hq}_{qb}"], [f"bank{sbk}"])
                    et = Et[ne % 4]
                    ek = f"Et{ne % 4}"
                    ne += 1
                    K.act(et[:, :], sb_[:, :], AF.Exp, [f"bank{sbk}"], [ek], scale=SCALE)
                    st, sp = kt == 0, kt == 31
                    K.mm([(B[ob][:, :], Vt[:, kt, 0:128], et[:, :], st, sp), (B[db][:, :], K.ones[:], et[:, :], st, sp)],
                         ["Vt", ek, "ones"], [f"bank{ob}", f"bank{db}"])
                ri = (g * 2 + qb) % 2
                K.rc(rr[ri][:, :], B[db][:, :], [f"bank{db}"], [f"rr{ri}"])
                K.tt(oT[:, 8 + hq, Q0:Q0 + 512], B[ob][:, :], rr[ri][:, :], ALU.mult, [f"bank{ob}", f"rr{ri}"],
                     [f"oT{8 + hq}_{qb}"])
                okeys.append(f"oT{8 + hq}_{qb}")
    out_proj_phase(K, wout, 16, oT, okeys)


def c_phase(K, L, vec_d, win, wout, xc_loc, xc_pad, TRC):
    cf = K.cf
    B = K.bank
    ws = K.ws
    K.barrier()
    load_vec(K, vec_d, 0, 16, C_GA, "gA")
    load_vec(K, vec_d, 16, 16, C_GB, "gB")
    xn = K.at(O_R1, [128, DC, T], BF16)
    xn_keys = pre_norm_full(K, xn)
    K.barrier()
    qC = K.at(O_R2, [128, 24, T], BF16)
    so = O_R2 + 49152
    stgk = [K.at(so + i * 2048, [128, T], BF16) for i in range(2)]
    stgv = [K.at(so + 4096 + i * 512, [128, 256], BF16) for i in range(4)]
    xkeys = []
    nb = 0
    for g in range(3):
        d = DILS[g]
        for s_ in range(2):
            for h in range(8):
                col0 = ((g * 3 + s_) * 8 + h) * 128
                wt, wk = ws.fetch(win, DC, col0, 128)
                for hb in range(2):
                    bi = nb % 2
                    nb += 1
                    pb = B[bi]
                    K.mm([(pb[:, :], wt[:, kc, :], xn[:, kc, hb * 512:(hb + 1) * 512], kc == 0, kc == DC - 1)
                          for kc in range(DC)], wk + xn_keys, [f"bank{bi}"])
                    src = pb[:, :].rearrange("p (l r) -> p r l", r=d)
                    if s_ == 0:
                        dst = qC[:, g * 8 + h, :]
                        wkey = f"qC{g * 8 + h}_{hb}"
                    else:
                        dst = stgk[h % 2][:, :]
                        wkey = f"stgk{h % 2}_{hb}"
                    dst = dst.rearrange("p (r l) -> p r l", r=d)[:, :, hb * (512 // d):(hb + 1) * (512 // d)]
                    K.act(dst, src, AF.Identity, [f"bank{bi}"], [wkey])
                if s_ == 1:
                    row0 = (g * 8 + h) * 128
                    K.store(xc_loc[row0:row0 + 128, :], stgk[h % 2][:, :], [f"stgk{h % 2}_0", f"stgk{h % 2}_1"],
                            f"xc_k{g}_{h}")
                    xkeys.append(f"xc_k{g}_{h}")
        nv = 0
        for cg in range(4):
            col0 = ((g * 3 + 2) * 8) * 128 + cg * 256
            wt, wk = ws.fetch(win, DC, col0, 256)
            for tt in range(8):
                bi = 4 + nv % 2
                pb = B[bi]
                steps = []
                for kc in range(DC):
                    xv = xn[:, kc, :].rearrange("p (l r) -> p r l", r=d)
                    if d == 1:
                        lt = xn[:, kc, tt * 128:(tt + 1) * 128]
                    elif d == 4:
                        lt = xv[:, tt // 2, (tt % 2) * 128:(tt % 2) * 128 + 128]
                    else:
                        lt = xn[:, kc, :].rearrange("p (m e) -> p e m", e=8)[:, tt, :]
                    steps.append((pb[:, 0:256], lt, wt[:, kc, :], kc == 0, kc == DC - 1))
                K.mm(steps, wk + xn_keys, [f"bank{bi}"])
                sv = stgv[nv % 4]
                K.act(sv[:, :], pb[:, 0:256], AF.Identity, [f"bank{bi}"], [f"stgv{nv % 4}"])
                r0 = 3072 + g * 1024 + tt * 128
                K.store(xc_loc[r0:r0 + 128, cg * 256:(cg + 1) * 256], sv[:, :], [f"stgv{nv % 4}"], f"xc_v{g}_{cg}_{tt}")
                xkeys.append(f"xc_v{g}_{cg}_{tt}")
                nv += 1
    K.add("pool", lambda e: e.collective_compute("AllGather", ALU.bypass, replica_groups=[list(range(NCORES))],
                                                  ins=[xc_loc], outs=[xc_pad[6144:6144 * 9, :]]),
          reads=xkeys, writes=["xcg"], sig=True)
    K.barrier()
    oT = K.at(O_R1, [128, 8, T], BF16)
    accN = K.at(O_R1 + 16384, [128, T], F32)
    accD = K.at(O_R1 + 20480, [128, T], F32)
    BAB = [[K.at(O_R1 + 24576 + (i * 3 + g) * 1024, [128, 256], F32) for g in range(3)] for i in range(2)]
    Kw = [K.at(so + i * 6144, [128, 3 * T], BF16) for i in range(2)]
    Vw = [K.at(so + 12288 + i * 8192, [128, 32, 128], BF16) for i in range(2)]
    Et = [K.at(O_CV + i * 1024, [128, 512], BF16) for i in range(4)]
    tmp = [K.at(O_CV + 4096 + i * 2048, [128, 512], F32) for i in range(2)]
    xc3d = xc_pad.rearrange("(s n) f -> s n f", s=10)
    xc3 = K.xc_win

    def cpwin(e):
        pid = getpv(e)['pid']
        return e.dma_start(out=xc3[:, :, :], in_=xc3d[bass.ds(pid, 3), :, :])

    K.add("sp", cpwin, reads=["xcg"], writes=["xcw"], dma=True)
    okeys = []
    ne = 0
    nkv = 0
    ns = 0
    for h in range(8):
        hb_ = h % 2
        for g in range(3):
            gh = g * 8 + h
            for ab, off in ((0, 255), (1, 127)):
                src = bass.AP(tensor=TRC.tensor, offset=gh * 128 * 384 + off, ap=[[383, 128], [1, 128]])
                K.add("sp", lambda e, src=src, dst=BAB[hb_][g][:, ab * 128:(ab + 1) * 128]: e.dma_start(out=dst, in_=src),
                      reads=K.trc_keys, writes=[f"BAB{hb_}_{g}"], dma=True)
        for g in range(3):
            d = DILS[g]
            ns_ = T // d
            kb = nkv % 2
            nkv += 1
            kwk, vwk = f"Kw{kb}", f"Vw{kb}"
            kw = Kw[kb]
            vw = Vw[kb]
            r0k = (g * 8 + h) * 128

            for sl in range(3):
                def ldk(e, kw=kw, r0k=r0k, d=d, sl=sl):
                    return e.dma_start(out=kw[:, :].rearrange("p (r s l) -> p r s l", r=d, s=3)[:, :, sl, :],
                                       in_=xc3[:, r0k:r0k + 128, :][sl].rearrange("p (r l) -> p r l", r=d))

                K.add("sp", ldk, reads=["xcw"], writes=[kwk], dma=True)
            tpr = {1: 9, 4: 3, 16: 2}[d]
            vw4 = vw[:, 0:d * tpr, :].rearrange("p (r t) f -> p r t f", r=d)
            rv0 = 3072 + g * 1024
            c0 = h * 128
            if d == 1:
                segs = [(0, 960, 1024, 0, 0), (1, 0, 64, 0, 64), (1, 64, 960, 1, 0), (1, 960, 1024, 8, 0), (2, 0, 64, 8, 64)]
            elif d == 4:
                segs = [(0, 192, 256, 0, 0), (1, 0, 64, 0, 64), (1, 64, 192, 1, 0), (1, 192, 256, 2, 0), (2, 0, 64, 2, 64)]
            else:
                segs = [(0, 0, 64, 0, 0), (1, 0, 64, 0, 64), (2, 0, 64, 1, 0)]
            for (sl, la, lb, ti, p0) in segs:
                nrow = lb - la

                if d == 16:
                    for b_ in range(2):
                        def ldv(e, sl=sl, ti=ti, p0=p0, vw4=vw4, rv0=rv0, c0=c0, b_=b_):
                            src = xc3[:, rv0:rv0 + T, c0:c0 + 128][sl].rearrange("(t l b) f -> b l t f", t=8, b=2)[b_]
                            return e.dma_start(out=vw4[p0:p0 + 64, b_ * 8:(b_ + 1) * 8, ti, :], in_=src)

                        K.add("sp", ldv, reads=["xcw"], writes=[vwk], dma=True)
                    continue

                def ldv(e, sl=sl, la=la, nrow=nrow, ti=ti, p0=p0, vw4=vw4, d=d, ns_=ns_, rv0=rv0, c0=c0):
                    src = xc3[:, rv0:rv0 + T, c0:c0 + 128][sl].rearrange("(r l) f -> r l f", r=d)
                    if nrow <= 128:
                        return e.dma_start(out=vw4[p0:p0 + nrow, :, ti, :], in_=src[:, la:la + nrow, :].rearrange("r l f -> l r f"))
                    nt = nrow // 128
                    return e.dma_start(out=vw4[:, :, ti:ti + nt, :],
                                       in_=src[:, la:la + nrow, :].rearrange("r (t p) f -> p r t f", p=128))

                K.add("sp", ldv, reads=["xcw"], writes=[vwk], dma=True)
            kw3 = kw[:, :].rearrange("p (r w) -> p r w", r=d)
            QN = min(128, ns_)
            nqt = max(1, ns_ // 128)
            for r in range(d):
                for qt in range(nqt):
                    q0 = r * ns_ + qt * 128
                    qap = qC[:, g * 8 + h, q0:q0 + QN]
                    if d == 16:
                        wA, KB, tA, tB = 0, 64, 0, 1
                    else:
                        wA, KB, tA, tB = ns_ + qt * 128 - 64, 128, qt, qt + 1
                    sbi = ns % 2
                    ns += 1
                    sb_ = B[sbi]
                    K.mm([(sb_[:, 0:QN], kw3[:, r, wA:wA + 128], qap, True, True),
                          (sb_[0:KB, 128:128 + QN], kw3[:, r, wA + 128:wA + 128 + KB], qap, True, True)],
                         [kwk, f"qC{g * 8 + h}_0", f"qC{g * 8 + h}_1"], [f"bank{sbi}"])
                    tm = tmp[sbi]
                    bab = BAB[hb_][g]
                    K.stt(tm[:, 0:QN], sb_[:, 0:QN], SCALE, bab[:, 0:QN], ALU.mult, ALU.add, [f"bank{sbi}", f"BAB{hb_}_{g}"],
                          [f"tmp{sbi}"])
                    K.stt(tm[0:KB, 128:128 + QN], sb_[0:KB, 128:128 + QN], SCALE, bab[0:KB, 128:128 + QN], ALU.mult, ALU.add,
                          [f"bank{sbi}", f"BAB{hb_}_{g}", f"tmp{sbi}"], [f"tmp{sbi}"])
                    et = Et[ne % 4]
                    ek = f"Et{ne % 4}"
                    ne += 1
                    mA = C_ML if qt == 0 else C_M0
                    mB = C_MRLO if d == 16 else (C_MR if qt == nqt - 1 else C_M0)
                    K.act(et[:, 0:QN], tm[:, 0:QN], AF.Exp, [f"tmp{sbi}", "pc"], [ek], bias=cf[:, mA:mA + 1])
                    K.act(et[0:KB, 128:128 + QN], tm[0:KB, 128:128 + QN], AF.Exp, [f"tmp{sbi}", "pc", ek], [ek],
                          bias=cf[0:KB, mB:mB + 1])
                    if d == 1:
                        nbk, cc0 = 2 + qt // 4, (qt % 4) * 128
                    else:
                        nbk, cc0 = 2, qt * 128
                    K.mm([(B[nbk][:, cc0:cc0 + QN], vw4[:, r, tA, :], et[:, 0:QN], True, False),
                          (B[nbk][:, cc0:cc0 + QN], vw4[0:KB, r, tB, :], et[0:KB, 128:128 + QN], False, True),
                          (B[nbk + 2][:, cc0:cc0 + QN], K.ones[:, :], et[:, 0:QN], True, False),
                          (B[nbk + 2][:, cc0:cc0 + QN], K.ones[0:KB, :], et[0:KB, 128:128 + QN], False, True)],
                         [vwk, ek, "ones"], [f"bank{nbk}", f"bank{nbk + 2}"])
                aN = accN[:, :].rearrange("p (l r) -> p r l", r=d)[:, r, :]
                aD = accD[:, :].rearrange("p (l r) -> p r l", r=d)[:, r, :]
                pieces = [(2, 0, 512, 0), (3, 0, 512, 512)] if d == 1 else [(2, 0, ns_, 0)]
                for (bk_, ca, n_, oa) in pieces:
                    for (acc_, bo, key) in ((aN, 0, "accN"), (aD, 2, "accD")):
                        if g == 0:
                            K.cp(acc_[:, oa:oa + n_], B[bk_ + bo][:, ca:ca + n_], [f"bank{bk_ + bo}"], [key])
                        else:
                            K.tt(acc_[:, oa:oa + n_], B[bk_ + bo][:, ca:ca + n_], acc_[:, oa:oa + n_], ALU.add,
                                 [f"bank{bk_ + bo}", key], [key])
        K.rc(accD[:, :], accD[:, :], ["accD"], ["accD"])
        K.tt(oT[:, h, :], accN[:, :], accD[:, :], ALU.mult, ["accN", "accD"], [f"oT{h}"])
        okeys.append(f"oT{h}")
    out_proj_phase(K, wout, 8, oT, okeys)


W_SHAPES = {}


def weight_specs(layers):
    specs = []
    for L in layers:
        if L % 2 == 0:
            specs += [(f"l{L}_w_in", D, 4608), (f"l{L}_w_out", 2048, D)]
        else:
            specs += [(f"l{L}_w_in", D, 9216), (f"l{L}_w_out", 1024, D)]
        specs += [(f"l{L}_w_up", D, 2 * DFF), (f"l{L}_w_down", DFF, D)]
    return specs


def build_mega(layers=(0, 1, 2, 3), do_ffn=True, do_mix=True):
    K = Ctx()
    nc = K.nc
    setup_common(K)
    xT = K.dram_in("xT", [128, DC, T], F32)
    outT = K.dram_out("outT", [128, DC, T], F32)
    table = K.dram_in("table", [32, 28], F32)
    ohA = K.dram_in("ohA", [32, 5120], BF16)
    ohC = K.dram_in("ohC", [33, 3 * 384], BF16)
    ropeC = K.dram_in("ropeC", [128, T], F32)
    ropeS = K.dram_in("ropeS", [128, T], F32)
    PT = K.dram_in("PT", [128, 128], BF16)
    pc = K.dram_in("pc", [128, 6], F32)
    vec = {L: K.dram_in(f"vec{L}", [128, 416], F32) for L in layers}
    abv = {L: K.dram_in(f"ab{L}", [128, 4], F32) for L in layers if L % 2 == 0}
    lam = {L: K.dram_in(f"lam{L}", [1, 512], F32) for L in layers if L % 2 == 0}
    wsh, wfull, wloc = {}, {}, {}
    for (name, rows, cols) in weight_specs(layers):
        wsh[name] = K.dram_in(name, [rows // NCORES, cols], F32)
        wloc[name] = K.dram(name + "_loc", [rows // NCORES, cols], F32)
        wfull[name] = K.dram(name + "_full", [rows, cols], F32)
    TR2 = K.dram("TR2", [4 * 128, 5120], F32)
    TRC = K.dram("TRC", [24 * 128, 384], F32)
    xb_loc = K.dram("xb_loc", [2560, 1024], BF16)
    xb_g = K.dram("xb_g", [NCORES * 2560, 1024], BF16)
    xc_loc = K.dram("xc_loc", [6144, 1024], BF16)
    xc_pad = K.dram("xc_pad", [10 * 6144, 1024], BF16)
    xe_loc = K.dram("xe_loc", [256, 16], BF16)
    xg_pad = K.dram("xg_pad", [10 * 256, 16], BF16)
    K.xb_win = K.dram("xb_win", [4, 2560, 1024], BF16)
    K.xc_win = K.dram("xc_win", [3, 6144, 1024], BF16)
    K.xg_win = K.dram("xg_win", [3, 2, 128, 16], BF16)
    for c in range(DC):
        K.load(K.hT[:, c, :], xT[:, c, :], f"h{c}")
    K.load(K.cf[:, C_VL:C_VL + 6], pc, "pc")
    zt = K.at(O_R2, [128, 8, 1024], BF16)
    K.add("pool", lambda e: e.memset(zt[:, :, :], 0.0), writes=["zt"])
    for slot in (0, 9):
        for a in range(6):
            r0 = slot * 6144 + a * 1024
            K.store(xc_pad[r0:r0 + 1024, :].rearrange("(a p) f -> p a f", p=128), zt[:, :, :], ["zt"], f"zpad{slot}_{a}")
        K.store(xg_pad[slot * 256:(slot + 1) * 256, :].rearrange("(a p) f -> p a f", p=128), zt[:, 0:2, 0:16], ["zt"],
                f"zpadg{slot}")
    order = [n for (n, _, _) in weight_specs(layers)]

    def gather_weight(name):
        K.add("sp", lambda e: e.dma_start(out=wloc[name], in_=wsh[name]), writes=[name + "_loc"], dma=True)
        K.add("pool", lambda e: e.collective_compute("AllGather", ALU.bypass, replica_groups=[list(range(NCORES))],
                                                      ins=[wloc[name]], outs=[wfull[name]]),
              reads=[name + "_loc"], writes=[name + "_full"], sig=True)

    pending = list(order)

    def gather_next(n):
        for _ in range(n):
            if pending:
                gather_weight(pending.pop(0))

    gather_next(2)
    bias_setup(K, table, ohA, ohC, TR2, TRC)
    for L in layers:
        gather_next(2)
        if not do_mix:
            pass
        elif L % 2 == 0:
            ab_phase(K, L, vec[L], abv[L], lam[L], wfull[f"l{L}_w_in"], wfull[f"l{L}_w_out"], xb_loc, xb_g, TR2, ropeC,
                     ropeS, PT, None)
        else:
            c_phase(K, L, vec[L], wfull[f"l{L}_w_in"], wfull[f"l{L}_w_out"], xc_loc, xc_pad, TRC)
        gather_next(2)
        if do_ffn:
            ffn_phase(K, L, vec[L], wfull[f"l{L}_w_up"], wfull[f"l{L}_w_down"], xe_loc, xg_pad, None)
    K.barrier()
    for c in range(DC):
        K.store(outT[:, c, :], K.hT[:, c, :], [f"h{c}"])
    return K.finish()


def _rel_bucket(rel):
    rel = np.asarray(rel, np.int64)
    n = np.abs(rel)
    nf = np.maximum(n, 1).astype(np.float32)
    large = 8 + (np.log(nf / np.float32(8)) / np.float32(np.log(1024 / 8)) * np.float32(8)).astype(np.int32)
    large = np.minimum(large, 15)
    return np.where(rel > 0, 16, 0) + np.where(n < 8, n, large)


def _vec_fm(v, n):
    return np.ascontiguousarray(np.asarray(v, np.float32).reshape(n, 128).T)


def host_constants(core):
    q = core % 4
    c = {}
    n = np.arange(5120)
    i = 5119 - n
    rel = i - 1023 - q * 1024
    oh = np.zeros((32, 5120), np.float32)
    oh[_rel_bucket(rel), n] = 1.0
    c["ohA"] = oh.astype(BF)
    ohc = np.zeros((33, 3 * 384), np.float32)
    for g, d in enumerate(DILS):
        nn = np.arange(384)
        rs = 191 - nn
        ok = np.abs(rs) <= 64
        b = _rel_bucket(rs * d)
        ohc[b[ok], g * 384 + nn[ok]] = 1.0
        ohc[32, g * 384 + nn[~ok]] = 1.0
    c["ohC"] = ohc.astype(BF)
    t = q * 1024 + np.arange(1024)
    row = (t // 64).astype(np.float32)
    col = (t % 64).astype(np.float32)
    inv = (np.float32(10000.0) ** (-np.arange(32, dtype=np.float32) * np.float32(2.0) / np.float32(64))).astype(np.float32)
    C_ = np.zeros((128, 1024), np.float32)
    S_ = np.zeros((128, 1024), np.float32)
    for dim in range(128):
        pos = row if dim < 64 else col
        ang = pos * inv[dim % 32]
        C_[dim] = np.cos(ang)
        S_[dim] = np.sin(ang)
    c["ropeC"] = C_
    c["ropeS"] = S_
    P = np.zeros((128, 128), np.float32)
    for i_ in range(128):
        if (i_ % 64) < 32:
            P[i_ + 32, i_] = -1.0
        else:
            P[i_ - 32, i_] = 1.0
    c["PT"] = P.astype(BF)
    vl = 1.0 if q > 0 else 0.0
    vr = 1.0 if q < 3 else 0.0
    pcv = np.zeros((128, 6), np.float32)
    pcv[:, 0] = vl
    pcv[:, 1] = vr
    p = np.arange(128)
    pcv[:, 3] = np.where(p < 64, 0.0 if vl else -30000.0, 0.0)
    pcv[:, 4] = np.where(p >= 64, 0.0 if vr else -30000.0, 0.0)
    pcv[:, 5] = np.where(p < 64, 0.0 if vr else -30000.0, 0.0)
    c["pc"] = pcv
    return c


def make_inputs(inputs, layers=(0, 1, 2, 3)):
    x = np.asarray(inputs["x"], np.float32)
    maps = []
    for core in range(NCORES):
        b, q = divmod(core, 4)
        m = host_constants(core)
        blk = x[b, q * 1024:(q + 1) * 1024, :]
        m["xT"] = np.ascontiguousarray(blk.T.reshape(DC, 128, T).transpose(1, 0, 2))
        m["table"] = np.asarray(inputs["rel_bias_table"], np.float32)
        for L in layers:
            cwv = np.asarray(inputs[f"l{L}_conv_w"], np.float32)
            nf = cwv.shape[1] // 128
            cw = cwv.T.reshape(nf, 128, 3).transpose(1, 0, 2).reshape(128, nf * 3)
            vecs = [_vec_fm(inputs[f"l{L}_mix_pre_norm"], 16), _vec_fm(inputs[f"l{L}_mix_post_norm"], 16),
                    _vec_fm(inputs[f"l{L}_ffn_pre_norm"], 16), _vec_fm(inputs[f"l{L}_ffn_post_norm"], 16),
                    cw, _vec_fm(inputs[f"l{L}_conv_b"], nf)]
            v = np.concatenate(vecs, axis=1)
            vv = np.zeros((128, 416), np.float32)
            vv[:, :64] = v[:, :64]
            vv[:, 64:64 + nf * 3] = v[:, 64:64 + nf * 3]
            vv[:, 328:328 + nf] = v[:, 64 + nf * 3:]
            m[f"vec{L}"] = vv
            if L % 2 == 0:
                ab = np.zeros((128, 4), np.float32)
                ab[:, 0:2] = _vec_fm(inputs[f"l{L}_diff_subln"], 2)
                qk = np.asarray(inputs[f"l{L}_qk_norm"], np.float32)
                ab[:, 2] = qk[0]
                ab[:, 3] = qk[1]
                m[f"ab{L}"] = ab
                m[f"lam{L}"] = np.asarray(inputs[f"l{L}_diff_lambda"], np.float32).reshape(1, 512)
            for nm in ("w_in", "w_out", "w_up", "w_down"):
                w = np.asarray(inputs[f"l{L}_{nm}"], np.float32)
                rs = w.shape[0] // NCORES
                m[f"l{L}_{nm}"] = w[core * rs:(core + 1) * rs]
        maps.append(m)
    return maps


def kernel(**inputs):
    nc = build_mega()
    maps = make_inputs(inputs)
    res = run_bass_kernel_spmd(nc, maps, core_ids=list(range(NCORES)))
    out = np.zeros((2, SEQ, D), np.float32)
    for core in range(NCORES):
        b, q = divmod(core, 4)
        o = res.results[core]["outT"]
        out[b, q * 1024:(q + 1) * 1024, :] = o.transpose(1, 0, 2).reshape(D, T).T
    return out
```

```python
import contextlib
import numpy as np
import ml_dtypes
import concourse.bass as bass
import concourse.mybir as mybir
from concourse.bass_utils import run_bass_kernel_spmd

F32 = mybir.dt.float32
BF16 = mybir.dt.bfloat16
I32 = mybir.dt.int32
AF = mybir.ActivationFunctionType
ALU = mybir.AluOpType
BF = ml_dtypes.bfloat16

NCORES = 8
D = 2048
DC = 16
T = 1024
SEQ = 4096
DFF = 5632
FC = 44
EPS = 1e-6
NDMA_SEM = 12


class Op:
    __slots__ = ("eng", "fn", "deps", "needed", "count", "dma", "dsem", "dval", "prev_dma", "where")

    def __init__(self, eng, fn, dma):
        self.eng = eng
        self.fn = fn
        self.deps = []
        self.needed = False
        self.count = 0
        self.dma = dma
        self.dsem = None
        self.dval = 0
        self.prev_dma = None


class Sched:
    ENGS = ("pe", "act", "dve", "pool", "sp")

    def __init__(self):
        self.ops = []
        self.lastw = {}
        self.readers = {}
        self.dma_hist = {e: [] for e in self.ENGS}
        self.swq = []

    def add(self, eng, fn, reads=(), writes=(), dma=False, sig=False, ndesc=0):
        op = Op(eng, fn, dma)
        op.needed = sig
        import sys as _s
        f_ = _s._getframe(1)
        op.where = []
        while f_ is not None and len(op.where) < 5:
            op.where.append(f_.f_lineno)
            f_ = f_.f_back
        deps = {}
        for r in reads:
            w = self.lastw.get(r)
            if w is not None:
                deps[id(w)] = w
        for w_ in writes:
            lw = self.lastw.get(w_)
            if lw is not None:
                deps[id(lw)] = lw
            rd = self.readers.get(w_)
            if rd:
                for o in rd[0].values():
                    deps[id(o)] = o
                for o in rd[1]:
                    deps[id(o)] = o
        for r in reads:
            rd = self.readers.get(r)
            if rd is None:
                rd = self.readers[r] = ({}, [])
            if dma:
                rd[1].append(op)
            else:
                rd[0][eng] = op
        for w_ in writes:
            self.lastw[w_] = op
            self.readers[w_] = ({}, [])
        for d in deps.values():
            if d is op:
                continue
            if d.eng == "pe" and eng == "pe" and not d.dma and not dma:
                continue
            d.needed = True
            op.deps.append(d)
        if dma and ndesc:
            out = self.swq
            while out and sum(n for _, n in out) + ndesc > 700:
                o, _ = out.pop(0)
                op.deps.append(o)
            out.append((op, ndesc))
        if dma:
            hist = self.dma_hist[eng]
            j = len(hist)
            op.dsem = j % NDMA_SEM
            op.dval = 16 * (j // NDMA_SEM + 1)
            if j >= NDMA_SEM:
                op.prev_dma = hist[j - NDMA_SEM]
            hist.append(op)
        self.ops.append(op)
        return op

    def emit(self, nc):
        cnt = {e: 0 for e in self.ENGS}
        for op in self.ops:
            if op.dma:
                continue
            if op.needed:
                cnt[op.eng] += 1
                op.count = cnt[op.eng]
        per_eng = {e: [o for o in self.ops if o.eng == e] for e in self.ENGS}
        with contextlib.ExitStack() as st:
            esem = {e: st.enter_context(nc.semaphore("s_" + e)) for e in self.ENGS}
            dsem = {
                e: [st.enter_context(nc.semaphore(f"d_{e}_{i}")) for i in range(NDMA_SEM)]
                for e in self.ENGS
                if self.dma_hist[e]
            }
            block = st.enter_context(nc.Block())

            def run(engname, eng):
                waited = {}
                for op in per_eng[engname]:
                    ws = {}
                    deps = list(op.deps)
                    if op.prev_dma is not None:
                        deps.append(op.prev_dma)
                    for d in deps:
                        if d.dma:
                            key = ("d", d.eng, d.dsem)
                            val = d.dval
                        else:
                            key = ("e", d.eng)
                            val = d.count
                        if waited.get(key, 0) >= val:
                            continue
                        if ws.get(key, 0) < val:
                            ws[key] = val
                    for key, val in ws.items():
                        if key[0] == "d":
                            eng.wait_ge(dsem[key[1]][key[2]], val)
                        else:
                            eng.wait_ge(esem[key[1]], val)
                        waited[key] = val
                    try:
                        inst = op.fn(eng)
                    except Exception:
                        print('OP FAILED at lines', op.where, flush=True)
                        raise
                    if op.dma:
                        inst.then_inc(dsem[engname][op.dsem], 16)
                    elif op.needed:
                        inst.then_inc(esem[engname], 1)
                if engname == "sp":
                    for e in self.ENGS:
                        last = {}
                        for o in self.dma_hist[e]:
                            last[o.dsem] = o.dval
                        for s, v in last.items():
                            eng.wait_ge(dsem[e][s], v)
                        if cnt[e] > 0:
                            eng.wait_ge(esem[e], cnt[e])

            @block.tensor
            def _(e):
                run("pe", e)

            @block.scalar
            def _(e):
                run("act", e)

            @block.vector
            def _(e):
                run("dve", e)

            @block.gpsimd
            def _(e):
                run("pool", e)

            @block.sync
            def _(e):
                run("sp", e)


O_HT = 0
O_ONES = 65536
O_CF = O_ONES + 256
O_RSTD = 67584
O_R1 = 69632
O_R2 = 102464
O_R3 = 147520
O_WS = 180288
O_CV = 202816
O_END = 211008
WS_ELEMS = 11264

C_EPS, C_NLAM, C_VL, C_VR, C_M0, C_ML, C_MR, C_MRLO, C_GS0, C_GS1, C_GQ, C_GK = range(12)
C_GA = 16
C_GB = 32
C_CW = 48
C_CB = 48 + 264

LAMBDA_INIT = [0.8 - 0.6 * float(np.exp(-0.3 * i)) for i in range(4)]
SCALE = 128 ** -0.5
DILS = (1, 4, 16)


class Ctx:
    def __init__(self):
        self.nc = bass.Bass("TRN2", target_bir_lowering=False)
        self.S = Sched()
        self.base = (self.nc.sbuf_base + 63) // 64 * 64
        assert self.base + O_END <= self.nc.sbuf_top, (self.base, self.nc.sbuf_top)
        self.n = 0
        self.barrier_op = None

    def dram_in(self, name, shape, dt):
        return self.nc.dram_tensor(name, list(shape), dt, kind="ExternalInput").ap()

    def dram_out(self, name, shape, dt):
        return self.nc.dram_tensor(name, list(shape), dt, kind="ExternalOutput").ap()

    def dram(self, name, shape, dt):
        return self.nc.dram_tensor(name, list(shape), dt).ap()

    def at(self, off, shape, dt):
        self.n += 1
        return self.nc.alloc_sbuf_tensor_at(f"t{self.n}", list(shape), dt, offset=self.base + off)

    def add(self, eng, fn, reads=(), writes=(), dma=False, sig=False, ndesc=0):
        return self.S.add(eng, fn, list(reads) + ["ARENA"], writes, dma, sig, ndesc)

    def barrier(self):
        self.S.add("dve", lambda e: e.memset(self.dummy, 0.0), reads=["ARENA"], writes=["ARENA", "dummy"])

    def load(self, dst, src, key, eng="sp"):
        self.add(eng, lambda e: e.dma_start(out=dst, in_=src), writes=[key] if isinstance(key, str) else key, dma=True)

    def store(self, dst, src, rkeys, wkey=None, eng="sp"):
        self.add(eng, lambda e: e.dma_start(out=dst, in_=src), reads=[rkeys] if isinstance(rkeys, str) else rkeys,
                 writes=[wkey] if wkey else [], dma=True)

    def act(self, out, in_, func, reads, writes, **kw):
        self.add("act", lambda e: e.activation(out=out, in_=in_, func=func, **kw), reads, writes)

    def tt(self, out, in0, in1, op, reads, writes, eng="dve"):
        self.add(eng, lambda e: e.tensor_tensor(out=out, in0=in0, in1=in1, op=op), reads, writes)

    def stt(self, out, in0, scalar, in1, op0, op1, reads, writes):
        self.add("dve", lambda e: e.scalar_tensor_tensor(out=out, in0=in0, scalar=scalar, in1=in1, op0=op0, op1=op1),
                 reads, writes)

    def ts(self, out, in0, s1, op0, reads, writes, s2=None, op1=None, eng="dve"):
        if op1 is None:
            self.add(eng, lambda e: e.tensor_scalar(out=out, in0=in0, scalar1=s1, scalar2=None, op0=op0), reads, writes)
        else:
            self.add(eng, lambda e: e.tensor_scalar(out=out, in0=in0, scalar1=s1, scalar2=s2, op0=op0, op1=op1), reads,
                     writes)

    def cp(self, out, in_, reads, writes, eng="dve"):
        self.add(eng, lambda e: e.tensor_copy(out=out, in_=in_), reads, writes)

    def rc(self, out, in_, reads, writes):
        self.add("dve", lambda e: e.reciprocal(out=out, in_=in_), reads, writes)

    def mm(self, steps, reads, writes):
        steps = list(steps)

        def f(e):
            inst = None
            for (o, l, r, st, sp) in steps:
                inst = e.matmul(o, lhsT=l, rhs=r, start=st, stop=sp)
            return inst

        self.add("pe", f, reads, writes)

    def finish(self):
        self.S.emit(self.nc)
        return self.nc


_PV = {}


def getpv(e):
    k = id(e)
    if k not in _PV:
        pid = e.partition_id()
        b4 = (pid // 4) * 4
        _PV.clear()
        _PV[k] = dict(pid=pid, b4=b4)
    return _PV[k]


class WStream:
    GR = 512

    def __init__(self, K):
        self.K = K
        self.buf = K.at(O_WS, [128, WS_ELEMS], BF16)
        self.i = 0
        self.cur = None

    def fetch(self, w, kc, col0, ncols, row0=0):
        size = kc * ncols
        if self.cur != size:
            self.cur = size
            self.i = 0
        nslots = WS_ELEMS // size
        s = self.i % nslots
        self.i += 1
        off = s * size
        t = self.buf[:, off:off + size].rearrange("p (c e) -> p c e", c=kc)
        keys = [f"wsg{g}" for g in range(off // self.GR, (off + size + self.GR - 1) // self.GR)]
        src = w[row0:row0 + 128 * kc, col0:col0 + ncols].rearrange("(c p) e -> p c e", p=128)
        self.K.add("pool", lambda e: e.dma_start(out=t, in_=src), reads=[w.tensor.name], writes=keys, dma=True,
                   ndesc=8 * kc)
        return t, keys


def wstream(ws, specs):
    specs = list(specs)
    size = specs[0][1] * specs[0][3]
    depth = max(1, WS_ELEMS // size - 1)
    q = []
    issued = 0
    for i in range(len(specs)):
        while issued < min(len(specs), i + depth):
            q.append(ws.fetch(*specs[issued]))
            issued += 1
        yield q.pop(0)


def setup_common(K):
    nc = K.nc
    K._st = contextlib.ExitStack()
    K.bank = [K._st.enter_context(nc.psum_tensor(f"bank{i}", [128, 512], F32)) for i in range(8)]
    K.hT = K.at(O_HT, [128, DC, T], F32)
    K.ones = K.at(O_ONES, [128, 128], BF16)
    K.cf = K.at(O_CF, [128, 448], F32)
    K.rstd = K.at(O_RSTD, [128, 512], F32)
    K.dummy = K.cf[:, 440:441]
    K.add("dve", lambda e: e.memset(K.ones[:], 1.0), writes=["ones"])
    K.add("dve", lambda e: e.memset(K.cf[:, C_EPS:C_EPS + 1], EPS), writes=["cf_eps"])
    K.ws = WStream(K)
    K.hkeys = [f"h{c}" for c in range(DC)]


def rmsnorm_stats(K, src3, src_keys, TN, sq3, sq_keys, bank, inv_n):
    C = src3.shape[1]
    K.act(sq3, src3, AF.Square, src_keys, sq_keys)
    pb = K.bank[bank]
    K.mm([(pb[:, 0:TN], K.ones[:], sq3[:, c, :], c == 0, c == C - 1) for c in range(C)],
         list(sq_keys) + ["ones"], [f"bank{bank}"])
    rs = K.rstd[:, 0:TN]
    K.act(rs, pb[:, 0:TN], AF.Sqrt, [f"bank{bank}", "cf_eps"], ["rstd"], bias=K.cf[:, C_EPS:C_EPS + 1], scale=inv_n)
    K.rc(rs, rs, ["rstd"], ["rstd"])


def load_vec(K, vec_d, lo, n, col, key):
    K.load(K.cf[:, col:col + n], vec_d[:, lo:lo + n], key)


def ffn_phase(K, L, vec_d, wup, wdn, xedge_loc, xg_pad, pid):
    hT = K.hT
    cf = K.cf
    ws = K.ws
    B = K.bank
    K.barrier()
    load_vec(K, vec_d, 32, 16, C_GA, "gA")
    load_vec(K, vec_d, 48, 16, C_GB, "gB")
    load_vec(K, vec_d, 64, 352, C_CW, "cwb")
    xn = K.at(O_R1, [128, DC, 514], BF16)
    xmid = K.at(O_R1 + 16448, [128, DC, 2], BF16)
    xh = K.at(O_R1 + 16448 + 64, [128, 2, DC], BF16)
    xe = K.at(O_R1 + 16448 + 128, [128, 2, DC], BF16)
    act = K.at(O_R2, [128, FC, 512], BF16)
    ybuf = K.at(O_R3, [128, DC, 512], F32)
    cvb = [[K.at(O_CV + (p * 2 + q) * 2048, [128, 512], F32) for q in range(2)] for p in range(2)]
    sq = act[:, 0:DC, :]
    sqk = [f"act{j}" for j in range(DC)]
    rstd = K.rstd

    def gA(c):
        return cf[:, C_GA + c:C_GA + c + 1]

    def gB(c):
        return cf[:, C_GB + c:C_GB + c + 1]

    e4 = K.at(O_R3, [128, DC, 4], F32)
    for i, tok in enumerate((0, 511, 512, 1023)):
        K.cp(e4[:, :, i:i + 1], hT[:, :, tok:tok + 1], K.hkeys, ["e4"])
    rmsnorm_stats(K, e4[:, :, :], ["e4"], 4, act[:, 0:DC, 0:4], sqk, 0, 1.0 / D)
    for c in range(DC):
        K.stt(e4[:, c, :], e4[:, c, :], gA(c), rstd[:, 0:4], ALU.mult, ALU.mult, ["e4", "rstd", "gA"], ["e4"])
    K.cp(xmid[:, :, :], e4[:, :, 1:3], ["e4"], ["xmid"])
    K.cp(xe[:, 0, :], e4[:, :, 0], ["e4"], ["xe"])
    K.cp(xe[:, 1, :], e4[:, :, 3], ["e4"], ["xe"])
    K.store(xedge_loc.rearrange("(a p) c -> p a c", p=128), xe[:, :, :], ["xe"], "xedge")
    K.add("pool", lambda e: e.collective_compute("AllGather", ALU.bypass, replica_groups=[list(range(NCORES))],
                                                  ins=[xedge_loc], outs=[xg_pad[256:256 * 9, :]]),
          reads=["xedge"], writes=["xgath"], sig=True)
    xg3 = xg_pad.rearrange("(s a p) c -> s a p c", a=2, p=128)

    xgw = K.xg_win

    def cpwin(e):
        pid = getpv(e)['pid']
        return e.dma_start(out=xgw[:, :, :, :], in_=xg3[bass.ds(pid, 3), :, :, :])

    K.add("sp", cpwin, reads=["xgath"], writes=["xgw"], dma=True)

    def ld_l(e):
        return e.dma_start(out=xh[:, 0, :], in_=xgw[0, 1, :, :])

    def ld_r(e):
        return e.dma_start(out=xh[:, 1, :], in_=xgw[2, 0, :, :])

    K.add("sp", ld_l, reads=["xgw"], writes=["xh0"], dma=True)
    K.add("sp", ld_r, reads=["xgw"], writes=["xh1"], dma=True)
    xhf = K.at(O_R1 + 16448 + 256, [128, DC, 2], F32)
    K.ts(xhf[:, :, 0], xh[:, 0, :], cf[:, C_VL:C_VL + 1], ALU.mult, ["xh0", "pc"], ["xhf"])
    K.ts(xhf[:, :, 1], xh[:, 1, :], cf[:, C_VR:C_VR + 1], ALU.mult, ["xh1", "pc", "xhf"], ["xhf"])
    for hb in range(2):
        E0 = hb * 512
        rmsnorm_stats(K, hT[:, :, E0:E0 + 512], K.hkeys, 512, sq, sqk, 0, 1.0 / D)
        for c in range(DC):
            K.stt(xn[:, c, 1:513], hT[:, c, E0:E0 + 512], gA(c), rstd[:, :], ALU.mult, ALU.mult,
                  [f"h{c}", "rstd", "gA"], [f"xn{c}"])
        lsrc = xhf[:, :, 0:1] if hb == 0 else xmid[:, :, 0:1]
        rsrc = xmid[:, :, 1:2] if hb == 0 else xhf[:, :, 1:2]
        K.cp(xn[:, :, 0:1], lsrc, ["xhf", "xmid"], ["xnl"])
        K.cp(xn[:, :, 513:514], rsrc, ["xhf", "xmid"], ["xnr"])
        xn_keys = [f"xn{c}" for c in range(DC)] + ["xnl", "xnr"]
        pair = 0
        wgen = wstream(ws, [(wup, DC, part * DFF + j * 128, 128) for j in range(FC) for part in range(2)])
        for j in range(FC):
            for part in range(2):
                f = part * FC + j
                wt, wk = next(wgen)
                b0 = (pair % 3) * 2
                pair += 1
                pA, pB = B[b0], B[b0 + 1]
                K.mm([(pA[:, 0:258], wt[:, kc, :], xn[:, kc, 0:258], kc == 0, kc == DC - 1) for kc in range(DC)] +
                     [(pB[:, 0:258], wt[:, kc, :], xn[:, kc, 256:514], kc == 0, kc == DC - 1) for kc in range(DC)],
                     wk + xn_keys, [f"bank{b0}", f"bank{b0 + 1}"])
                cv = cvb[part][j % 2]
                ck = f"cv{part}_{j % 2}"
                for blk, pb in enumerate((pA, pB)):
                    o = cv[:, blk * 256:(blk + 1) * 256]
                    kk = ck + f"_{blk}"
                    bk = f"bank{b0 + blk}"
                    K.act(o, pb[:, 1:257], AF.Identity, [bk, "cwb"], [kk], bias=cf[:, C_CB + f:C_CB + f + 1],
                          scale=cf[:, C_CW + 3 * f + 1:C_CW + 3 * f + 2])
                    K.stt(o, pb[:, 0:256], cf[:, C_CW + 3 * f:C_CW + 3 * f + 1], o, ALU.mult, ALU.add, [bk, "cwb", kk], [kk])
                    K.stt(o, pb[:, 2:258], cf[:, C_CW + 3 * f + 2:C_CW + 3 * f + 3], o, ALU.mult, ALU.add,
                          [bk, "cwb", kk], [kk])
            cg = cvb[0][j % 2]
            cvv = cvb[1][j % 2]
            gk = [f"cv0_{j % 2}_0", f"cv0_{j % 2}_1"]
            vk = [f"cv1_{j % 2}_0", f"cv1_{j % 2}_1"]
            K.act(cg[:, :], cg[:, :], AF.Gelu_apprx_tanh, gk, gk)
            K.tt(act[:, j, :], cg[:, :], cvv[:, :], ALU.mult, gk + vk, [f"act{j}"])
        act_keys = [f"act{j}" for j in range(FC)]
        wgen = wstream(ws, [(wdn, FC, dc * 128, 128) for dc in range(DC)])
        for dc in range(DC):
            wt, wk = next(wgen)
            bi = 6 + dc % 2
            pb = B[bi]
            K.mm([(pb[:, :], wt[:, fc, :], act[:, fc, :], fc == 0, fc == FC - 1) for fc in range(FC)],
                 wk + act_keys, [f"bank{bi}"])
            K.act(ybuf[:, dc, :], pb[:, :], AF.Identity, [f"bank{bi}"], [f"y{dc}"])
        post_norm_residual(K, ybuf, E0, sq, sqk)


def post_norm_residual(K, ybuf, E0, sq, sqk):
    ykeys = [f"y{dc}" for dc in range(DC)]
    rmsnorm_stats(K, ybuf[:, :, :], ykeys, 512, sq, sqk, 0, 1.0 / D)
    for c in range(DC):
        K.tt(ybuf[:, c, :], ybuf[:, c, :], K.rstd[:, :], ALU.mult, [f"y{c}", "rstd"], [f"y{c}"], eng="pool")
        K.stt(K.hT[:, c, E0:E0 + 512], ybuf[:, c, :], K.cf[:, C_GB + c:C_GB + c + 1], K.hT[:, c, E0:E0 + 512],
              ALU.mult, ALU.add, [f"y{c}", "gB", f"h{c}"], [f"h{c}"])


def out_proj_phase(K, wout, ec_n, oT, okeys):
    K.barrier()
    ybuf = K.at(O_R3, [128, DC, 512], F32)
    sq = K.at(O_R2, [128, DC, 512], BF16)[:, :, :]
    sqk = [f"sq{j}" for j in range(DC)]
    for hb in range(2):
        E0 = hb * 512
        wgen = wstream(K.ws, [(wout, ec_n, dc * 128, 128) for dc in range(DC)])
        for dc in range(DC):
            wt, wk = next(wgen)
            bi = 6 + dc % 2
            pb = K.bank[bi]
            K.mm([(pb[:, :], wt[:, ec, :], oT[:, ec, E0:E0 + 512], ec == 0, ec == ec_n - 1) for ec in range(ec_n)],
                 wk + okeys, [f"bank{bi}"])
            K.act(ybuf[:, dc, :], pb[:, :], AF.Identity, [f"bank{bi}"], [f"y{dc}"])
        post_norm_residual(K, ybuf, E0, sq, sqk)


def pre_norm_full(K, xn):
    sq = K.at(O_R2, [128, DC, 512], BF16)[:, :, :]
    sqk = [f"sq{j}" for j in range(DC)]
    for hb in range(2):
        E0 = hb * 512
        rmsnorm_stats(K, K.hT[:, :, E0:E0 + 512], K.hkeys, 512, sq, sqk, 0, 1.0 / D)
        for c in range(DC):
            K.stt(xn[:, c, E0:E0 + 512], K.hT[:, c, E0:E0 + 512], K.cf[:, C_GA + c:C_GA + c + 1], K.rstd[:, :],
                  ALU.mult, ALU.mult, [f"h{c}", "rstd", "gA"], [f"xn{c}_{hb}"])
    return [f"xn{c}_{hb}" for c in range(DC) for hb in range(2)]


def bias_setup(K, table_d, ohA_d, ohC_d, TR2, TRC):
    K.barrier()
    tabx = K.at(O_R2, [33, 28], F32)
    ones33 = K.at(O_R2 + 128, [33, 128], F32)
    trf = [K.at(O_R2 + 1024 + i * 512, [33, 128], F32) for i in range(2)]
    trr = [K.at(O_R2 + 2048 + i * 512, [33, 128], F32) for i in range(2)]
    trh = [K.at(O_R2 + 3072 + i * 256, [33, 128], BF16) for i in range(2)]
    trl = [K.at(O_R2 + 3584 + i * 256, [33, 128], BF16) for i in range(2)]
    ohA = K.at(O_R1, [32, 5120], BF16)
    ohC = K.at(O_R1 + 10240, [33, 3 * 384], BF16)
    stg = [K.at(O_R3 + i * 2048, [128, 512], F32) for i in range(2)]
    K.add("dve", lambda e: e.memset(tabx[:, :], -30000.0), writes=["tabx"])
    K.load(tabx[0:32, :], table_d, "tabx")
    K.add("dve", lambda e: e.memset(ones33[:, :], 1.0), writes=["ones33"])
    K.load(ohA[:, :], ohA_d, "ohA")
    K.load(ohC[:, :], ohC_d, "ohC")
    n = 0
    for c in range(28):
        i2 = c % 2
        tk = f"tabrep{i2}"
        K.ts(trf[i2][:, :], ones33[:, :], tabx[:, c:c + 1], ALU.mult, ["ones33", "tabx"], [tk + "f"])
        K.cp(trh[i2][:, :], trf[i2][:, :], [tk + "f"], [tk + "h"])
        K.tt(trr[i2][:, :], trf[i2][:, :], trh[i2][:, :], ALU.subtract, [tk + "f", tk + "h"], [tk + "r"])
        K.cp(trl[i2][:, :], trr[i2][:, :], [tk + "r"], [tk + "l"])
        tks = [tk + "h", tk + "l"]
        if c < 4:
            for blk in range(10):
                bi = n % 2
                pb = K.bank[bi]
                rhs = ohA[:, blk * 512:(blk + 1) * 512]
                K.mm([(pb[:, :], trh[i2][0:32, :], rhs, True, False), (pb[:, :], trl[i2][0:32, :], rhs, False, True)],
                     tks + ["ohA"], [f"bank{bi}"])
                K.act(stg[bi][:, :], pb[:, :], AF.Identity, [f"bank{bi}"], [f"stg{bi}"])
                K.store(TR2[c * 128:(c + 1) * 128, blk * 512:(blk + 1) * 512], stg[bi][:, :], [f"stg{bi}"], f"TR2_{c}_{blk}")
                n += 1
        else:
            g = (c - 4) // 8
            bi = n % 2
            pb = K.bank[bi]
            rhs = ohC[:, g * 384:(g + 1) * 384]
            K.mm([(pb[:, 0:384], trh[i2][:, :], rhs, True, False), (pb[:, 0:384], trl[i2][:, :], rhs, False, True)],
                 tks + ["ohC"], [f"bank{bi}"])
            K.act(stg[bi][:, 0:384], pb[:, 0:384], AF.Identity, [f"bank{bi}"], [f"stg{bi}"])
            K.store(TRC[(c - 4) * 128:(c - 3) * 128, :], stg[bi][:, 0:384], [f"stg{bi}"], f"TRC_{c}")
            n += 1
    K.tr2_keys = [f"TR2_{c}_{blk}" for c in range(4) for blk in range(10)]
    K.trc_keys = [f"TRC_{c}" for c in range(4, 28)]


def ab_phase(K, L, vec_d, ab_d, lam_d, win, wout, xb_loc, xb_g, TR2, ropeC_d, ropeS_d, PT_d, pid):
    cf = K.cf
    hT = K.hT
    B = K.bank
    ws = K.ws
    K.barrier()
    load_vec(K, vec_d, 0, 16, C_GA, "gA")
    load_vec(K, vec_d, 16, 16, C_GB, "gB")
    K.load(cf[:, C_GS0:C_GS0 + 4], ab_d, "abv")
    lam = K.at(O_CV, [1, 512], F32)
    lw = K.at(O_CV + 2048, [1, 256], F32)
    ls = K.at(O_CV + 3072, [1, 8], F32)
    onef = K.at(O_CV + 3200, [1, 128], F32)
    K.load(lam[:, :], lam_d, "lam")
    K.add("dve", lambda e: e.memset(onef[:, :], 1.0), writes=["onef"])
    K.tt(lw[:, 0:128], lam[:, 0:128], lam[:, 128:256], ALU.mult, ["lam"], ["lw"])
    K.tt(lw[:, 128:256], lam[:, 256:384], lam[:, 384:512], ALU.mult, ["lam", "lw"], ["lw"])
    K.add("dve", lambda e: e.tensor_reduce(out=ls[:, 0:2], in_=lw[:, :].rearrange("p (a b) -> p a b", a=2),
                                           axis=mybir.AxisListType.X, op=ALU.add), reads=["lw"], writes=["ls"])
    K.act(ls[:, 2:4], ls[:, 0:2], AF.Exp, ["ls"], ["ls2"])
    K.tt(ls[:, 4:5], ls[:, 2:3], ls[:, 3:4], ALU.subtract, ["ls2"], ["ls3"])
    K.ts(ls[:, 5:6], ls[:, 4:5], -1.0, ALU.mult, ["ls3"], ["ls4"], s2=-LAMBDA_INIT[L], op1=ALU.add)
    K.mm([(B[0][:, 0:1], onef[0:1, :], ls[0:1, 5:6], True, True)], ["onef", "ls4"], ["bank0"])
    K.act(cf[:, C_NLAM:C_NLAM + 1], B[0][:, 0:1], AF.Identity, ["bank0"], ["nlam"])
    K.ts(cf[:, C_GS0:C_GS0 + 2], cf[:, C_GS0:C_GS0 + 2], 1.0 - LAMBDA_INIT[L], ALU.mult, ["abv"], ["abv"])
    xn = K.at(O_R1, [128, DC, T], BF16)
    xn_keys = pre_norm_full(K, xn)
    K.barrier()
    qA = K.at(O_R2, [128, 8, T], BF16)
    qB = K.at(O_R2 + 16384, [128, 8, T], BF16)
    ropeC = K.at(O_R2 + 32768, [128, T], F32)
    ropeS = K.at(O_R2 + 36864, [128, T], F32)
    PT = K.at(O_R2 + 40960, [128, 128], BF16)
    K.load(ropeC[:, :], ropeC_d, "ropeC")
    K.load(ropeS[:, :], ropeS_d, "ropeS")
    K.load(PT[:, :], PT_d, "PT")
    stgk = [K.at(O_R3 + i * 2048, [128, T], BF16) for i in range(2)]
    qraw = [K.at(O_R3 + 4096 + i * 2048, [128, 512], F32) for i in range(2)]
    qn = [K.at(O_R3 + 8192 + i * 2048, [128, 512], F32) for i in range(2)]
    sqb = [K.at(O_R3 + 12288 + i * 1024, [128, 512], BF16) for i in range(2)]
    t1 = [K.at(O_R3 + 14336 + i * 2048, [128, 512], F32) for i in range(2)]
    stgv = [K.at(O_R3 + 18432 + i * 512, [128, 256], BF16) for i in range(4)]
    qh = [K.at(O_R3 + 20480 + i * 1024, [128, 512], BF16) for i in range(2)]
    ql = [K.at(O_R3 + 22528 + i * 1024, [128, 512], BF16) for i in range(2)]
    qr = [K.at(O_R3 + 24576 + i * 2048, [128, 512], F32) for i in range(2)]
    xkeys = []
    nb = 0
    nk = 0
    cbs = list(range(0, 16)) + list(range(24, 34))
    wgen = wstream(ws, [(win, DC, cb * 128, 128) for cb in cbs])
    for cb in cbs:
        wt, wk = next(wgen)
        for hb in range(2):
            E0 = hb * 512
            bi = nb % 2
            nb += 1
            pb = B[bi]
            bk = f"bank{bi}"
            K.mm([(pb[:, :], wt[:, kc, :], xn[:, kc, E0:E0 + 512], kc == 0, kc == DC - 1) for kc in range(DC)],
                 wk + xn_keys, [bk])
            if cb < 8:
                K.act(qA[:, cb, E0:E0 + 512], pb[:, :], AF.Identity, [bk], [f"qA{cb}_{hb}"])
            elif cb < 16:
                sk = stgk[(cb) % 2]
                K.act(sk[:, E0:E0 + 512], pb[:, :], AF.Identity, [bk], [f"stgk{cb % 2}_{hb}"])
            else:
                i2 = nk % 2
                nk += 1
                gcol = C_GQ if cb < 32 else C_GK
                K.act(qraw[i2][:, :], pb[:, :], AF.Identity, [bk], [f"qraw{i2}"])
                K.act(sqb[i2][:, :], qraw[i2][:, :], AF.Square, [f"qraw{i2}"], [f"sqb{i2}"])
                K.mm([(B[2][:, :], K.ones[:], sqb[i2][:, :], True, True)], [f"sqb{i2}", "ones"], ["bank2"])
                K.act(K.rstd[:, :], B[2][:, :], AF.Sqrt, ["bank2", "cf_eps"], ["rstd"], bias=cf[:, C_EPS:C_EPS + 1],
                      scale=1.0 / 128)
                K.rc(K.rstd[:, :], K.rstd[:, :], ["rstd"], ["rstd"])
                K.stt(qn[i2][:, :], qraw[i2][:, :], cf[:, gcol:gcol + 1], K.rstd[:, :], ALU.mult, ALU.mult,
                      [f"qraw{i2}", "rstd", "abv"], [f"qn{i2}"])
                K.cp(qh[i2][:, :], qn[i2][:, :], [f"qn{i2}"], [f"qh{i2}"])
                K.tt(qr[i2][:, :], qn[i2][:, :], qh[i2][:, :], ALU.subtract, [f"qn{i2}", f"qh{i2}"], [f"qr{i2}"])
                K.cp(ql[i2][:, :], qr[i2][:, :], [f"qr{i2}"], [f"ql{i2}"])
                K.mm([(B[3][:, :], PT[:, :], qh[i2][:, :], True, False), (B[3][:, :], PT[:, :], ql[i2][:, :], False, True)],
                     ["PT", f"qh{i2}", f"ql{i2}"], ["bank3"])
                K.tt(t1[i2][:, :], qn[i2][:, :], ropeC[:, E0:E0 + 512], ALU.mult, [f"qn{i2}", "ropeC"], [f"t1{i2}"])
                K.tt(qn[i2][:, :], B[3][:, :], ropeS[:, E0:E0 + 512], ALU.mult, ["bank3", "ropeS", f"qn{i2}"], [f"qn{i2}"])
                if cb < 32:
                    K.tt(qB[:, cb - 24, E0:E0 + 512], t1[i2][:, :], qn[i2][:, :], ALU.add, [f"t1{i2}", f"qn{i2}"],
                         [f"qB{cb - 24}_{hb}"])
                else:
                    sk = stgk[cb % 2]
                    K.tt(sk[:, E0:E0 + 512], t1[i2][:, :], qn[i2][:, :], ALU.add, [f"t1{i2}", f"qn{i2}"],
                         [f"stgk{cb % 2}_{hb}"])
        if 8 <= cb < 16 or cb >= 32:
            row0 = (cb - 8) * 128 if cb < 16 else 1024 + (cb - 32) * 128
            sk = stgk[cb % 2]
            K.store(xb_loc[row0:row0 + 128, :], sk[:, :], [f"stgk{cb % 2}_0", f"stgk{cb % 2}_1"], f"xb_k{cb}")
            xkeys.append(f"xb_k{cb}")
    nv = 0
    wgen = wstream(ws, [(win, DC, 2048 + cg * 256 if cg < 4 else 4352, 256) for cg in range(5)])
    for cg in range(5):
        wt, wk = next(wgen)
        for tt in range(8):
            bi = 4 + nv % 2
            pb = B[bi]
            K.mm([(pb[:, 0:256], xn[:, kc, tt * 128:(tt + 1) * 128], wt[:, kc, :], kc == 0, kc == DC - 1)
                  for kc in range(DC)], wk + xn_keys, [f"bank{bi}"])
            sv = stgv[nv % 4]
            K.act(sv[:, :], pb[:, 0:256], AF.Identity, [f"bank{bi}"], [f"stgv{nv % 4}"])
            if cg < 4:
                dst = xb_loc[1280 + tt * 128:1280 + (tt + 1) * 128, cg * 256:(cg + 1) * 256]
            else:
                dst = xb_loc[2304:2560, :].rearrange("a (b f) -> (a b) f", f=256)[tt * 128:(tt + 1) * 128, :]
            K.store(dst, sv[:, :], [f"stgv{nv % 4}"], f"xb_v{cg}_{tt}")
            xkeys.append(f"xb_v{cg}_{tt}")
            nv += 1
    K.add("pool", lambda e: e.collective_compute("AllGather", ALU.bypass, replica_groups=[list(range(NCORES))],
                                                  ins=[xb_loc], outs=[xb_g]), reads=xkeys, writes=["xbg"], sig=True)
    K.barrier()
    oT = K.at(O_R1, [128, DC, T], BF16)
    kv0 = O_R2 + 32768
    Kt = K.at(kv0, [128, 2, SEQ], BF16)
    Vt = K.at(kv0 + 16384, [128, 32, 256], BF16)
    G = K.at(O_WS, [128, 5120], F32)
    eo = O_R2 + 65536
    Et = [K.at(O_CV + i * 1024, [128, 512], BF16) for i in range(4)]
    tmp = [K.at(O_CV + 4096 + i * 2048, [128, 512], F32) for i in range(2)]
    rr = [K.at(eo + i * 2048, [128, 512], F32) for i in range(2)]
    od = K.at(eo + 4096, [128, 2, 512], F32)
    tq = [K.at(eo + 8192 + i * 2048, [128, 512], F32) for i in range(2)]
    sqo = K.at(O_WS + 20480, [128, 2, 512], BF16)
    xg3d = xb_g.rearrange("(r n) f -> r n f", r=NCORES)
    xg3 = K.xb_win

    def cpwin(e):
        b4 = getpv(e)['b4']
        return e.dma_start(out=xg3[:, :, :], in_=xg3d[bass.ds(b4, 4), :, :])

    K.add("sp", cpwin, reads=["xbg"], writes=["xbw"], dma=True)
    okeys = []
    ne = 0
    for h in range(4):
        for m in range(2):
            def ldk(e, h=h, m=m):
                return e.dma_start(out=Kt[:, m, :].rearrange("p (r f) -> p r f", r=4),
                                   in_=xg3[:, (h * 2 + m) * 128:(h * 2 + m + 1) * 128, :].rearrange(
                                       "r p f -> p r f"))

            K.add("sp", ldk, reads=["xbw"], writes=["Kt"], dma=True)
        for r in range(4):
            def ldv(e, h=h, r=r):
                return e.dma_start(out=Vt[:, r * 8:(r + 1) * 8, :],
                                   in_=xg3[:, 1280:2304, h * 256:(h + 1) * 256][r].rearrange(
                                       "(t p) f -> p t f", p=128))

            K.add("sp", ldv, reads=["xbw"], writes=["Vt"], dma=True)
        gsrc = bass.AP(tensor=TR2.tensor, offset=h * 128 * 5120 + 127, ap=[[5119, 128], [1, 4993]])
        K.add("sp", lambda e, gsrc=gsrc: e.dma_start(out=G[:, 0:4993], in_=gsrc), reads=K.tr2_keys, writes=["G"], dma=True)
        for qb in range(2):
            Q0 = qb * 512
            pend = None
            for kt in range(33):
                cur = []
                if kt < 32:
                    m0 = 3969 - kt * 128 + qb * 512
                    for m in range(2):
                        sb_ = B[m]
                        K.mm([(sb_[:, :], Kt[:, m, kt * 128:(kt + 1) * 128], qA[:, h * 2 + m, Q0:Q0 + 512], True, True)],
                             ["Kt", f"qA{h * 2 + m}_{qb}"], [f"bank{m}"])
                        tm = tmp[m]
                        K.stt(tm[:, :], sb_[:, :], SCALE, G[:, m0:m0 + 512], ALU.mult, ALU.add, [f"bank{m}", "G"], [f"tmp{m}"])
                        et = Et[ne % 4]
                        ek = f"Et{ne % 4}"
                        ne += 1
                        K.act(et[:, :], tm[:, :], AF.Exp, [f"tmp{m}"], [ek])
                        cur.append((m, kt, et, ek))
                if pend is not None:
                    for (m, pk, et, ek) in pend:
                        st, sp = pk == 0, pk == 31
                        K.mm([(B[2 + 3 * m][:, :], Vt[:, pk, 0:128], et[:, :], st, sp),
                              (B[3 + 3 * m][:, :], Vt[:, pk, 128:256], et[:, :], st, sp),
                              (B[4 + 3 * m][:, :], K.ones[:], et[:, :], st, sp)],
                             ["Vt", ek, "ones"], [f"bank{2 + 3 * m}", f"bank{3 + 3 * m}", f"bank{4 + 3 * m}"])
                pend = cur if kt < 32 else None
            for m in range(2):
                K.rc(rr[m][:, :], B[4 + 3 * m][:, :], [f"bank{4 + 3 * m}"], [f"rr{m}"])
            for dv in range(2):
                K.tt(tq[0][:, :], B[2 + dv][:, :], rr[0][:, :], ALU.mult, [f"bank{2 + dv}", "rr0"], ["tq0"])
                K.tt(tq[1][:, :], B[5 + dv][:, :], rr[1][:, :], ALU.mult, [f"bank{5 + dv}", "rr1"], ["tq1"])
                K.stt(od[:, dv, :], tq[1][:, :], cf[:, C_NLAM:C_NLAM + 1], tq[0][:, :], ALU.mult, ALU.add,
                      ["tq0", "tq1", "nlam"], [f"od{dv}"])
            rmsnorm_stats(K, od[:, :, :], ["od0", "od1"], 512, sqo[:, :, :], ["sqo"], 0, 1.0 / 256)
            for dv in range(2):
                K.stt(oT[:, h * 2 + dv, Q0:Q0 + 512], od[:, dv, :], cf[:, C_GS0 + dv:C_GS0 + dv + 1], K.rstd[:, :],
                      ALU.mult, ALU.mult, [f"od{dv}", "rstd", "abv"], [f"oT{h * 2 + dv}_{qb}"])
                okeys.append(f"oT{h * 2 + dv}_{qb}")
    for kv in range(2):
        def ldk(e, kv=kv):
            return e.dma_start(out=Kt[:, 0, :].rearrange("p (r f) -> p r f", r=4),
                               in_=xg3[:, 1024 + kv * 128:1024 + (kv + 1) * 128, :].rearrange(
                                   "r p f -> p r f"))

        K.add("sp", ldk, reads=["xbw"], writes=["Kt"], dma=True)

        for r in range(4):
            def ldv(e, kv=kv, r=r):
                src = xg3[:, 2304:2560, :][r].rearrange("a (b f) -> (a b) f", f=256)
                return e.dma_start(out=Vt[:, r * 8:(r + 1) * 8, 0:128],
                                   in_=src[:, kv * 128:(kv + 1) * 128].rearrange("(t p) f -> p t f", p=128))

            K.add("sp", ldv, reads=["xbw"], writes=["Vt"], dma=True)
        for g in range(4):
            hq = kv * 4 + g
            for qb in range(2):
                Q0 = qb * 512
                ob, db = (4, 5) if (g * 2 + qb) % 2 == 0 else (6, 7)
                pend = None
                for kt in range(33):
                    cur = None
                    if kt < 32:
                        sbk = kt % 2
                        sb_ = B[sbk]
                        K.mm([(sb_[:, :], Kt[:, 0, kt * 128:(kt + 1) * 128], qB[:, hq, Q0:Q0 + 512], True, True)],
                             ["Kt", f"qB{hq}_{qb}"], [f"bank{sbk}"])
                        et = Et[ne % 4]
                        ek = f"Et{ne % 4}"
                        ne += 1
                        K.act(et[:, :], sb_[:, :], AF.Exp, [f"bank{sbk}"], [ek], scale=SCALE)
                        cur = (kt, et, ek)
                    if pend is not None:
                        pk, et, ek = pend
                        st, sp = pk == 0, pk == 31
                        K.mm([(B[ob][:, :], Vt[:, pk, 0:128], et[:, :], st, sp), (B[db][:, :], K.ones[:], et[:, :], st, sp)],
                             ["Vt", ek, "ones"], [f"bank{ob}", f"bank{db}"])
                    pend = cur
                ri = (g * 2 + qb) % 2
                K.rc(rr[ri][:, :], B[db][:, :], [f"bank{db}"], [f"rr{ri}"])
                K.tt(oT[:, 8 + hq, Q0:Q0 + 512], B[ob][:, :], rr[ri][:, :], ALU.mult, [f"bank{ob}", f"rr{ri}"],
                     [f"oT{8 + hq}_{qb}"])
                okeys.append(f"oT{8 + hq}_{qb}")
    out_proj_phase(K, wout, 16, oT, okeys)


def c_phase(K, L, vec_d, win, wout, xc_loc, xc_pad, TRC):
    cf = K.cf
    B = K.bank
    ws = K.ws
    K.barrier()
    load_vec(K, vec_d, 0, 16, C_GA, "gA")
    load_vec(K, vec_d, 16, 16, C_GB, "gB")
    xn = K.at(O_R1, [128, DC, T], BF16)
    xn_keys = pre_norm_full(K, xn)
    K.barrier()
    qC = K.at(O_R2, [128, 24, T], BF16)
    so = O_R2 + 49152
    stgk = [K.at(so + i * 2048, [128, T], BF16) for i in range(2)]
    stgv = [K.at(so + 4096 + i * 512, [128, 256], BF16) for i in range(4)]
    xkeys = []
    nb = 0
    for g in range(3):
        d = DILS[g]
        wgen = wstream(ws, [(win, DC, ((g * 3 + s_) * 8 + h) * 128, 128) for s_ in range(2) for h in range(8)])
        for s_ in range(2):
            for h in range(8):
                wt, wk = next(wgen)
                for hb in range(2):
                    bi = nb % 2
                    nb += 1
                    pb = B[bi]
                    K.mm([(pb[:, :], wt[:, kc, :], xn[:, kc, hb * 512:(hb + 1) * 512], kc == 0, kc == DC - 1)
                          for kc in range(DC)], wk + xn_keys, [f"bank{bi}"])
                    src = pb[:, :].rearrange("p (l r) -> p r l", r=d)
                    if s_ == 0:
                        dst = qC[:, g * 8 + h, :]
                        wkey = f"qC{g * 8 + h}_{hb}"
                    else:
                        dst = stgk[h % 2][:, :]
                        wkey = f"stgk{h % 2}_{hb}"
                    dst = dst.rearrange("p (r l) -> p r l", r=d)[:, :, hb * (512 // d):(hb + 1) * (512 // d)]
                    K.act(dst, src, AF.Identity, [f"bank{bi}"], [wkey])
                if s_ == 1:
                    row0 = (g * 8 + h) * 128
                    K.store(xc_loc[row0:row0 + 128, :], stgk[h % 2][:, :], [f"stgk{h % 2}_0", f"stgk{h % 2}_1"],
                            f"xc_k{g}_{h}")
                    xkeys.append(f"xc_k{g}_{h}")
        nv = 0
        wgen = wstream(ws, [(win, DC, ((g * 3 + 2) * 8) * 128 + cg * 256, 256) for cg in range(4)])
        for cg in range(4):
            wt, wk = next(wgen)
            for tt in range(8):
                bi = 4 + nv % 2
                pb = B[bi]
                steps = []
                for kc in range(DC):
                    xv = xn[:, kc, :].rearrange("p (l r) -> p r l", r=d)
                    if d == 1:
                        lt = xn[:, kc, tt * 128:(tt + 1) * 128]
                    elif d == 4:
                        lt = xv[:, tt // 2, (tt % 2) * 128:(tt % 2) * 128 + 128]
                    else:
                        lt = xn[:, kc, :].rearrange("p (m e) -> p e m", e=8)[:, tt, :]
                    steps.append((pb[:, 0:256], lt, wt[:, kc, :], kc == 0, kc == DC - 1))
                K.mm(steps, wk + xn_keys, [f"bank{bi}"])
                sv = stgv[nv % 4]
                K.act(sv[:, :], pb[:, 0:256], AF.Identity, [f"bank{bi}"], [f"stgv{nv % 4}"])
                r0 = 3072 + g * 1024 + tt * 128
                K.store(xc_loc[r0:r0 + 128, cg * 256:(cg + 1) * 256], sv[:, :], [f"stgv{nv % 4}"], f"xc_v{g}_{cg}_{tt}")
                xkeys.append(f"xc_v{g}_{cg}_{tt}")
                nv += 1
    K.add("pool", lambda e: e.collective_compute("AllGather", ALU.bypass, replica_groups=[list(range(NCORES))],
                                                  ins=[xc_loc], outs=[xc_pad[6144:6144 * 9, :]]),
          reads=xkeys, writes=["xcg"], sig=True)
    K.barrier()
    oT = K.at(O_R1, [128, 8, T], BF16)
    accN = K.at(O_R1 + 16384, [128, T], F32)
    accD = K.at(O_R1 + 20480, [128, T], F32)
    BAB = [[K.at(O_R1 + 24576 + (i * 3 + g) * 1024, [128, 256], F32) for g in range(3)] for i in range(2)]
    Kw = [K.at(so + i * 6144, [128, 3 * T], BF16) for i in range(2)]
    Vw = [K.at(so + 12288 + i * 8192, [128, 32, 128], BF16) for i in range(2)]
    Et = [K.at(O_CV + i * 1024, [128, 512], BF16) for i in range(4)]
    tmp = [K.at(O_CV + 4096 + i * 2048, [128, 512], F32) for i in range(2)]
    xc3d = xc_pad.rearrange("(s n) f -> s n f", s=10)
    xc3 = K.xc_win

    def cpwin(e):
        pid = getpv(e)['pid']
        return e.dma_start(out=xc3[:, :, :], in_=xc3d[bass.ds(pid, 3), :, :])

    K.add("sp", cpwin, reads=["xcg"], writes=["xcw"], dma=True)
    okeys = []
    ne = 0
    nkv = 0
    ns = 0
    for h in range(8):
        hb_ = h % 2
        for g in range(3):
            gh = g * 8 + h
            for ab, off in ((0, 255), (1, 127)):
                src = bass.AP(tensor=TRC.tensor, offset=gh * 128 * 384 + off, ap=[[383, 128], [1, 128]])
                K.add("sp", lambda e, src=src, dst=BAB[hb_][g][:, ab * 128:(ab + 1) * 128]: e.dma_start(out=dst, in_=src),
                      reads=K.trc_keys, writes=[f"BAB{hb_}_{g}"], dma=True)
        for g in range(3):
            d = DILS[g]
            ns_ = T // d
            kb = nkv % 2
            nkv += 1
            kwk, vwk = f"Kw{kb}", f"Vw{kb}"
            kw = Kw[kb]
            vw = Vw[kb]
            r0k = (g * 8 + h) * 128

            for sl in range(3):
                def ldk(e, kw=kw, r0k=r0k, d=d, sl=sl):
                    return e.dma_start(out=kw[:, :].rearrange("p (r s l) -> p r s l", r=d, s=3)[:, :, sl, :],
                                       in_=xc3[:, r0k:r0k + 128, :][sl].rearrange("p (r l) -> p r l", r=d))

                K.add("sp", ldk, reads=["xcw"], writes=[kwk], dma=True)
            tpr = {1: 9, 4: 3, 16: 2}[d]
            vw4 = vw[:, 0:d * tpr, :].rearrange("p (r t) f -> p r t f", r=d)
            rv0 = 3072 + g * 1024
            c0 = h * 128
            if d == 1:
                segs = [(0, 960, 1024, 0, 0), (1, 0, 64, 0, 64), (1, 64, 960, 1, 0), (1, 960, 1024, 8, 0), (2, 0, 64, 8, 64)]
            elif d == 4:
                segs = [(0, 192, 256, 0, 0), (1, 0, 64, 0, 64), (1, 64, 192, 1, 0), (1, 192, 256, 2, 0), (2, 0, 64, 2, 64)]
            else:
                segs = [(0, 0, 64, 0, 0), (1, 0, 64, 0, 64), (2, 0, 64, 1, 0)]
            for (sl, la, lb, ti, p0) in segs:
                nrow = lb - la

                if d == 16:
                    for b_ in range(2):
                        def ldv(e, sl=sl, ti=ti, p0=p0, vw4=vw4, rv0=rv0, c0=c0, b_=b_):
                            src = xc3[:, rv0:rv0 + T, c0:c0 + 128][sl].rearrange("(t l b) f -> b l t f", t=8, b=2)[b_]
                            return e.dma_start(out=vw4[p0:p0 + 64, b_ * 8:(b_ + 1) * 8, ti, :], in_=src)

                        K.add("sp", ldv, reads=["xcw"], writes=[vwk], dma=True)
                    continue

                def ldv(e, sl=sl, la=la, nrow=nrow, ti=ti, p0=p0, vw4=vw4, d=d, ns_=ns_, rv0=rv0, c0=c0):
                    src = xc3[:, rv0:rv0 + T, c0:c0 + 128][sl].rearrange("(r l) f -> r l f", r=d)
                    if nrow <= 128:
                        return e.dma_start(out=vw4[p0:p0 + nrow, :, ti, :], in_=src[:, la:la + nrow, :].rearrange("r l f -> l r f"))
                    nt = nrow // 128
                    return e.dma_start(out=vw4[:, :, ti:ti + nt, :],
                                       in_=src[:, la:la + nrow, :].rearrange("r (t p) f -> p r t f", p=128))

                K.add("sp", ldv, reads=["xcw"], writes=[vwk], dma=True)
            kw3 = kw[:, :].rearrange("p (r w) -> p r w", r=d)
            QN = min(128, ns_)
            nqt = max(1, ns_ // 128)
            for r in range(d):
                for qt in range(nqt):
                    q0 = r * ns_ + qt * 128
                    qap = qC[:, g * 8 + h, q0:q0 + QN]
                    if d == 16:
                        wA, KB, tA, tB = 0, 64, 0, 1
                    else:
                        wA, KB, tA, tB = ns_ + qt * 128 - 64, 128, qt, qt + 1
                    sbi = ns % 2
                    ns += 1
                    sb_ = B[sbi]
                    K.mm([(sb_[:, 0:QN], kw3[:, r, wA:wA + 128], qap, True, True),
                          (sb_[0:KB, 128:128 + QN], kw3[:, r, wA + 128:wA + 128 + KB], qap, True, True)],
                         [kwk, f"qC{g * 8 + h}_0", f"qC{g * 8 + h}_1"], [f"bank{sbi}"])
                    tm = tmp[sbi]
                    bab = BAB[hb_][g]
                    K.stt(tm[:, 0:QN], sb_[:, 0:QN], SCALE, bab[:, 0:QN], ALU.mult, ALU.add, [f"bank{sbi}", f"BAB{hb_}_{g}"],
                          [f"tmp{sbi}"])
                    K.stt(tm[0:KB, 128:128 + QN], sb_[0:KB, 128:128 + QN], SCALE, bab[0:KB, 128:128 + QN], ALU.mult, ALU.add,
                          [f"bank{sbi}", f"BAB{hb_}_{g}", f"tmp{sbi}"], [f"tmp{sbi}"])
                    et = Et[ne % 4]
                    ek = f"Et{ne % 4}"
                    ne += 1
                    mA = C_ML if qt == 0 else C_M0
                    mB = C_MRLO if d == 16 else (C_MR if qt == nqt - 1 else C_M0)
                    K.act(et[:, 0:QN], tm[:, 0:QN], AF.Exp, [f"tmp{sbi}", "pc"], [ek], bias=cf[:, mA:mA + 1])
                    K.act(et[0:KB, 128:128 + QN], tm[0:KB, 128:128 + QN], AF.Exp, [f"tmp{sbi}", "pc", ek], [ek],
                          bias=cf[0:KB, mB:mB + 1])
                    if d == 1:
                        nbk, cc0 = 2 + qt // 4, (qt % 4) * 128
                    else:
                        nbk, cc0 = 2, qt * 128
                    K.mm([(B[nbk][:, cc0:cc0 + QN], vw4[:, r, tA, :], et[:, 0:QN], True, False),
                          (B[nbk][:, cc0:cc0 + QN], vw4[0:KB, r, tB, :], et[0:KB, 128:128 + QN], False, True),
                          (B[nbk + 2][:, cc0:cc0 + QN], K.ones[:, :], et[:, 0:QN], True, False),
                          (B[nbk + 2][:, cc0:cc0 + QN], K.ones[0:KB, :], et[0:KB, 128:128 + QN], False, True)],
                         [vwk, ek, "ones"], [f"bank{nbk}", f"bank{nbk + 2}"])
                aN = accN[:, :].rearrange("p (l r) -> p r l", r=d)[:, r, :]
                aD = accD[:, :].rearrange("p (l r) -> p r l", r=d)[:, r, :]
                pieces = [(2, 0, 512, 0), (3, 0, 512, 512)] if d == 1 else [(2, 0, ns_, 0)]
                for (bk_, ca, n_, oa) in pieces:
                    for (acc_, bo, key) in ((aN, 0, "accN"), (aD, 2, "accD")):
                        if g == 0:
                            K.cp(acc_[:, oa:oa + n_], B[bk_ + bo][:, ca:ca + n_], [f"bank{bk_ + bo}"], [key])
                        else:
                            K.tt(acc_[:, oa:oa + n_], B[bk_ + bo][:, ca:ca + n_], acc_[:, oa:oa + n_], ALU.add,
                                 [f"bank{bk_ + bo}", key], [key])
        K.rc(accD[:, :], accD[:, :], ["accD"], ["accD"])
        K.tt(oT[:, h, :], accN[:, :], accD[:, :], ALU.mult, ["accN", "accD"], [f"oT{h}"])
        okeys.append(f"oT{h}")
    out_proj_phase(K, wout, 8, oT, okeys)


W_SHAPES = {}


def weight_specs(layers):
    specs = []
    for L in layers:
        if L % 2 == 0:
            specs += [(f"l{L}_w_in", D, 4608), (f"l{L}_w_out", 2048, D)]
        else:
            specs += [(f"l{L}_w_in", D, 9216), (f"l{L}_w_out", 1024, D)]
        specs += [(f"l{L}_w_up", D, 2 * DFF), (f"l{L}_w_down", DFF, D)]
    return specs


def build_mega(layers=(0, 1, 2, 3), do_ffn=True, do_mix=True):
    K = Ctx()
    nc = K.nc
    setup_common(K)
    xT = K.dram_in("xT", [128, DC, T], F32)
    outT = K.dram_out("outT", [128, DC, T], F32)
    table = K.dram_in("table", [32, 28], F32)
    ohA = K.dram_in("ohA", [32, 5120], BF16)
    ohC = K.dram_in("ohC", [33, 3 * 384], BF16)
    ropeC = K.dram_in("ropeC", [128, T], F32)
    ropeS = K.dram_in("ropeS", [128, T], F32)
    PT = K.dram_in("PT", [128, 128], BF16)
    pc = K.dram_in("pc", [128, 6], F32)
    vec = {L: K.dram_in(f"vec{L}", [128, 416], F32) for L in layers}
    abv = {L: K.dram_in(f"ab{L}", [128, 4], F32) for L in layers if L % 2 == 0}
    lam = {L: K.dram_in(f"lam{L}", [1, 512], F32) for L in layers if L % 2 == 0}
    wsh, wfull, wloc = {}, {}, {}
    for (name, rows, cols) in weight_specs(layers):
        wsh[name] = K.dram_in(name, [rows // NCORES, cols], F32)
        wloc[name] = K.dram(name + "_loc", [rows // NCORES, cols], F32)
        wfull[name] = K.dram(name + "_full", [rows, cols], F32)
    TR2 = K.dram("TR2", [4 * 128, 5120], F32)
    TRC = K.dram("TRC", [24 * 128, 384], F32)
    xb_loc = K.dram("xb_loc", [2560, 1024], BF16)
    xb_g = K.dram("xb_g", [NCORES * 2560, 1024], BF16)
    xc_loc = K.dram("xc_loc", [6144, 1024], BF16)
    xc_pad = K.dram("xc_pad", [10 * 6144, 1024], BF16)
    xe_loc = K.dram("xe_loc", [256, 16], BF16)
    xg_pad = K.dram("xg_pad", [10 * 256, 16], BF16)
    K.xb_win = K.dram("xb_win", [4, 2560, 1024], BF16)
    K.xc_win = K.dram("xc_win", [3, 6144, 1024], BF16)
    K.xg_win = K.dram("xg_win", [3, 2, 128, 16], BF16)
    for c in range(DC):
        K.load(K.hT[:, c, :], xT[:, c, :], f"h{c}")
    K.load(K.cf[:, C_VL:C_VL + 6], pc, "pc")
    zt = K.at(O_R2, [128, 8, 1024], BF16)
    K.add("pool", lambda e: e.memset(zt[:, :, :], 0.0), writes=["zt"])
    for slot in (0, 9):
        for a in range(6):
            r0 = slot * 6144 + a * 1024
            K.store(xc_pad[r0:r0 + 1024, :].rearrange("(a p) f -> p a f", p=128), zt[:, :, :], ["zt"], f"zpad{slot}_{a}")
        K.store(xg_pad[slot * 256:(slot + 1) * 256, :].rearrange("(a p) f -> p a f", p=128), zt[:, 0:2, 0:16], ["zt"],
                f"zpadg{slot}")
    order = [n for (n, _, _) in weight_specs(layers)]

    def gather_weight(name):
        K.add("sp", lambda e: e.dma_start(out=wloc[name], in_=wsh[name]), writes=[name + "_loc"], dma=True)
        K.add("pool", lambda e: e.collective_compute("AllGather", ALU.bypass, replica_groups=[list(range(NCORES))],
                                                      ins=[wloc[name]], outs=[wfull[name]]),
              reads=[name + "_loc"], writes=[name + "_full"], sig=True)

    pending = list(order)

    def gather_next(n):
        for _ in range(n):
            if pending:
                gather_weight(pending.pop(0))

    gather_next(2)
    bias_setup(K, table, ohA, ohC, TR2, TRC)
    for L in layers:
        gather_next(2)
        if not do_mix:
            pass
        elif L % 2 == 0:
            ab_phase(K, L, vec[L], abv[L], lam[L], wfull[f"l{L}_w_in"], wfull[f"l{L}_w_out"], xb_loc, xb_g, TR2, ropeC,
                     ropeS, PT, None)
        else:
            c_phase(K, L, vec[L], wfull[f"l{L}_w_in"], wfull[f"l{L}_w_out"], xc_loc, xc_pad, TRC)
        gather_next(2)
        if do_ffn:
            ffn_phase(K, L, vec[L], wfull[f"l{L}_w_up"], wfull[f"l{L}_w_down"], xe_loc, xg_pad, None)
    K.barrier()
    for c in range(DC):
        K.store(outT[:, c, :], K.hT[:, c, :], [f"h{c}"])
    return K.finish()


def _rel_bucket(rel):
    rel = np.asarray(rel, np.int64)
    n = np.abs(rel)
    nf = np.maximum(n, 1).astype(np.float32)
    large = 8 + (np.log(nf / np.float32(8)) / np.float32(np.log(1024 / 8)) * np.float32(8)).astype(np.int32)
    large = np.minimum(large, 15)
    return np.where(rel > 0, 16, 0) + np.where(n < 8, n, large)


def _vec_fm(v, n):
    return np.ascontiguousarray(np.asarray(v, np.float32).reshape(n, 128).T)


def host_constants(core):
    q = core % 4
    c = {}
    n = np.arange(5120)
    i = 5119 - n
    rel = i - 1023 - q * 1024
    oh = np.zeros((32, 5120), np.float32)
    oh[_rel_bucket(rel), n] = 1.0
    c["ohA"] = oh.astype(BF)
    ohc = np.zeros((33, 3 * 384), np.float32)
    for g, d in enumerate(DILS):
        nn = np.arange(384)
        rs = 191 - nn
        ok = np.abs(rs) <= 64
        b = _rel_bucket(rs * d)
        ohc[b[ok], g * 384 + nn[ok]] = 1.0
        ohc[32, g * 384 + nn[~ok]] = 1.0
    c["ohC"] = ohc.astype(BF)
    t = q * 1024 + np.arange(1024)
    row = (t // 64).astype(np.float32)
    col = (t % 64).astype(np.float32)
    inv = (np.float32(10000.0) ** (-np.arange(32, dtype=np.float32) * np.float32(2.0) / np.float32(64))).astype(np.float32)
    C_ = np.zeros((128, 1024), np.float32)
    S_ = np.zeros((128, 1024), np.float32)
    for dim in range(128):
        pos = row if dim < 64 else col
        ang = pos * inv[dim % 32]
        C_[dim] = np.cos(ang)
        S_[dim] = np.sin(ang)
    c["ropeC"] = C_
    c["ropeS"] = S_
    P = np.zeros((128, 128), np.float32)
    for i_ in range(128):
        if (i_ % 64) < 32:
            P[i_ + 32, i_] = -1.0
        else:
            P[i_ - 32, i_] = 1.0
    c["PT"] = P.astype(BF)
    vl = 1.0 if q > 0 else 0.0
    vr = 1.0 if q < 3 else 0.0
    pcv = np.zeros((128, 6), np.float32)
    pcv[:, 0] = vl
    pcv[:, 1] = vr
    p = np.arange(128)
    pcv[:, 3] = np.where(p < 64, 0.0 if vl else -30000.0, 0.0)
    pcv[:, 4] = np.where(p >= 64, 0.0 if vr else -30000.0, 0.0)
    pcv[:, 5] = np.where(p < 64, 0.0 if vr else -30000.0, 0.0)
    c["pc"] = pcv
    return c


def make_inputs(inputs, layers=(0, 1, 2, 3)):
    x = np.asarray(inputs["x"], np.float32)
    maps = []
    for core in range(NCORES):
        b, q = divmod(core, 4)
        m = host_constants(core)
        blk = x[b, q * 1024:(q + 1) * 1024, :]
        m["xT"] = np.ascontiguousarray(blk.T.reshape(DC, 128, T).transpose(1, 0, 2))
        m["table"] = np.asarray(inputs["rel_bias_table"], np.float32)
        for L in layers:
            cwv = np.asarray(inputs[f"l{L}_conv_w"], np.float32)
            nf = cwv.shape[1] // 128
            cw = cwv.T.reshape(nf, 128, 3).transpose(1, 0, 2).reshape(128, nf * 3)
            vecs = [_vec_fm(inputs[f"l{L}_mix_pre_norm"], 16), _vec_fm(inputs[f"l{L}_mix_post_norm"], 16),
                    _vec_fm(inputs[f"l{L}_ffn_pre_norm"], 16), _vec_fm(inputs[f"l{L}_ffn_post_norm"], 16),
                    cw, _vec_fm(inputs[f"l{L}_conv_b"], nf)]
            v = np.concatenate(vecs, axis=1)
            vv = np.zeros((128, 416), np.float32)
            vv[:, :64] = v[:, :64]
            vv[:, 64:64 + nf * 3] = v[:, 64:64 + nf * 3]
            vv[:, 328:328 + nf] = v[:, 64 + nf * 3:]
            m[f"vec{L}"] = vv
            if L % 2 == 0:
                ab = np.zeros((128, 4), np.float32)
                ab[:, 0:2] = _vec_fm(inputs[f"l{L}_diff_subln"], 2)
                qk = np.asarray(inputs[f"l{L}_qk_norm"], np.float32)
                ab[:, 2] = qk[0]
                ab[:, 3] = qk[1]
                m[f"ab{L}"] = ab
                m[f"lam{L}"] = np.asarray(inputs[f"l{L}_diff_lambda"], np.float32).reshape(1, 512)
            for nm in ("w_in", "w_out", "w_up", "w_down"):
                w = np.asarray(inputs[f"l{L}_{nm}"], np.float32)
                rs = w.shape[0] // NCORES
                m[f"l{L}_{nm}"] = w[core * rs:(core + 1) * rs]
        maps.append(m)
    return maps


def kernel(**inputs):
    nc = build_mega()
    maps = make_inputs(inputs)
    res = run_bass_kernel_spmd(nc, maps, core_ids=list(range(NCORES)))
    out = np.zeros((2, SEQ, D), np.float32)
    for core in range(NCORES):
        b, q = divmod(core, 4)
        o = res.results[core]["outT"]
        out[b, q * 1024:(q + 1) * 1024, :] = o.transpose(1, 0, 2).reshape(D, T).T
    return out
```

```python
import contextlib
import numpy as np
import ml_dtypes
import concourse.bass as bass
import concourse.mybir as mybir
from concourse.bass_utils import run_bass_kernel_spmd

F32 = mybir.dt.float32
BF16 = mybir.dt.bfloat16
I32 = mybir.dt.int32
AF = mybir.ActivationFunctionType
ALU = mybir.AluOpType
BF = ml_dtypes.bfloat16

NCORES = 8
D = 2048
DC = 16
T = 1024
SEQ = 4096
DFF = 5632
FC = 44
EPS = 1e-6
NDMA_SEM = 12


class Op:
    __slots__ = ("eng", "fn", "deps", "needed", "count", "dma", "dsem", "dval", "prev_dma", "where")

    def __init__(self, eng, fn, dma):
        self.eng = eng
        self.fn = fn
        self.deps = []
        self.needed = False
        self.count = 0
        self.dma = dma
        self.dsem = None
        self.dval = 0
        self.prev_dma = None


class Sched:
    ENGS = ("pe", "act", "dve", "pool", "sp")

    def __init__(self):
        self.ops = []
        self.lastw = {}
        self.readers = {}
        self.dma_hist = {e: [] for e in self.ENGS}
        self.swq = []

    def add(self, eng, fn, reads=(), writes=(), dma=False, sig=False, ndesc=0):
        op = Op(eng, fn, dma)
        op.needed = sig
        import sys as _s
        f_ = _s._getframe(1)
        op.where = []
        while f_ is not None and len(op.where) < 5:
            op.where.append(f_.f_lineno)
            f_ = f_.f_back
        deps = {}
        for r in reads:
            w = self.lastw.get(r)
            if w is not None:
                deps[id(w)] = w
        for w_ in writes:
            lw = self.lastw.get(w_)
            if lw is not None:
                deps[id(lw)] = lw
            rd = self.readers.get(w_)
            if rd:
                for o in rd[0].values():
                    deps[id(o)] = o
                for o in rd[1]:
                    deps[id(o)] = o
        for r in reads:
            rd = self.readers.get(r)
            if rd is None:
                rd = self.readers[r] = ({}, [])
            if dma:
                rd[1].append(op)
            else:
                rd[0][eng] = op
        for w_ in writes:
            self.lastw[w_] = op
            self.readers[w_] = ({}, [])
        for d in deps.values():
            if d is op:
                continue
            if d.eng == "pe" and eng == "pe" and not d.dma and not dma:
                continue
            d.needed = True
            op.deps.append(d)
        if dma and ndesc:
            out = self.swq
            while out and sum(n for _, n in out) + ndesc > 700:
                o, _ = out.pop(0)
                op.deps.append(o)
            out.append((op, ndesc))
        if dma:
            hist = self.dma_hist[eng]
            j = len(hist)
            op.dsem = j % NDMA_SEM
            op.dval = 16 * (j // NDMA_SEM + 1)
            if j >= NDMA_SEM:
                op.prev_dma = hist[j - NDMA_SEM]
            hist.append(op)
        self.ops.append(op)
        return op

    def emit(self, nc):
        cnt = {e: 0 for e in self.ENGS}
        for op in self.ops:
            if op.dma:
                continue
            if op.needed:
                cnt[op.eng] += 1
                op.count = cnt[op.eng]
        per_eng = {e: [o for o in self.ops if o.eng == e] for e in self.ENGS}
        with contextlib.ExitStack() as st:
            esem = {e: st.enter_context(nc.semaphore("s_" + e)) for e in self.ENGS}
            dsem = {
                e: [st.enter_context(nc.semaphore(f"d_{e}_{i}")) for i in range(NDMA_SEM)]
                for e in self.ENGS
                if self.dma_hist[e]
            }
            block = st.enter_context(nc.Block())

            def run(engname, eng):
                waited = {}
                for op in per_eng[engname]:
                    ws = {}
                    deps = list(op.deps)
                    if op.prev_dma is not None:
                        deps.append(op.prev_dma)
                    for d in deps:
                        if d.dma:
                            key = ("d", d.eng, d.dsem)
                            val = d.dval
                        else:
                            key = ("e", d.eng)
                            val = d.count
                        if waited.get(key, 0) >= val:
                            continue
                        if ws.get(key, 0) < val:
                            ws[key] = val
                    for key, val in ws.items():
                        if key[0] == "d":
                            eng.wait_ge(dsem[key[1]][key[2]], val)
                        else:
                            eng.wait_ge(esem[key[1]], val)
                        waited[key] = val
                    try:
                        inst = op.fn(eng)
                    except Exception:
                        print('OP FAILED at lines', op.where, flush=True)
                        raise
                    if op.dma:
                        inst.then_inc(dsem[engname][op.dsem], 16)
                    elif op.needed:
                        inst.then_inc(esem[engname], 1)
                if engname == "sp":
                    for e in self.ENGS:
                        last = {}
                        for o in self.dma_hist[e]:
                            last[o.dsem] = o.dval
                        for s, v in last.items():
                            eng.wait_ge(dsem[e][s], v)
                        if cnt[e] > 0:
                            eng.wait_ge(esem[e], cnt[e])

            @block.tensor
            def _(e):
                run("pe", e)

            @block.scalar
            def _(e):
                run("act", e)

            @block.vector
            def _(e):
                run("dve", e)

            @block.gpsimd
            def _(e):
                run("pool", e)

            @block.sync
            def _(e):
                run("sp", e)


O_HT = 0
O_ONES = 65536
O_CF = O_ONES + 256
O_RSTD = 67584
O_R1 = 69632
O_R2 = 102464
O_R3 = 147520
O_WS = 180288
O_CV = 202816
O_END = 211008
WS_ELEMS = 11264

C_EPS, C_NLAM, C_VL, C_VR, C_M0, C_ML, C_MR, C_MRLO, C_GS0, C_GS1, C_GQ, C_GK = range(12)
C_GA = 16
C_GB = 32
C_CW = 48
C_CB = 48 + 264

LAMBDA_INIT = [0.8 - 0.6 * float(np.exp(-0.3 * i)) for i in range(4)]
SCALE = 128 ** -0.5
DILS = (1, 4, 16)


class Ctx:
    def __init__(self):
        self.nc = bass.Bass("TRN2", target_bir_lowering=False)
        self.S = Sched()
        self.base = (self.nc.sbuf_base + 63) // 64 * 64
        assert self.base + O_END <= self.nc.sbuf_top, (self.base, self.nc.sbuf_top)
        self.n = 0
        self.barrier_op = None

    def dram_in(self, name, shape, dt):
        return self.nc.dram_tensor(name, list(shape), dt, kind="ExternalInput").ap()

    def dram_out(self, name, shape, dt):
        return self.nc.dram_tensor(name, list(shape), dt, kind="ExternalOutput").ap()

    def dram(self, name, shape, dt):
        return self.nc.dram_tensor(name, list(shape), dt).ap()

    def at(self, off, shape, dt):
        self.n += 1
        return self.nc.alloc_sbuf_tensor_at(f"t{self.n}", list(shape), dt, offset=self.base + off)

    def add(self, eng, fn, reads=(), writes=(), dma=False, sig=False, ndesc=0):
        return self.S.add(eng, fn, list(reads) + ["ARENA"], writes, dma, sig, ndesc)

    def barrier(self):
        self.S.add("dve", lambda e: e.memset(self.dummy, 0.0), reads=["ARENA"], writes=["ARENA", "dummy"])

    def load(self, dst, src, key, eng="sp"):
        self.add(eng, lambda e: e.dma_start(out=dst, in_=src), writes=[key] if isinstance(key, str) else key, dma=True)

    def store(self, dst, src, rkeys, wkey=None, eng="sp"):
        self.add(eng, lambda e: e.dma_start(out=dst, in_=src), reads=[rkeys] if isinstance(rkeys, str) else rkeys,
                 writes=[wkey] if wkey else [], dma=True)

    def act(self, out, in_, func, reads, writes, **kw):
        self.add("act", lambda e: e.activation(out=out, in_=in_, func=func, **kw), reads, writes)

    def tt(self, out, in0, in1, op, reads, writes, eng="dve"):
        self.add(eng, lambda e: e.tensor_tensor(out=out, in0=in0, in1=in1, op=op), reads, writes)

    def stt(self, out, in0, scalar, in1, op0, op1, reads, writes):
        self.add("dve", lambda e: e.scalar_tensor_tensor(out=out, in0=in0, scalar=scalar, in1=in1, op0=op0, op1=op1),
                 reads, writes)

    def ts(self, out, in0, s1, op0, reads, writes, s2=None, op1=None, eng="dve"):
        if op1 is None:
            self.add(eng, lambda e: e.tensor_scalar(out=out, in0=in0, scalar1=s1, scalar2=None, op0=op0), reads, writes)
        else:
            self.add(eng, lambda e: e.tensor_scalar(out=out, in0=in0, scalar1=s1, scalar2=s2, op0=op0, op1=op1), reads,
                     writes)

    def cp(self, out, in_, reads, writes, eng="dve"):
        self.add(eng, lambda e: e.tensor_copy(out=out, in_=in_), reads, writes)

    def rc(self, out, in_, reads, writes):
        self.add("dve", lambda e: e.reciprocal(out=out, in_=in_), reads, writes)

    def mm(self, steps, reads, writes):
        steps = list(steps)

        def f(e):
            inst = None
            for (o, l, r, st, sp) in steps:
                inst = e.matmul(o, lhsT=l, rhs=r, start=st, stop=sp)
            return inst

        self.add("pe", f, reads, writes)

    def finish(self):
        self.S.emit(self.nc)
        return self.nc


_PV = {}


def getpv(e):
    k = id(e)
    if k not in _PV:
        pid = e.partition_id()
        b4 = (pid // 4) * 4
        _PV.clear()
        _PV[k] = dict(pid=pid, b4=b4)
    return _PV[k]


class WStream:
    GR = 512

    def __init__(self, K):
        self.K = K
        self.buf = K.at(O_WS, [128, WS_ELEMS], BF16)
        self.i = 0
        self.cur = None

    def fetch(self, w, kc, col0, ncols, row0=0):
        size = kc * ncols
        if self.cur != size:
            self.cur = size
            self.i = 0
        nslots = WS_ELEMS // size
        s = self.i % nslots
        self.i += 1
        off = s * size
        t = self.buf[:, off:off + size].rearrange("p (c e) -> p c e", c=kc)
        keys = [f"wsg{g}" for g in range(off // self.GR, (off + size + self.GR - 1) // self.GR)]
        src = w[row0:row0 + 128 * kc, col0:col0 + ncols].rearrange("(c p) e -> p c e", p=128)
        self.K.add("pool", lambda e: e.dma_start(out=t, in_=src), reads=[w.tensor.name], writes=keys, dma=True,
                   ndesc=8 * kc)
        return t, keys


def wstream(ws, specs):
    specs = list(specs)
    size = specs[0][1] * specs[0][3]
    depth = max(1, WS_ELEMS // size - 1)
    q = []
    issued = 0
    for i in range(len(specs)):
        while issued < min(len(specs), i + depth):
            q.append(ws.fetch(*specs[issued]))
            issued += 1
        yield q.pop(0)


def setup_common(K):
    nc = K.nc
    K._st = contextlib.ExitStack()
    K.bank = [K._st.enter_context(nc.psum_tensor(f"bank{i}", [128, 512], F32)) for i in range(8)]
    K.hT = K.at(O_HT, [128, DC, T], F32)
    K.ones = K.at(O_ONES, [128, 128], BF16)
    K.cf = K.at(O_CF, [128, 448], F32)
    K.rstd = K.at(O_RSTD, [128, 512], F32)
    K.dummy = K.cf[:, 440:441]
    K.add("dve", lambda e: e.memset(K.ones[:], 1.0), writes=["ones"])
    K.add("dve", lambda e: e.memset(K.cf[:, C_EPS:C_EPS + 1], EPS), writes=["cf_eps"])
    K.ws = WStream(K)
    K.hkeys = [f"h{c}" for c in range(DC)]


def rmsnorm_stats(K, src3, src_keys, TN, sq3, sq_keys, bank, inv_n):
    C = src3.shape[1]
    K.act(sq3, src3, AF.Square, src_keys, sq_keys)
    pb = K.bank[bank]
    K.mm([(pb[:, 0:TN], K.ones[:], sq3[:, c, :], c == 0, c == C - 1) for c in range(C)],
         list(sq_keys) + ["ones"], [f"bank{bank}"])
    rs = K.rstd[:, 0:TN]
    K.act(rs, pb[:, 0:TN], AF.Sqrt, [f"bank{bank}", "cf_eps"], ["rstd"], bias=K.cf[:, C_EPS:C_EPS + 1], scale=inv_n)
    K.rc(rs, rs, ["rstd"], ["rstd"])


def load_vec(K, vec_d, lo, n, col, key):
    K.load(K.cf[:, col:col + n], vec_d[:, lo:lo + n], key)


def ffn_phase(K, L, vec_d, wup, wdn, xedge_loc, xg_pad, pid):
    hT = K.hT
    cf = K.cf
    ws = K.ws
    B = K.bank
    K.barrier()
    load_vec(K, vec_d, 32, 16, C_GA, "gA")
    load_vec(K, vec_d, 48, 16, C_GB, "gB")
    load_vec(K, vec_d, 64, 352, C_CW, "cwb")
    xn = K.at(O_R1, [128, DC, 514], BF16)
    xmid = K.at(O_R1 + 16448, [128, DC, 2], BF16)
    xh = K.at(O_R1 + 16448 + 64, [128, 2, DC], BF16)
    xe = K.at(O_R1 + 16448 + 128, [128, 2, DC], BF16)
    act = K.at(O_R2, [128, FC, 512], BF16)
    ybuf = K.at(O_R3, [128, DC, 512], F32)
    cvb = [[K.at(O_CV + (p * 2 + q) * 2048, [128, 512], F32) for q in range(2)] for p in range(2)]
    sq = act[:, 0:DC, :]
    sqk = [f"act{j}" for j in range(DC)]
    rstd = K.rstd

    def gA(c):
        return cf[:, C_GA + c:C_GA + c + 1]

    def gB(c):
        return cf[:, C_GB + c:C_GB + c + 1]

    e4 = K.at(O_R3, [128, DC, 4], F32)
    for i, tok in enumerate((0, 511, 512, 1023)):
        K.cp(e4[:, :, i:i + 1], hT[:, :, tok:tok + 1], K.hkeys, ["e4"])
    rmsnorm_stats(K, e4[:, :, :], ["e4"], 4, act[:, 0:DC, 0:4], sqk, 0, 1.0 / D)
    for c in range(DC):
        K.stt(e4[:, c, :], e4[:, c, :], gA(c), rstd[:, 0:4], ALU.mult, ALU.mult, ["e4", "rstd", "gA"], ["e4"])
    K.cp(xmid[:, :, :], e4[:, :, 1:3], ["e4"], ["xmid"])
    K.cp(xe[:, 0, :], e4[:, :, 0], ["e4"], ["xe"])
    K.cp(xe[:, 1, :], e4[:, :, 3], ["e4"], ["xe"])
    K.store(xedge_loc.rearrange("(a p) c -> p a c", p=128), xe[:, :, :], ["xe"], "xedge")
    K.add("pool", lambda e: e.collective_compute("AllGather", ALU.bypass, replica_groups=[list(range(NCORES))],
                                                  ins=[xedge_loc], outs=[xg_pad[256:256 * 9, :]]),
          reads=["xedge"], writes=["xgath"], sig=True)
    xg3 = xg_pad.rearrange("(s a p) c -> s a p c", a=2, p=128)

    xgw = K.xg_win

    def cpwin(e):
        pid = getpv(e)['pid']
        return e.dma_start(out=xgw[:, :, :, :], in_=xg3[bass.ds(pid, 3), :, :, :])

    K.add("sp", cpwin, reads=["xgath"], writes=["xgw"], dma=True)

    def ld_l(e):
        return e.dma_start(out=xh[:, 0, :], in_=xgw[0, 1, :, :])

    def ld_r(e):
        return e.dma_start(out=xh[:, 1, :], in_=xgw[2, 0, :, :])

    K.add("sp", ld_l, reads=["xgw"], writes=["xh0"], dma=True)
    K.add("sp", ld_r, reads=["xgw"], writes=["xh1"], dma=True)
    xhf = K.at(O_R1 + 16448 + 256, [128, DC, 2], F32)
    K.ts(xhf[:, :, 0], xh[:, 0, :], cf[:, C_VL:C_VL + 1], ALU.mult, ["xh0", "pc"], ["xhf"])
    K.ts(xhf[:, :, 1], xh[:, 1, :], cf[:, C_VR:C_VR + 1], ALU.mult, ["xh1", "pc", "xhf"], ["xhf"])
    for hb in range(2):
        E0 = hb * 512
        rmsnorm_stats(K, hT[:, :, E0:E0 + 512], K.hkeys, 512, sq, sqk, 0, 1.0 / D)
        for c in range(DC):
            K.stt(xn[:, c, 1:513], hT[:, c, E0:E0 + 512], gA(c), rstd[:, :], ALU.mult, ALU.mult,
                  [f"h{c}", "rstd", "gA"], [f"xn{c}"])
        lsrc = xhf[:, :, 0:1] if hb == 0 else xmid[:, :, 0:1]
        rsrc = xmid[:, :, 1:2] if hb == 0 else xhf[:, :, 1:2]
        K.cp(xn[:, :, 0:1], lsrc, ["xhf", "xmid"], ["xnl"])
        K.cp(xn[:, :, 513:514], rsrc, ["xhf", "xmid"], ["xnr"])
        xn_keys = [f"xn{c}" for c in range(DC)] + ["xnl", "xnr"]
        pair = 0
        wgen = wstream(ws, [(wup, DC, part * DFF + j * 128, 128) for j in range(FC) for part in range(2)])
        for j in range(FC):
            for part in range(2):
                f = part * FC + j
                wt, wk = next(wgen)
                b0 = (pair % 3) * 2
                pair += 1
                pA, pB = B[b0], B[b0 + 1]
                K.mm([(pA[:, 0:258], wt[:, kc, :], xn[:, kc, 0:258], kc == 0, kc == DC - 1) for kc in range(DC)] +
                     [(pB[:, 0:258], wt[:, kc, :], xn[:, kc, 256:514], kc == 0, kc == DC - 1) for kc in range(DC)],
                     wk + xn_keys, [f"bank{b0}", f"bank{b0 + 1}"])
                cv = cvb[part][j % 2]
                ck = f"cv{part}_{j % 2}"
                for blk, pb in enumerate((pA, pB)):
                    o = cv[:, blk * 256:(blk + 1) * 256]
                    kk = ck + f"_{blk}"
                    bk = f"bank{b0 + blk}"
                    K.act(o, pb[:, 1:257], AF.Identity, [bk, "cwb"], [kk], bias=cf[:, C_CB + f:C_CB + f + 1],
                          scale=cf[:, C_CW + 3 * f + 1:C_CW + 3 * f + 2])
                    K.stt(o, pb[:, 0:256], cf[:, C_CW + 3 * f:C_CW + 3 * f + 1], o, ALU.mult, ALU.add, [bk, "cwb", kk], [kk])
                    K.stt(o, pb[:, 2:258], cf[:, C_CW + 3 * f + 2:C_CW + 3 * f + 3], o, ALU.mult, ALU.add,
                          [bk, "cwb", kk], [kk])
            cg = cvb[0][j % 2]
            cvv = cvb[1][j % 2]
            gk = [f"cv0_{j % 2}_0", f"cv0_{j % 2}_1"]
            vk = [f"cv1_{j % 2}_0", f"cv1_{j % 2}_1"]
            K.act(cg[:, :], cg[:, :], AF.Gelu_apprx_tanh, gk, gk)
            K.tt(act[:, j, :], cg[:, :], cvv[:, :], ALU.mult, gk + vk, [f"act{j}"])
        act_keys = [f"act{j}" for j in range(FC)]
        wgen = wstream(ws, [(wdn, FC, dc * 128, 128) for dc in range(DC)])
        for dc in range(DC):
            wt, wk = next(wgen)
            bi = 6 + dc % 2
            pb = B[bi]
            K.mm([(pb[:, :], wt[:, fc, :], act[:, fc, :], fc == 0, fc == FC - 1) for fc in range(FC)],
                 wk + act_keys, [f"bank{bi}"])
            K.act(ybuf[:, dc, :], pb[:, :], AF.Identity, [f"bank{bi}"], [f"y{dc}"])
        post_norm_residual(K, ybuf, E0, sq, sqk)


def post_norm_residual(K, ybuf, E0, sq, sqk):
    ykeys = [f"y{dc}" for dc in range(DC)]
    rmsnorm_stats(K, ybuf[:, :, :], ykeys, 512, sq, sqk, 0, 1.0 / D)
    for c in range(DC):
        K.tt(ybuf[:, c, :], ybuf[:, c, :], K.rstd[:, :], ALU.mult, [f"y{c}", "rstd"], [f"y{c}"], eng="pool")
        K.stt(K.hT[:, c, E0:E0 + 512], ybuf[:, c, :], K.cf[:, C_GB + c:C_GB + c + 1], K.hT[:, c, E0:E0 + 512],
              ALU.mult, ALU.add, [f"y{c}", "gB", f"h{c}"], [f"h{c}"])


def out_proj_phase(K, wout, ec_n, oT, okeys):
    K.barrier()
    ybuf = K.at(O_R3, [128, DC, 512], F32)
    sq = K.at(O_R2, [128, DC, 512], BF16)[:, :, :]
    sqk = [f"sq{j}" for j in range(DC)]
    for hb in range(2):
        E0 = hb * 512
        wgen = wstream(K.ws, [(wout, ec_n, dc * 128, 128) for dc in range(DC)])
        for dc in range(DC):
            wt, wk = next(wgen)
            bi = 6 + dc % 2
            pb = K.bank[bi]
            K.mm([(pb[:, :], wt[:, ec, :], oT[:, ec, E0:E0 + 512], ec == 0, ec == ec_n - 1) for ec in range(ec_n)],
                 wk + okeys, [f"bank{bi}"])
            K.act(ybuf[:, dc, :], pb[:, :], AF.Identity, [f"bank{bi}"], [f"y{dc}"])
        post_norm_residual(K, ybuf, E0, sq, sqk)


def pre_norm_full(K, xn):
    sq = K.at(O_R2, [128, DC, 512], BF16)[:, :, :]
    sqk = [f"sq{j}" for j in range(DC)]
    for hb in range(2):
        E0 = hb * 512
        rmsnorm_stats(K, K.hT[:, :, E0:E0 + 512], K.hkeys, 512, sq, sqk, 0, 1.0 / D)
        for c in range(DC):
            K.stt(xn[:, c, E0:E0 + 512], K.hT[:, c, E0:E0 + 512], K.cf[:, C_GA + c:C_GA + c + 1], K.rstd[:, :],
                  ALU.mult, ALU.mult, [f"h{c}", "rstd", "gA"], [f"xn{c}_{hb}"])
    return [f"xn{c}_{hb}" for c in range(DC) for hb in range(2)]


def bias_setup(K, table_d, ohA_d, ohC_d, TR2, TRC):
    K.barrier()
    tabx = K.at(O_R2, [33, 28], F32)
    ones33 = K.at(O_R2 + 128, [33, 128], F32)
    trf = [K.at(O_R2 + 1024 + i * 512, [33, 128], F32) for i in range(2)]
    trr = [K.at(O_R2 + 2048 + i * 512, [33, 128], F32) for i in range(2)]
    trh = [K.at(O_R2 + 3072 + i * 256, [33, 128], BF16) for i in range(2)]
    trl = [K.at(O_R2 + 3584 + i * 256, [33, 128], BF16) for i in range(2)]
    ohA = K.at(O_R1, [32, 5120], BF16)
    ohC = K.at(O_R1 + 10240, [33, 3 * 384], BF16)
    stg = [K.at(O_R3 + i * 2048, [128, 512], F32) for i in range(2)]
    K.add("dve", lambda e: e.memset(tabx[:, :], -30000.0), writes=["tabx"])
    K.load(tabx[0:32, :], table_d, "tabx")
    K.add("dve", lambda e: e.memset(ones33[:, :], 1.0), writes=["ones33"])
    K.load(ohA[:, :], ohA_d, "ohA")
    K.load(ohC[:, :], ohC_d, "ohC")
    n = 0
    for c in range(28):
        i2 = c % 2
        tk = f"tabrep{i2}"
        K.ts(trf[i2][:, :], ones33[:, :], tabx[:, c:c + 1], ALU.mult, ["ones33", "tabx"], [tk + "f"])
        K.cp(trh[i2][:, :], trf[i2][:, :], [tk + "f"], [tk + "h"])
        K.tt(trr[i2][:, :], trf[i2][:, :], trh[i2][:, :], ALU.subtract, [tk + "f", tk + "h"], [tk + "r"])
        K.cp(trl[i2][:, :], trr[i2][:, :], [tk + "r"], [tk + "l"])
        tks = [tk + "h", tk + "l"]
        if c < 4:
            for blk in range(10):
                bi = n % 2
                pb = K.bank[bi]
                rhs = ohA[:, blk * 512:(blk + 1) * 512]
                K.mm([(pb[:, :], trh[i2][0:32, :], rhs, True, False), (pb[:, :], trl[i2][0:32, :], rhs, False, True)],
                     tks + ["ohA"], [f"bank{bi}"])
                K.act(stg[bi][:, :], pb[:, :], AF.Identity, [f"bank{bi}"], [f"stg{bi}"])
                K.store(TR2[c * 128:(c + 1) * 128, blk * 512:(blk + 1) * 512], stg[bi][:, :], [f"stg{bi}"], f"TR2_{c}_{blk}")
                n += 1
        else:
            g = (c - 4) // 8
            bi = n % 2
            pb = K.bank[bi]
            rhs = ohC[:, g * 384:(g + 1) * 384]
            K.mm([(pb[:, 0:384], trh[i2][:, :], rhs, True, False), (pb[:, 0:384], trl[i2][:, :], rhs, False, True)],
                 tks + ["ohC"], [f"bank{bi}"])
            K.act(stg[bi][:, 0:384], pb[:, 0:384], AF.Identity, [f"bank{bi}"], [f"stg{bi}"])
            K.store(TRC[(c - 4) * 128:(c - 3) * 128, :], stg[bi][:, 0:384], [f"stg{bi}"], f"TRC_{c}")
            n += 1
    K.tr2_keys = [f"TR2_{c}_{blk}" for c in range(4) for blk in range(10)]
    K.trc_keys = [f"TRC_{c}" for c in range(4, 28)]


def ab_phase(K, L, vec_d, ab_d, lam_d, win, wout, xb_loc, xb_g, TR2, ropeC_d, ropeS_d, PT_d, pid):
    cf = K.cf
    hT = K.hT
    B = K.bank
    ws = K.ws
    K.barrier()
    load_vec(K, vec_d, 0, 16, C_GA, "gA")
    load_vec(K, vec_d, 16, 16, C_GB, "gB")
    K.load(cf[:, C_GS0:C_GS0 + 4], ab_d, "abv")
    lam = K.at(O_CV, [1, 512], F32)
    lw = K.at(O_CV + 2048, [1, 256], F32)
    ls = K.at(O_CV + 3072, [1, 8], F32)
    onef = K.at(O_CV + 3200, [1, 128], F32)
    K.load(lam[:, :], lam_d, "lam")
    K.add("dve", lambda e: e.memset(onef[:, :], 1.0), writes=["onef"])
    K.tt(lw[:, 0:128], lam[:, 0:128], lam[:, 128:256], ALU.mult, ["lam"], ["lw"])
    K.tt(lw[:, 128:256], lam[:, 256:384], lam[:, 384:512], ALU.mult, ["lam", "lw"], ["lw"])
    K.add("dve", lambda e: e.tensor_reduce(out=ls[:, 0:2], in_=lw[:, :].rearrange("p (a b) -> p a b", a=2),
                                           axis=mybir.AxisListType.X, op=ALU.add), reads=["lw"], writes=["ls"])
    K.act(ls[:, 2:4], ls[:, 0:2], AF.Exp, ["ls"], ["ls2"])
    K.tt(ls[:, 4:5], ls[:, 2:3], ls[:, 3:4], ALU.subtract, ["ls2"], ["ls3"])
    K.ts(ls[:, 5:6], ls[:, 4:5], -1.0, ALU.mult, ["ls3"], ["ls4"], s2=-LAMBDA_INIT[L], op1=ALU.add)
    K.mm([(B[0][:, 0:1], onef[0:1, :], ls[0:1, 5:6], True, True)], ["onef", "ls4"], ["bank0"])
    K.act(cf[:, C_NLAM:C_NLAM + 1], B[0][:, 0:1], AF.Identity, ["bank0"], ["nlam"])
    K.ts(cf[:, C_GS0:C_GS0 + 2], cf[:, C_GS0:C_GS0 + 2], 1.0 - LAMBDA_INIT[L], ALU.mult, ["abv"], ["abv"])
    xn = K.at(O_R1, [128, DC, T], BF16)
    xn_keys = pre_norm_full(K, xn)
    K.barrier()
    qA = K.at(O_R2, [128, 8, T], BF16)
    qB = K.at(O_R2 + 16384, [128, 8, T], BF16)
    ropeC = K.at(O_R2 + 32768, [128, T], F32)
    ropeS = K.at(O_R2 + 36864, [128, T], F32)
    PT = K.at(O_R2 + 40960, [128, 128], BF16)
    K.load(ropeC[:, :], ropeC_d, "ropeC")
    K.load(ropeS[:, :], ropeS_d, "ropeS")
    K.load(PT[:, :], PT_d, "PT")
    stgk = [K.at(O_R3 + i * 2048, [128, T], BF16) for i in range(2)]
    qraw = [K.at(O_R3 + 4096 + i * 2048, [128, 512], F32) for i in range(2)]
    qn = [K.at(O_R3 + 8192 + i * 2048, [128, 512], F32) for i in range(2)]
    sqb = [K.at(O_R3 + 12288 + i * 1024, [128, 512], BF16) for i in range(2)]
    t1 = [K.at(O_R3 + 14336 + i * 2048, [128, 512], F32) for i in range(2)]
    stgv = [K.at(O_R3 + 18432 + i * 512, [128, 256], BF16) for i in range(4)]
    qh = [K.at(O_R3 + 20480 + i * 1024, [128, 512], BF16) for i in range(2)]
    ql = [K.at(O_R3 + 22528 + i * 1024, [128, 512], BF16) for i in range(2)]
    qr = [K.at(O_R3 + 24576 + i * 2048, [128, 512], F32) for i in range(2)]
    xkeys = []
    nb = 0
    nk = 0
    cbs = list(range(0, 16)) + list(range(24, 34))
    wgen = wstream(ws, [(win, DC, cb * 128, 128) for cb in cbs])
    for cb in cbs:
        wt, wk = next(wgen)
        for hb in range(2):
            E0 = hb * 512
            bi = nb % 2
            nb += 1
            pb = B[bi]
            bk = f"bank{bi}"
            K.mm([(pb[:, :], wt[:, kc, :], xn[:, kc, E0:E0 + 512], kc == 0, kc == DC - 1) for kc in range(DC)],
                 wk + xn_keys, [bk])
            if cb < 8:
                K.act(qA[:, cb, E0:E0 + 512], pb[:, :], AF.Identity, [bk], [f"qA{cb}_{hb}"])
            elif cb < 16:
                sk = stgk[(cb) % 2]
                K.act(sk[:, E0:E0 + 512], pb[:, :], AF.Identity, [bk], [f"stgk{cb % 2}_{hb}"])
            else:
                i2 = nk % 2
                nk += 1
                gcol = C_GQ if cb < 32 else C_GK
                K.act(qraw[i2][:, :], pb[:, :], AF.Identity, [bk], [f"qraw{i2}"])
                K.act(sqb[i2][:, :], qraw[i2][:, :], AF.Square, [f"qraw{i2}"], [f"sqb{i2}"])
                K.mm([(B[2][:, :], K.ones[:], sqb[i2][:, :], True, True)], [f"sqb{i2}", "ones"], ["bank2"])
                K.act(K.rstd[:, :], B[2][:, :], AF.Sqrt, ["bank2", "cf_eps"], ["rstd"], bias=cf[:, C_EPS:C_EPS + 1],
                      scale=1.0 / 128)
                K.rc(K.rstd[:, :], K.rstd[:, :], ["rstd"], ["rstd"])
                K.stt(qn[i2][:, :], qraw[i2][:, :], cf[:, gcol:gcol + 1], K.rstd[:, :], ALU.mult, ALU.mult,
                      [f"qraw{i2}", "rstd", "abv"], [f"qn{i2}"])
                K.cp(qh[i2][:, :], qn[i2][:, :], [f"qn{i2}"], [f"qh{i2}"])
                K.tt(qr[i2][:, :], qn[i2][:, :], qh[i2][:, :], ALU.subtract, [f"qn{i2}", f"qh{i2}"], [f"qr{i2}"])
                K.cp(ql[i2][:, :], qr[i2][:, :], [f"qr{i2}"], [f"ql{i2}"])
                K.mm([(B[3][:, :], PT[:, :], qh[i2][:, :], True, False), (B[3][:, :], PT[:, :], ql[i2][:, :], False, True)],
                     ["PT", f"qh{i2}", f"ql{i2}"], ["bank3"])
                K.tt(t1[i2][:, :], qn[i2][:, :], ropeC[:, E0:E0 + 512], ALU.mult, [f"qn{i2}", "ropeC"], [f"t1{i2}"])
                K.tt(qn[i2][:, :], B[3][:, :], ropeS[:, E0:E0 + 512], ALU.mult, ["bank3", "ropeS", f"qn{i2}"], [f"qn{i2}"])
                if cb < 32:
                    K.tt(qB[:, cb - 24, E0:E0 + 512], t1[i2][:, :], qn[i2][:, :], ALU.add, [f"t1{i2}", f"qn{i2}"],
                         [f"qB{cb - 24}_{hb}"])
                else:
                    sk = stgk[cb % 2]
                    K.tt(sk[:, E0:E0 + 512], t1[i2][:, :], qn[i2][:, :], ALU.add, [f"t1{i2}", f"qn{i2}"],
                         [f"stgk{cb % 2}_{hb}"])
        if 8 <= cb < 16 or cb >= 32:
            row0 = (cb - 8) * 128 if cb < 16 else 1024 + (cb - 32) * 128
            sk = stgk[cb % 2]
            K.store(xb_loc[row0:row0 + 128, :], sk[:, :], [f"stgk{cb % 2}_0", f"stgk{cb % 2}_1"], f"xb_k{cb}")
            xkeys.append(f"xb_k{cb}")
    nv = 0
    wgen = wstream(ws, [(win, DC, 2048 + cg * 256 if cg < 4 else 4352, 256) for cg in range(5)])
    for cg in range(5):
        wt, wk = next(wgen)
        for tt in range(8):
            bi = 4 + nv % 2
            pb = B[bi]
            K.mm([(pb[:, 0:256], xn[:, kc, tt * 128:(tt + 1) * 128], wt[:, kc, :], kc == 0, kc == DC - 1)
                  for kc in range(DC)], wk + xn_keys, [f"bank{bi}"])
            sv = stgv[nv % 4]
            K.act(sv[:, :], pb[:, 0:256], AF.Identity, [f"bank{bi}"], [f"stgv{nv % 4}"])
            if cg < 4:
                dst = xb_loc[1280 + tt * 128:1280 + (tt + 1) * 128, cg * 256:(cg + 1) * 256]
            else:
                dst = xb_loc[2304:2560, :].rearrange("a (b f) -> (a b) f", f=256)[tt * 128:(tt + 1) * 128, :]
            K.store(dst, sv[:, :], [f"stgv{nv % 4}"], f"xb_v{cg}_{tt}")
            xkeys.append(f"xb_v{cg}_{tt}")
            nv += 1
    K.add("pool", lambda e: e.collective_compute("AllGather", ALU.bypass, replica_groups=[list(range(NCORES))],
                                                  ins=[xb_loc], outs=[xb_g]), reads=xkeys, writes=["xbg"], sig=True)
    K.barrier()
    oT = K.at(O_R1, [128, DC, T], BF16)
    kv0 = O_R2 + 32768
    Kt = K.at(kv0, [128, 2, SEQ], BF16)
    Vt = K.at(kv0 + 16384, [128, 32, 256], BF16)
    G = K.at(O_WS, [128, 5120], F32)
    eo = O_R2 + 65536
    Et = [K.at(O_CV + i * 1024, [128, 512], BF16) for i in range(4)]
    tmp = [K.at(O_CV + 4096 + i * 2048, [128, 512], F32) for i in range(2)]
    rr = [K.at(eo + i * 2048, [128, 512], F32) for i in range(2)]
    od = K.at(eo + 4096, [128, 2, 512], F32)
    tq = [K.at(eo + 8192 + i * 2048, [128, 512], F32) for i in range(2)]
    sqo = K.at(O_WS + 20480, [128, 2, 512], BF16)
    xg3d = xb_g.rearrange("(r n) f -> r n f", r=NCORES)
    xg3 = K.xb_win

    def cpwin(e):
        b4 = getpv(e)['b4']
        return e.dma_start(out=xg3[:, :, :], in_=xg3d[bass.ds(b4, 4), :, :])

    K.add("sp", cpwin, reads=["xbg"], writes=["xbw"], dma=True)
    okeys = []
    ne = 0
    for h in range(4):
        for m in range(2):
            def ldk(e, h=h, m=m):
                return e.dma_start(out=Kt[:, m, :].rearrange("p (r f) -> p r f", r=4),
                                   in_=xg3[:, (h * 2 + m) * 128:(h * 2 + m + 1) * 128, :].rearrange(
                                       "r p f -> p r f"))

            K.add("sp", ldk, reads=["xbw"], writes=["Kt"], dma=True)
        for r in range(4):
            def ldv(e, h=h, r=r):
                return e.dma_start(out=Vt[:, r * 8:(r + 1) * 8, :],
                                   in_=xg3[:, 1280:2304, h * 256:(h + 1) * 256][r].rearrange(
                                       "(t p) f -> p t f", p=128))

            K.add("sp", ldv, reads=["xbw"], writes=["Vt"], dma=True)
        gsrc = bass.AP(tensor=TR2.tensor, offset=h * 128 * 5120 + 127, ap=[[5119, 128], [1, 4993]])
        K.add("sp", lambda e, gsrc=gsrc: e.dma_start(out=G[:, 0:4993], in_=gsrc), reads=K.tr2_keys, writes=["G"], dma=True)
        for qb in range(2):
            Q0 = qb * 512
            pend = None
            for kt in range(33):
                cur = []
                if kt < 32:
                    m0 = 3969 - kt * 128 + qb * 512
                    for m in range(2):
                        sb_ = B[m]
                        K.mm([(sb_[:, :], Kt[:, m, kt * 128:(kt + 1) * 128], qA[:, h * 2 + m, Q0:Q0 + 512], True, True)],
                             ["Kt", f"qA{h * 2 + m}_{qb}"], [f"bank{m}"])
                        tm = tmp[m]
                        K.stt(tm[:, :], sb_[:, :], SCALE, G[:, m0:m0 + 512], ALU.mult, ALU.add, [f"bank{m}", "G"], [f"tmp{m}"])
                        et = Et[ne % 4]
                        ek = f"Et{ne % 4}"
                        ne += 1
                        K.act(et[:, :], tm[:, :], AF.Exp, [f"tmp{m}"], [ek])
                        cur.append((m, kt, et, ek))
                if pend is not None:
                    for (m, pk, et, ek) in pend:
                        st, sp = pk == 0, pk == 31
                        K.mm([(B[2 + 3 * m][:, :], Vt[:, pk, 0:128], et[:, :], st, sp),
                              (B[3 + 3 * m][:, :], Vt[:, pk, 128:256], et[:, :], st, sp),
                              (B[4 + 3 * m][:, :], K.ones[:], et[:, :], st, sp)],
                             ["Vt", ek, "ones"], [f"bank{2 + 3 * m}", f"bank{3 + 3 * m}", f"bank{4 + 3 * m}"])
                pend = cur if kt < 32 else None
            for m in range(2):
                K.rc(rr[m][:, :], B[4 + 3 * m][:, :], [f"bank{4 + 3 * m}"], [f"rr{m}"])
            for dv in range(2):
                K.tt(tq[0][:, :], B[2 + dv][:, :], rr[0][:, :], ALU.mult, [f"bank{2 + dv}", "rr0"], ["tq0"])
                K.tt(tq[1][:, :], B[5 + dv][:, :], rr[1][:, :], ALU.mult, [f"bank{5 + dv}", "rr1"], ["tq1"])
                K.stt(od[:, dv, :], tq[1][:, :], cf[:, C_NLAM:C_NLAM + 1], tq[0][:, :], ALU.mult, ALU.add,
                      ["tq0", "tq1", "nlam"], [f"od{dv}"])
            rmsnorm_stats(K, od[:, :, :], ["od0", "od1"], 512, sqo[:, :, :], ["sqo"], 0, 1.0 / 256)
            for dv in range(2):
                K.stt(oT[:, h * 2 + dv, Q0:Q0 + 512], od[:, dv, :], cf[:, C_GS0 + dv:C_GS0 + dv + 1], K.rstd[:, :],
                      ALU.mult, ALU.mult, [f"od{dv}", "rstd", "abv"], [f"oT{h * 2 + dv}_{qb}"])
                okeys.append(f"oT{h * 2 + dv}_{qb}")
    for kv in range(2):
        def ldk(e, kv=kv):
            return e.dma_start(out=Kt[:, 0, :].rearrange("p (r f) -> p r f", r=4),
                               in_=xg3[:, 1024 + kv * 128:1024 + (kv + 1) * 128, :].rearrange(
                                   "r p f -> p r f"))

        K.add("sp", ldk, reads=["xbw"], writes=["Kt"], dma=True)

        for r in range(4):
            def ldv(e, kv=kv, r=r):
                src = xg3[:, 2304:2560, :][r].rearrange("a (b f) -> (a b) f", f=256)
                return e.dma_start(out=Vt[:, r * 8:(r + 1) * 8, 0:128],
                                   in_=src[:, kv * 128:(kv + 1) * 128].rearrange("(t p) f -> p t f", p=128))

            K.add("sp", ldv, reads=["xbw"], writes=["Vt"], dma=True)
        for g in range(4):
            hq = kv * 4 + g
            for qb in range(2):
                Q0 = qb * 512
                ob, db = (4, 5) if (g * 2 + qb) % 2 == 0 else (6, 7)
                pend = None
                for kt in range(33):
                    cur = None
                    if kt < 32:
                        sbk = kt % 2
                        sb_ = B[sbk]
                        K.mm([(sb_[:, :], Kt[:, 0, kt * 128:(kt + 1) * 128], qB[:, hq, Q0:Q0 + 512], True, True)],
                             ["Kt", f"qB{hq}_{qb}"], [f"bank{sbk}"])
                        et = Et[ne % 4]
                        ek = f"Et{ne % 4}"
                        ne += 1
                        K.act(et[:, :], sb_[:, :], AF.Exp, [f"bank{sbk}"], [ek], scale=SCALE)
                        cur = (kt, et, ek)
                    if pend is not None:
                        pk, et, ek = pend
                        st, sp = pk == 0, pk == 31
                        K.mm([(B[ob][:, :], Vt[:, pk, 0:128], et[:, :], st, sp), (B[db][:, :], K.ones[:], et[:, :], st, sp)],
                             ["Vt", ek, "ones"], [f"bank{ob}", f"bank{db}"])
                    pend = cur
                ri = (g * 2 + qb) % 2
                K.rc(rr[ri][:, :], B[db][:, :], [f"bank{db}"], [f"rr{ri}"])
                K.tt(oT[:, 8 + hq, Q0:Q0 + 512], B[ob][:, :], rr[ri][:, :], ALU.mult, [f"bank{ob}", f"rr{ri}"],
                     [f"oT{8 + hq}_{qb}"])
                okeys.append(f"oT{8 + hq}_{qb}")
    out_proj_phase(K, wout, 16, oT, okeys)


def c_phase(K, L, vec_d, win, wout, xc_loc, xc_pad, TRC):
    cf = K.cf
    B = K.bank
    ws = K.ws
    K.barrier()
    load_vec(K, vec_d, 0, 16, C_GA, "gA")
    load_vec(K, vec_d, 16, 16, C_GB, "gB")
    xn = K.at(O_R1, [128, DC, T], BF16)
    xn_keys = pre_norm_full(K, xn)
    K.barrier()
    qC = K.at(O_R2, [128, 24, T], BF16)
    so = O_R2 + 49152
    stgk = [K.at(so + i * 2048, [128, T], BF16) for i in range(2)]
    stgv = [K.at(so + 4096 + i * 512, [128, 256], BF16) for i in range(4)]
    xkeys = []
    nb = 0
    for g in range(3):
        d = DILS[g]
        wgen = wstream(ws, [(win, DC, ((g * 3 + s_) * 8 + h) * 128, 128) for s_ in range(2) for h in range(8)])
        for s_ in range(2):
            for h in range(8):
                wt, wk = next(wgen)
                for hb in range(2):
                    bi = nb % 2
                    nb += 1
                    pb = B[bi]
                    K.mm([(pb[:, :], wt[:, kc, :], xn[:, kc, hb * 512:(hb + 1) * 512], kc == 0, kc == DC - 1)
                          for kc in range(DC)], wk + xn_keys, [f"bank{bi}"])
                    src = pb[:, :].rearrange("p (l r) -> p r l", r=d)
                    if s_ == 0:
                        dst = qC[:, g * 8 + h, :]
                        wkey = f"qC{g * 8 + h}_{hb}"
                    else:
                        dst = stgk[h % 2][:, :]
                        wkey = f"stgk{h % 2}_{hb}"
                    dst = dst.rearrange("p (r l) -> p r l", r=d)[:, :, hb * (512 // d):(hb + 1) * (512 // d)]
                    K.act(dst, src, AF.Identity, [f"bank{bi}"], [wkey])
                if s_ == 1:
                    row0 = (g * 8 + h) * 128
                    K.store(xc_loc[row0:row0 + 128, :], stgk[h % 2][:, :], [f"stgk{h % 2}_0", f"stgk{h % 2}_1"],
                            f"xc_k{g}_{h}")
                    xkeys.append(f"xc_k{g}_{h}")
        nv = 0
        wgen = wstream(ws, [(win, DC, ((g * 3 + 2) * 8) * 128 + cg * 256, 256) for cg in range(4)])
        for cg in range(4):
            wt, wk = next(wgen)
            for tt in range(8):
                bi = 4 + nv % 2
                pb = B[bi]
                steps = []
                for kc in range(DC):
                    xv = xn[:, kc, :].rearrange("p (l r) -> p r l", r=d)
                    if d == 1:
                        lt = xn[:, kc, tt * 128:(tt + 1) * 128]
                    elif d == 4:
                        lt = xv[:, tt // 2, (tt % 2) * 128:(tt % 2) * 128 + 128]
                    else:
                        lt = xn[:, kc, :].rearrange("p (m e) -> p e m", e=8)[:, tt, :]
                    steps.append((pb[:, 0:256], lt, wt[:, kc, :], kc == 0, kc == DC - 1))
                K.mm(steps, wk + xn_keys, [f"bank{bi}"])
                sv = stgv[nv % 4]
                K.act(sv[:, :], pb[:, 0:256], AF.Identity, [f"bank{bi}"], [f"stgv{nv % 4}"])
                r0 = 3072 + g * 1024 + tt * 128
                K.store(xc_loc[r0:r0 + 128, cg * 256:(cg + 1) * 256], sv[:, :], [f"stgv{nv % 4}"], f"xc_v{g}_{cg}_{tt}")
                xkeys.append(f"xc_v{g}_{cg}_{tt}")
                nv += 1
    K.add("pool", lambda e: e.collective_compute("AllGather", ALU.bypass, replica_groups=[list(range(NCORES))],
                                                  ins=[xc_loc], outs=[xc_pad[6144:6144 * 9, :]]),
          reads=xkeys, writes=["xcg"], sig=True)
    K.barrier()
    oT = K.at(O_R1, [128, 8, T], BF16)
    accN = K.at(O_R1 + 16384, [128, T], F32)
    accD = K.at(O_R1 + 20480, [128, T], F32)
    BAB = [[K.at(O_R1 + 24576 + (i * 3 + g) * 1024, [128, 256], F32) for g in range(3)] for i in range(2)]
    Kw = [K.at(so + i * 6144, [128, 3 * T], BF16) for i in range(2)]
    Vw = [K.at(so + 12288 + i * 8192, [128, 32, 128], BF16) for i in range(2)]
    Et = [K.at(O_CV + i * 1024, [128, 512], BF16) for i in range(4)]
    tmp = [K.at(O_CV + 4096 + i * 2048, [128, 512], F32) for i in range(2)]
    xc3d = xc_pad.rearrange("(s n) f -> s n f", s=10)
    xc3 = K.xc_win

    def cpwin(e):
        pid = getpv(e)['pid']
        return e.dma_start(out=xc3[:, :, :], in_=xc3d[bass.ds(pid, 3), :, :])

    K.add("sp", cpwin, reads=["xcg"], writes=["xcw"], dma=True)
    okeys = []
    ne = 0
    nkv = 0
    ns = 0
    for h in range(8):
        hb_ = h % 2
        for g in range(3):
            gh = g * 8 + h
            for ab, off in ((0, 255), (1, 127)):
                src = bass.AP(tensor=TRC.tensor, offset=gh * 128 * 384 + off, ap=[[383, 128], [1, 128]])
                K.add("sp", lambda e, src=src, dst=BAB[hb_][g][:, ab * 128:(ab + 1) * 128]: e.dma_start(out=dst, in_=src),
                      reads=K.trc_keys, writes=[f"BAB{hb_}_{g}"], dma=True)
        for g in range(3):
            d = DILS[g]
            ns_ = T // d
            kb = nkv % 2
            nkv += 1
            kwk, vwk = f"Kw{kb}", f"Vw{kb}"
            kw = Kw[kb]
            vw = Vw[kb]
            r0k = (g * 8 + h) * 128

            for sl in range(3):
                def ldk(e, kw=kw, r0k=r0k, d=d, sl=sl):
                    return e.dma_start(out=kw[:, :].rearrange("p (r s l) -> p r s l", r=d, s=3)[:, :, sl, :],
                                       in_=xc3[:, r0k:r0k + 128, :][sl].rearrange("p (r l) -> p r l", r=d))

                K.add("sp", ldk, reads=["xcw"], writes=[kwk], dma=True)
            tpr = {1: 9, 4: 3, 16: 2}[d]
            vw4 = vw[:, 0:d * tpr, :].rearrange("p (r t) f -> p r t f", r=d)
            rv0 = 3072 + g * 1024
            c0 = h * 128
            if d == 1:
                segs = [(0, 960, 1024, 0, 0), (1, 0, 64, 0, 64), (1, 64, 960, 1, 0), (1, 960, 1024, 8, 0), (2, 0, 64, 8, 64)]
            elif d == 4:
                segs = [(0, 192, 256, 0, 0), (1, 0, 64, 0, 64), (1, 64, 192, 1, 0), (1, 192, 256, 2, 0), (2, 0, 64, 2, 64)]
            else:
                segs = [(0, 0, 64, 0, 0), (1, 0, 64, 0, 64), (2, 0, 64, 1, 0)]
            for (sl, la, lb, ti, p0) in segs:
                nrow = lb - la

                if d == 16:
                    for b_ in range(2):
                        def ldv(e, sl=sl, ti=ti, p0=p0, vw4=vw4, rv0=rv0, c0=c0, b_=b_):
                            src = xc3[:, rv0:rv0 + T, c0:c0 + 128][sl].rearrange("(t l b) f -> b l t f", t=8, b=2)[b_]
                            return e.dma_start(out=vw4[p0:p0 + 64, b_ * 8:(b_ + 1) * 8, ti, :], in_=src)

                        K.add("sp", ldv, reads=["xcw"], writes=[vwk], dma=True)
                    continue

                def ldv(e, sl=sl, la=la, nrow=nrow, ti=ti, p0=p0, vw4=vw4, d=d, ns_=ns_, rv0=rv0, c0=c0):
                    src = xc3[:, rv0:rv0 + T, c0:c0 + 128][sl].rearrange("(r l) f -> r l f", r=d)
                    if nrow <= 128:
                        return e.dma_start(out=vw4[p0:p0 + nrow, :, ti, :], in_=src[:, la:la + nrow, :].rearrange("r l f -> l r f"))
                    nt = nrow // 128
                    return e.dma_start(out=vw4[:, :, ti:ti + nt, :],
                                       in_=src[:, la:la + nrow, :].rearrange("r (t p) f -> p r t f", p=128))

                K.add("sp", ldv, reads=["xcw"], writes=[vwk], dma=True)
            kw3 = kw[:, :].rearrange("p (r w) -> p r w", r=d)
            QN = min(128, ns_)
            nqt = max(1, ns_ // 128)
            for r in range(d):
                for qt in range(nqt):
                    q0 = r * ns_ + qt * 128
                    qap = qC[:, g * 8 + h, q0:q0 + QN]
                    if d == 16:
                        wA, KB, tA, tB = 0, 64, 0, 1
                    else:
                        wA, KB, tA, tB = ns_ + qt * 128 - 64, 128, qt, qt + 1
                    sbi = ns % 2
                    ns += 1
                    sb_ = B[sbi]
                    K.mm([(sb_[:, 0:QN], kw3[:, r, wA:wA + 128], qap, True, True),
                          (sb_[0:KB, 128:128 + QN], kw3[:, r, wA + 128:wA + 128 + KB], qap, True, True)],
                         [kwk, f"qC{g * 8 + h}_0", f"qC{g * 8 + h}_1"], [f"bank{sbi}"])
                    tm = tmp[sbi]
                    bab = BAB[hb_][g]
                    K.stt(tm[:, 0:QN], sb_[:, 0:QN], SCALE, bab[:, 0:QN], ALU.mult, ALU.add, [f"bank{sbi}", f"BAB{hb_}_{g}"],
                          [f"tmp{sbi}"])
                    K.stt(tm[0:KB, 128:128 + QN], sb_[0:KB, 128:128 + QN], SCALE, bab[0:KB, 128:128 + QN], ALU.mult, ALU.add,
                          [f"bank{sbi}", f"BAB{hb_}_{g}", f"tmp{sbi}"], [f"tmp{sbi}"])
                    et = Et[ne % 4]
                    ek = f"Et{ne % 4}"
                    ne += 1
                    mA = C_ML if qt == 0 else C_M0
                    mB = C_MRLO if d == 16 else (C_MR if qt == nqt - 1 else C_M0)
                    K.act(et[:, 0:QN], tm[:, 0:QN], AF.Exp, [f"tmp{sbi}", "pc"], [ek], bias=cf[:, mA:mA + 1])
                    K.act(et[0:KB, 128:128 + QN], tm[0:KB, 128:128 + QN], AF.Exp, [f"tmp{sbi}", "pc", ek], [ek],
                          bias=cf[0:KB, mB:mB + 1])
                    if d == 1:
                        nbk, cc0 = 2 + qt // 4, (qt % 4) * 128
                    else:
                        nbk, cc0 = 2, qt * 128
                    K.mm([(B[nbk][:, cc0:cc0 + QN], vw4[:, r, tA, :], et[:, 0:QN], True, False),
                          (B[nbk][:, cc0:cc0 + QN], vw4[0:KB, r, tB, :], et[0:KB, 128:128 + QN], False, True),
                          (B[nbk + 2][:, cc0:cc0 + QN], K.ones[:, :], et[:, 0:QN], True, False),
                          (B[nbk + 2][:, cc0:cc0 + QN], K.ones[0:KB, :], et[0:KB, 128:128 + QN], False, True)],
                         [vwk, ek, "ones"], [f"bank{nbk}", f"bank{nbk + 2}"])
                aN = accN[:, :].rearrange("p (l r) -> p r l", r=d)[:, r, :]
                aD = accD[:, :].rearrange("p (l r) -> p r l", r=d)[:, r, :]
                pieces = [(2, 0, 512, 0), (3, 0, 512, 512)] if d == 1 else [(2, 0, ns_, 0)]
                for (bk_, ca, n_, oa) in pieces:
                    for (acc_, bo, key) in ((aN, 0, "accN"), (aD, 2, "accD")):
                        if g == 0:
                            K.cp(acc_[:, oa:oa + n_], B[bk_ + bo][:, ca:ca + n_], [f"bank{bk_ + bo}"], [key])
                        else:
                            K.tt(acc_[:, oa:oa + n_], B[bk_ + bo][:, ca:ca + n_], acc_[:, oa:oa + n_], ALU.add,
                                 [f"bank{bk_ + bo}", key], [key])
        K.rc(accD[:, :], accD[:, :], ["accD"], ["accD"])
        K.tt(oT[:, h, :], accN[:, :], accD[:, :], ALU.mult, ["accN", "accD"], [f"oT{h}"])
        okeys.append(f"oT{h}")
    out_proj_phase(K, wout, 8, oT, okeys)


W_SHAPES = {}


def weight_specs(layers):
    specs = []
    for L in layers:
        if L % 2 == 0:
            specs += [(f"l{L}_w_in", D, 4608), (f"l{L}_w_out", 2048, D)]
        else:
            specs += [(f"l{L}_w_in", D, 9216), (f"l{L}_w_out", 1024, D)]
        specs += [(f"l{L}_w_up", D, 2 * DFF), (f"l{L}_w_down", DFF, D)]
    return specs


def build_mega(layers=(0, 1, 2, 3), do_ffn=True, do_mix=True):
    K = Ctx()
    nc = K.nc
    setup_common(K)
    xT = K.dram_in("xT", [128, DC, T], F32)
    outT = K.dram_out("outT", [128, DC, T], F32)
    table = K.dram_in("table", [32, 28], F32)
    ohA = K.dram_in("ohA", [32, 5120], BF16)
    ohC = K.dram_in("ohC", [33, 3 * 384], BF16)
    ropeC = K.dram_in("ropeC", [128, T], F32)
    ropeS = K.dram_in("ropeS", [128, T], F32)
    PT = K.dram_in("PT", [128, 128], BF16)
    pc = K.dram_in("pc", [128, 6], F32)
    vec = {L: K.dram_in(f"vec{L}", [128, 416], F32) for L in layers}
    abv = {L: K.dram_in(f"ab{L}", [128, 4], F32) for L in layers if L % 2 == 0}
    lam = {L: K.dram_in(f"lam{L}", [1, 512], F32) for L in layers if L % 2 == 0}
    wsh, wfull, wloc = {}, {}, {}
    for (name, rows, cols) in weight_specs(layers):
        wsh[name] = K.dram_in(name, [rows // NCORES, cols], F32)
        wloc[name] = K.dram(name + "_loc", [rows // NCORES, cols], BF16)
        wfull[name] = K.dram(name + "_full", [rows, cols], BF16)
    TR2 = K.dram("TR2", [4 * 128, 5120], F32)
    TRC = K.dram("TRC", [24 * 128, 384], F32)
    xb_loc = K.dram("xb_loc", [2560, 1024], BF16)
    xb_g = K.dram("xb_g", [NCORES * 2560, 1024], BF16)
    xc_loc = K.dram("xc_loc", [6144, 1024], BF16)
    xc_pad = K.dram("xc_pad", [10 * 6144, 1024], BF16)
    xe_loc = K.dram("xe_loc", [256, 16], BF16)
    xg_pad = K.dram("xg_pad", [10 * 256, 16], BF16)
    K.xb_win = K.dram("xb_win", [4, 2560, 1024], BF16)
    K.xc_win = K.dram("xc_win", [3, 6144, 1024], BF16)
    K.xg_win = K.dram("xg_win", [3, 2, 128, 16], BF16)
    for c in range(DC):
        K.load(K.hT[:, c, :], xT[:, c, :], f"h{c}")
    K.load(K.cf[:, C_VL:C_VL + 6], pc, "pc")
    zt = K.at(O_R2, [128, 8, 1024], BF16)
    K.add("pool", lambda e: e.memset(zt[:, :, :], 0.0), writes=["zt"])
    for slot in (0, 9):
        for a in range(6):
            r0 = slot * 6144 + a * 1024
            K.store(xc_pad[r0:r0 + 1024, :].rearrange("(a p) f -> p a f", p=128), zt[:, :, :], ["zt"], f"zpad{slot}_{a}")
        K.store(xg_pad[slot * 256:(slot + 1) * 256, :].rearrange("(a p) f -> p a f", p=128), zt[:, 0:2, 0:16], ["zt"],
                f"zpadg{slot}")
    order = [n for (n, _, _) in weight_specs(layers)]

    def gather_weight(name):
        nd = (wsh[name].shape[0] * wsh[name].shape[1] * 4 // 2048) // 16 + 2
        K.add("pool", lambda e: e.dma_start(out=wloc[name], in_=wsh[name], max_dma_last_dim=2048), writes=[name + "_loc"],
              dma=True, ndesc=nd)
        K.add("pool", lambda e: e.collective_compute("AllGather", ALU.bypass, replica_groups=[list(range(NCORES))],
                                                      ins=[wloc[name]], outs=[wfull[name]]),
              reads=[name + "_loc"], writes=[name + "_full"], sig=True)

    pending = list(order)

    def gather_next(n):
        for _ in range(n):
            if pending:
                gather_weight(pending.pop(0))

    gather_next(2)
    bias_setup(K, table, ohA, ohC, TR2, TRC)
    for L in layers:
        gather_next(2)
        if not do_mix:
            pass
        elif L % 2 == 0:
            ab_phase(K, L, vec[L], abv[L], lam[L], wfull[f"l{L}_w_in"], wfull[f"l{L}_w_out"], xb_loc, xb_g, TR2, ropeC,
                     ropeS, PT, None)
        else:
            c_phase(K, L, vec[L], wfull[f"l{L}_w_in"], wfull[f"l{L}_w_out"], xc_loc, xc_pad, TRC)
        gather_next(2)
        if do_ffn:
            ffn_phase(K, L, vec[L], wfull[f"l{L}_w_up"], wfull[f"l{L}_w_down"], xe_loc, xg_pad, None)
    K.barrier()
    for c in range(DC):
        K.store(outT[:, c, :], K.hT[:, c, :], [f"h{c}"])
    return K.finish()


def _rel_bucket(rel):
    rel = np.asarray(rel, np.int64)
    n = np.abs(rel)
    nf = np.maximum(n, 1).astype(np.float32)
    large = 8 + (np.log(nf / np.float32(8)) / np.float32(np.log(1024 / 8)) * np.float32(8)).astype(np.int32)
    large = np.minimum(large, 15)
    return np.where(rel > 0, 16, 0) + np.where(n < 8, n, large)


def _vec_fm(v, n):
    return np.ascontiguousarray(np.asarray(v, np.float32).reshape(n, 128).T)


def host_constants(core):
    q = core % 4
    c = {}
    n = np.arange(5120)
    i = 5119 - n
    rel = i - 1023 - q * 1024
    oh = np.zeros((32, 5120), np.float32)
    oh[_rel_bucket(rel), n] = 1.0
    c["ohA"] = oh.astype(BF)
    ohc = np.zeros((33, 3 * 384), np.float32)
    for g, d in enumerate(DILS):
        nn = np.arange(384)
        rs = 191 - nn
        ok = np.abs(rs) <= 64
        b = _rel_bucket(rs * d)
        ohc[b[ok], g * 384 + nn[ok]] = 1.0
        ohc[32, g * 384 + nn[~ok]] = 1.0
    c["ohC"] = ohc.astype(BF)
    t = q * 1024 + np.arange(1024)
    row = (t // 64).astype(np.float32)
    col = (t % 64).astype(np.float32)
    inv = (np.float32(10000.0) ** (-np.arange(32, dtype=np.float32) * np.float32(2.0) / np.float32(64))).astype(np.float32)
    C_ = np.zeros((128, 1024), np.float32)
    S_ = np.zeros((128, 1024), np.float32)
    for dim in range(128):
        pos = row if dim < 64 else col
        ang = pos * inv[dim % 32]
        C_[dim] = np.cos(ang)
        S_[dim] = np.sin(ang)
    c["ropeC"] = C_
    c["ropeS"] = S_
    P = np.zeros((128, 128), np.float32)
    for i_ in range(128):
        if (i_ % 64) < 32:
            P[i_ + 32, i_] = -1.0
        else:
            P[i_ - 32, i_] = 1.0
    c["PT"] = P.astype(BF)
    vl = 1.0 if q > 0 else 0.0
    vr = 1.0 if q < 3 else 0.0
    pcv = np.zeros((128, 6), np.float32)
    pcv[:, 0] = vl
    pcv[:, 1] = vr
    p = np.arange(128)
    pcv[:, 3] = np.where(p < 64, 0.0 if vl else -30000.0, 0.0)
    pcv[:, 4] = np.where(p >= 64, 0.0 if vr else -30000.0, 0.0)
    pcv[:, 5] = np.where(p < 64, 0.0 if vr else -30000.0, 0.0)
    c["pc"] = pcv
    return c


def make_inputs(inputs, layers=(0, 1, 2, 3)):
    x = np.asarray(inputs["x"], np.float32)
    maps = []
    for core in range(NCORES):
        b, q = divmod(core, 4)
        m = host_constants(core)
        blk = x[b, q * 1024:(q + 1) * 1024, :]
        m["xT"] = np.ascontiguousarray(blk.T.reshape(DC, 128, T).transpose(1, 0, 2))
        m["table"] = np.asarray(inputs["rel_bias_table"], np.float32)
        for L in layers:
            cwv = np.asarray(inputs[f"l{L}_conv_w"], np.float32)
            nf = cwv.shape[1] // 128
            cw = cwv.T.reshape(nf, 128, 3).transpose(1, 0, 2).reshape(128, nf * 3)
            vecs = [_vec_fm(inputs[f"l{L}_mix_pre_norm"], 16), _vec_fm(inputs[f"l{L}_mix_post_norm"], 16),
                    _vec_fm(inputs[f"l{L}_ffn_pre_norm"], 16), _vec_fm(inputs[f"l{L}_ffn_post_norm"], 16),
                    cw, _vec_fm(inputs[f"l{L}_conv_b"], nf)]
            v = np.concatenate(vecs, axis=1)
            vv = np.zeros((128, 416), np.float32)
            vv[:, :64] = v[:, :64]
            vv[:, 64:64 + nf * 3] = v[:, 64:64 + nf * 3]
            vv[:, 328:328 + nf] = v[:, 64 + nf * 3:]
            m[f"vec{L}"] = vv
            if L % 2 == 0:
                ab = np.zeros((128, 4), np.float32)
                ab[:, 0:2] = _vec_fm(inputs[f"l{L}_diff_subln"], 2)
                qk = np.asarray(inputs[f"l{L}_qk_norm"], np.float32)
                ab[:, 2] = qk[0]
                ab[:, 3] = qk[1]
                m[f"ab{L}"] = ab
                m[f"lam{L}"] = np.asarray(inputs[f"l{L}_diff_lambda"], np.float32).reshape(1, 512)
            for nm in ("w_in", "w_out", "w_up", "w_down"):
                w = np.asarray(inputs[f"l{L}_{nm}"], np.float32)
                rs = w.shape[0] // NCORES
                m[f"l{L}_{nm}"] = w[core * rs:(core + 1) * rs]
        maps.append(m)
    return maps


def kernel(**inputs):
    nc = build_mega()
    maps = make_inputs(inputs)
    res = run_bass_kernel_spmd(nc, maps, core_ids=list(range(NCORES)))
    out = np.zeros((2, SEQ, D), np.float32)
    for core in range(NCORES):
        b, q = divmod(core, 4)
        o = res.results[core]["outT"]
        out[b, q * 1024:(q + 1) * 1024, :] = o.transpose(1, 0, 2).reshape(D, T).T
    return out
```
